# Optimizing a Trainium2 kernel written in Bass

```python
import math, functools
import jax, jax.numpy as jnp
from jax import lax
import numpy as np

D_MODEL = 1024
BATCH = 8
SEQ = 4096
DEPTH = 2
DEC_BATCH = 32
DEC_SEQ = 4
PAST_LEN = 16384
PAGE_SIZE = 128

D_ATT = D_MODEL // 2
D_SSM = D_MODEL - D_ATT
HEAD_DIM = 64
N_HEADS = D_ATT // HEAD_DIM
SSM_GROUP = 16
N_SSM_GROUPS = D_SSM // SSM_GROUP
SSM_STATE = 64
BRANCHES = ((128, 1), (512, 4), (2048, 16))
W_MAX = max(w for w, _ in BRANCHES)
N_BUCKETS = 32
MAX_DISTANCE = W_MAX
D_FF = ((8 * D_MODEL // 3 + 127) // 128) * 128
D_PLE = 256
D_IN = 3 * D_ATT + D_SSM
BLOCK_Q = 128
EPS = 1e-6
NEG = -1e30

kernel_name = 'hybrid_dilated_attn_s5_decoder_step'


def rmsnorm(x, g):
    xf = x.astype(jnp.float32)
    y = xf * lax.rsqrt(jnp.mean(xf * xf, axis=-1, keepdims=True) + EPS)
    return (y * g.astype(jnp.float32)).astype(x.dtype)


def swiglu(x, wg, wu, wd):
    return (jax.nn.silu(x @ wg) * (x @ wu)) @ wd


def t5_bucket(dist):
    dist = np.asarray(dist, dtype=np.int64)
    exact = N_BUCKETS // 2
    ratio = np.log(np.maximum(dist, 1) / exact) / np.log(MAX_DISTANCE / exact)
    large = np.minimum(exact + (ratio * (N_BUCKETS - exact)).astype(np.int64), N_BUCKETS - 1)
    return np.where(dist < exact, dist, large).astype(np.int32)


def branch_bias(rel_bias, window, dilation):
    n = window // dilation
    return rel_bias[t5_bucket(np.arange(n + 1) * dilation)].T.astype(jnp.float32)


def softmax_stats(logits):
    m = jnp.max(logits, axis=-1, keepdims=True)
    e = jnp.exp(logits - m)
    s = jnp.sum(e, axis=-1, keepdims=True)
    return e / s, (m + jnp.log(s))[..., 0]


def _strided(t, d):
    b, s = t.shape[:2]
    return t.reshape(b, s // d, d, *t.shape[2:]).swapaxes(1, 2).reshape(b * d, s // d, *t.shape[2:])


def _unstrided(t, b, d):
    ls = t.shape[1]
    return t.reshape(b, d, ls, *t.shape[2:]).swapaxes(1, 2).reshape(b, ls * d, *t.shape[2:])


def dilated_branch_prompt(q, k, v, bias, d, n):
    bsz = q.shape[0]
    qs, ks, vs = _strided(q, d), _strided(k, d), _strided(v, d)
    nseq, ls = qs.shape[:2]
    bq = min(BLOCK_Q, ls)
    nb = -(-ls // bq)
    lp = nb * bq
    qs = jnp.pad(qs, ((0, 0), (0, lp - ls), (0, 0), (0, 0)))
    ks = jnp.pad(ks, ((0, 0), (n, lp - ls), (0, 0), (0, 0)))
    vs = jnp.pad(vs, ((0, 0), (n, lp - ls), (0, 0), (0, 0)))
    idx = np.arange(nb)[:, None] * bq + np.arange(bq + n)[None, :]
    kb, vb = ks[:, idx], vs[:, idx]
    qb = qs.reshape(nseq, nb, bq, N_HEADS, HEAD_DIM)
    logits = jnp.einsum('nbqhd,nbkhd->nbhqk', qb, kb,
                        preferred_element_type=jnp.float32) / math.sqrt(HEAD_DIM)
    step = np.arange(bq)[:, None] - np.arange(bq + n)[None, :] + n
    valid = ((step >= 0) & (step <= n))[None] & (idx - n >= 0)[:, None, :]
    logits = logits + bias[:, np.clip(step, 0, n)][None, None]
    logits = jnp.where(valid[None, :, None], logits, NEG)
    probs, lse = softmax_stats(logits)
    o = jnp.einsum('nbhqk,nbkhd->nbqhd', probs, vb.astype(jnp.float32))
    o = o.reshape(nseq, lp, N_HEADS, HEAD_DIM)[:, :ls]
    lse = lse.transpose(0, 1, 3, 2).reshape(nseq, lp, N_HEADS)[:, :ls]
    return _unstrided(o, bsz, d), _unstrided(lse, bsz, d)


def dilated_branch_sample(q, k_ext, v_ext, bias, d, n):
    t_new = q.shape[1]
    past = k_ext.shape[1] - t_new
    idx = past + np.arange(t_new)[:, None] - np.arange(n + 1)[None, :] * d
    valid = idx >= 0
    idx = np.maximum(idx, 0)
    kg, vg = k_ext[:, idx], v_ext[:, idx]
    logits = jnp.einsum('bthd,btkhd->bhtk', q, kg,
                        preferred_element_type=jnp.float32) / math.sqrt(HEAD_DIM)
    logits = logits + bias[None, :, None, :]
    logits = jnp.where(valid[None, None], logits, NEG)
    probs, lse = softmax_stats(logits)
    o = jnp.einsum('bhtk,btkhd->bthd', probs, vg.astype(jnp.float32))
    return o, lse.transpose(0, 2, 1)


def merge_branches(outs, lses):
    w = jax.nn.softmax(jnp.stack(lses, axis=0), axis=0)
    return jnp.sum(w[..., None] * jnp.stack(outs, axis=0), axis=0)


def attend_prompt(q, k, v, rel_bias):
    outs, lses = [], []
    for window, dil in BRANCHES:
        o, l = dilated_branch_prompt(q, k, v, branch_bias(rel_bias, window, dil), dil, window // dil)
        outs.append(o)
        lses.append(l)
    keep = min(W_MAX, q.shape[1])
    return merge_branches(outs, lses), k[:, -keep:], v[:, -keep:]


def attend_sample(q, k, v, rel_bias, cache_k, cache_v):
    buf = cache_k.shape[1]
    k_ext = jnp.concatenate([cache_k.astype(k.dtype), k], axis=1)
    v_ext = jnp.concatenate([cache_v.astype(v.dtype), v], axis=1)
    outs, lses = [], []
    for window, dil in BRANCHES:
        o, l = dilated_branch_sample(q, k_ext, v_ext, branch_bias(rel_bias, window, dil), dil, window // dil)
        outs.append(o)
        lses.append(l)
    return merge_branches(outs, lses), k_ext[:, -buf:], v_ext[:, -buf:]


def s5_mixer(u, h0, a_re, a_im, log_dt, b_re, b_im, c_re, c_im, d_skip, w_glu, b_glu):
    f32 = jnp.float32
    bsz, seq, _ = u.shape
    uf = u.astype(f32)
    lam = lax.complex(a_re.astype(f32), a_im.astype(f32))
    dt = jnp.exp(log_dt.astype(f32))[:, None]
    abar = jnp.exp(lam * dt)
    bbar = ((abar - 1.0) / lam)[:, :, None] * lax.complex(b_re.astype(f32), b_im.astype(f32))
    ug = uf.reshape(bsz, seq, N_SSM_GROUPS, SSM_GROUP).astype(jnp.complex64)
    bu = jnp.einsum('blgi,gni->blgn', ug, bbar)
    bu = bu.at[:, 0].add(abar * h0)
    a = jnp.broadcast_to(abar, (1, seq) + abar.shape)

    def combine(left, right):
        a_l, b_l = left
        a_r, b_r = right
        return a_r * a_l, a_r * b_l + b_r

    _, h = lax.associative_scan(combine, (a, bu), axis=1)
    c = lax.complex(c_re.astype(f32), c_im.astype(f32))
    y = jnp.real(jnp.einsum('blgn,gon->blgo', h, c)).reshape(bsz, seq, D_SSM) + d_skip.astype(f32) * uf
    z = jax.nn.gelu(y)
    out = z * jax.nn.sigmoid(z @ w_glu.astype(f32) + b_glu.astype(f32))
    return out.astype(u.dtype), h[:, -1]


def trunk_layer(x, p, lw, rel_bias, attend, h0):
    bsz, seq, _ = x.shape
    h = rmsnorm(x, lw['norm_ffn'][0])
    x = x + 0.5 * swiglu(h, lw['ffn_w_gate'][0], lw['ffn_w_up'][0], lw['ffn_w_down'][0])
    h = rmsnorm(x, lw['norm_mix'])
    q, k, v, u = jnp.split(h @ lw['w_in'], [D_ATT, 2 * D_ATT, 3 * D_ATT], axis=-1)
    heads = (bsz, seq, N_HEADS, HEAD_DIM)
    att, k_state, v_state = attend(q.reshape(heads), k.reshape(heads), v.reshape(heads), rel_bias)
    ssm, h_last = s5_mixer(u, h0, lw['ssm_a_re'], lw['ssm_a_im'], lw['ssm_log_dt'], lw['ssm_b_re'],
                           lw['ssm_b_im'], lw['ssm_c_re'], lw['ssm_c_im'], lw['ssm_d'], lw['w_glu'], lw['b_glu'])
    mixed = jnp.concatenate([rmsnorm(att.reshape(bsz, seq, D_ATT).astype(x.dtype), lw['norm_att_out']),
                             rmsnorm(ssm, lw['norm_ssm_out'])], axis=-1)
    x = x + mixed @ lw['w_out']
    h = rmsnorm(x, lw['norm_ffn'][1])
    x = x + 0.5 * swiglu(h, lw['ffn_w_gate'][1], lw['ffn_w_up'][1], lw['ffn_w_down'][1])
    gate = jax.nn.sigmoid(rmsnorm(x, lw['norm_ple']) @ lw['w_ple_gate'])
    x = x + gate * (p.astype(x.dtype) @ lw['w_ple_proj'])
    return x, k_state, v_state, h_last


def setup_inputs(seed: int = 0) -> dict:
    key = jax.random.key(seed)
    ks = iter(jax.random.split(key, 40))
    f32 = jnp.float32

    def nrm(shape, scale):
        return scale * jax.random.normal(next(ks), shape, f32)

    def gain(shape):
        return 1.0 + 0.05 * jax.random.normal(next(ks), shape, f32)

    cache_len = min(W_MAX, PAST_LEN)
    G, N, I = N_SSM_GROUPS, SSM_STATE, SSM_GROUP
    return {
        'x_prompt': nrm((BATCH, SEQ, D_MODEL), 1.0),
        'x_sample': nrm((DEC_BATCH, DEC_SEQ, D_MODEL), 1.0),
        'p_prompt': nrm((DEPTH, BATCH, SEQ, D_PLE), 1.0),
        'p_sample': nrm((DEPTH, DEC_BATCH, DEC_SEQ, D_PLE), 1.0),
        'cache_k': nrm((DEPTH, DEC_BATCH, cache_len, N_HEADS, HEAD_DIM), 1.0),
        'cache_v': nrm((DEPTH, DEC_BATCH, cache_len, N_HEADS, HEAD_DIM), 1.0),
        'state_ssm_re': nrm((DEPTH, DEC_BATCH, G, N), 0.1),
        'state_ssm_im': nrm((DEPTH, DEC_BATCH, G, N), 0.1),
        'rel_bias': nrm((N_BUCKETS, N_HEADS), 0.5),
        'w_in': nrm((DEPTH, D_MODEL, D_IN), D_MODEL ** -0.5),
        'w_out': nrm((DEPTH, D_MODEL, D_MODEL), D_MODEL ** -0.5),
        'norm_mix': gain((DEPTH, D_MODEL)),
        'norm_att_out': gain((DEPTH, D_ATT)),
        'norm_ssm_out': gain((DEPTH, D_SSM)),
        'norm_ffn': gain((DEPTH, 2, D_MODEL)),
        'ffn_w_gate': nrm((DEPTH, 2, D_MODEL, D_FF), D_MODEL ** -0.5),
        'ffn_w_up': nrm((DEPTH, 2, D_MODEL, D_FF), D_MODEL ** -0.5),
        'ffn_w_down': nrm((DEPTH, 2, D_FF, D_MODEL), D_FF ** -0.5),
        'ssm_a_re': -0.5 * jnp.exp(nrm((DEPTH, G, N), 0.02)),
        'ssm_a_im': jnp.pi * jnp.arange(N, dtype=f32)[None, None, :] + nrm((DEPTH, G, N), 0.02),
        'ssm_log_dt': jax.random.uniform(next(ks), (DEPTH, G), f32, math.log(1e-3), math.log(1e-1)),
        'ssm_b_re': nrm((DEPTH, G, N, I), (2 * I) ** -0.5),
        'ssm_b_im': nrm((DEPTH, G, N, I), (2 * I) ** -0.5),
        'ssm_c_re': nrm((DEPTH, G, I, N), (2 * N) ** -0.5),
        'ssm_c_im': nrm((DEPTH, G, I, N), (2 * N) ** -0.5),
        'ssm_d': nrm((DEPTH, D_SSM), 1.0),
        'w_glu': nrm((DEPTH, D_SSM, D_SSM), D_SSM ** -0.5),
        'b_glu': nrm((DEPTH, D_SSM), 0.01),
        'norm_ple': gain((DEPTH, D_MODEL)),
        'w_ple_gate': nrm((DEPTH, D_MODEL, D_MODEL), D_MODEL ** -0.5),
        'w_ple_proj': nrm((DEPTH, D_PLE, D_MODEL), D_PLE ** -0.5),
        'norm_final': gain((D_MODEL,)),
    }


def reference(x_prompt, x_sample, p_prompt, p_sample, cache_k, cache_v, state_ssm_re, state_ssm_im,
              rel_bias, w_in, w_out, norm_mix, norm_att_out, norm_ssm_out, norm_ffn, ffn_w_gate,
              ffn_w_up, ffn_w_down, ssm_a_re, ssm_a_im, ssm_log_dt, ssm_b_re, ssm_b_im, ssm_c_re,
              ssm_c_im, ssm_d, w_glu, b_glu, norm_ple, w_ple_gate, w_ple_proj, norm_final):
    f32 = jnp.float32
    xp, xs = x_prompt, x_sample
    kp, vp, rp, ip = [], [], [], []
    kss, vss, rss, iss = [], [], [], []
    for i in range(DEPTH):
        lw = dict(w_in=w_in[i], w_out=w_out[i], norm_mix=norm_mix[i], norm_att_out=norm_att_out[i],
                  norm_ssm_out=norm_ssm_out[i], norm_ffn=norm_ffn[i], ffn_w_gate=ffn_w_gate[i],
                  ffn_w_up=ffn_w_up[i], ffn_w_down=ffn_w_down[i], ssm_a_re=ssm_a_re[i],
                  ssm_a_im=ssm_a_im[i], ssm_log_dt=ssm_log_dt[i], ssm_b_re=ssm_b_re[i],
                  ssm_b_im=ssm_b_im[i], ssm_c_re=ssm_c_re[i], ssm_c_im=ssm_c_im[i], ssm_d=ssm_d[i],
                  w_glu=w_glu[i], b_glu=b_glu[i], norm_ple=norm_ple[i], w_ple_gate=w_ple_gate[i],
                  w_ple_proj=w_ple_proj[i])
        h0p = jnp.zeros((xp.shape[0], N_SSM_GROUPS, SSM_STATE), jnp.complex64)
        xp, k_new, v_new, h_last = trunk_layer(xp, p_prompt[i], lw, rel_bias, attend_prompt, h0p)
        kp.append(k_new)
        vp.append(v_new)
        rp.append(jnp.real(h_last))
        ip.append(jnp.imag(h_last))
        h0s = lax.complex(state_ssm_re[i].astype(f32), state_ssm_im[i].astype(f32))
        attend_s = functools.partial(attend_sample, cache_k=cache_k[i], cache_v=cache_v[i])
        xs, k_new, v_new, h_last = trunk_layer(xs, p_sample[i], lw, rel_bias, attend_s, h0s)
        kss.append(k_new)
        vss.append(v_new)
        rss.append(jnp.real(h_last))
        iss.append(jnp.imag(h_last))
    y_prompt = rmsnorm(xp, norm_final)
    y_sample = rmsnorm(xs, norm_final)
    return (y_prompt, y_sample, jnp.stack(kp), jnp.stack(vp), jnp.stack(rp), jnp.stack(ip),
            jnp.stack(kss), jnp.stack(vss), jnp.stack(rss), jnp.stack(iss))
```

```python
import math
import os
from contextlib import ExitStack
import numpy as np
import concourse.bass as bass
import concourse.mybir as mybir
from concourse.bass_utils import run_bass_kernel_spmd

F32, BF16 = mybir.dt.float32, mybir.dt.bfloat16
AF = mybir.ActivationFunctionType
ALU = mybir.AluOpType
AX = mybir.AxisListType

D = 1024; DFF = 2816; DEPTH = 2; NH = 8; HD = 64; EPS = 1e-6
NKF = DFF // 128
BRANCH_D = (1, 4, 16)
EW = 384
MAGIC = 12582912.0


def t5_bucket(dist):
    dist = np.asarray(dist, dtype=np.int64)
    exact = 16
    ratio = np.log(np.maximum(dist, 1) / exact) / np.log(2048 / exact)
    large = np.minimum(exact + (ratio * (32 - exact)).astype(np.int64), 31)
    return np.where(dist < exact, dist, large).astype(np.int32)


class Res:
    def __init__(self, ap, sem=None):
        self.ap = ap; self.ready = None; self.frees = []; self.sem = sem; self.dcnt = 0


class SemH:
    def __init__(self, sem):
        self.sem = sem; self.cnt = 0


class KB:
    def __init__(self, nc, es):
        self.nc = nc; self.es = es
        self.engs = {'pe': nc.tensor, 'act': nc.scalar, 'dve': nc.vector, 'pool': nc.gpsimd, 'sp': nc.sync}
        self.sem = {}; self.cnt = {}; self.waited = {}
        for e in ('pe', 'act', 'dve', 'pool'):
            self.sem[e] = es.enter_context(nc.semaphore("sem_" + e)); self.cnt[e] = 0
        self.nsem = 0
        self.out_evs = []
        self.last_dma = {}
        self.sem_pool = []
        self.phase_sems = None
        self.serial = set()

    def fence(self):
        for evt in list(self.last_dma.values()):
            self.wait('sp', evt)
        ET = mybir.EngineType
        self.nc.multi_engine_barrier([ET.PE, ET.Activation, ET.DVE, ET.SP])

    def newsem(self):
        if self.sem_pool:
            h = self.sem_pool.pop()
        else:
            self.nsem += 1
            h = SemH(self.es.enter_context(self.nc.semaphore("ds%d" % self.nsem)))
        if self.phase_sems is not None:
            self.phase_sems.append(h)
        return h

    def soft_fence(self):
        evs = [(e, self.sem[e], self.cnt[e]) for e in ('pe', 'act', 'dve') if self.cnt[e] > 0] + list(self.last_dma.values())
        for e in ('pe', 'act', 'dve', 'sp'):
            for v in evs:
                self.wait(e, v)

    def begin_phase(self):
        self.phase_sems = []

    def end_phase(self):
        self.sem_pool.extend(self.phase_sems)
        self.phase_sems = None

    def last(self, r):
        return ('dma', r.sem.sem, r.sem.cnt)

    def ev(self, e, ins):
        if e in self.serial:
            return (e, self.sem[e], self.cnt[e])
        self.cnt[e] += 1
        ins.then_inc(self.sem[e], 1)
        return (e, self.sem[e], self.cnt[e])

    def wait(self, e, *evs):
        for v in evs:
            if v is None:
                continue
            src, s, val = v
            if src == e:
                continue
            key = (e, s.name if hasattr(s, 'name') else id(s))
            if self.waited.get(key, 0) >= val:
                continue
            self.engs[e].wait_ge(s, val)
            self.waited[key] = val

    def wr(self, e, r):
        self.wait(e, r.ready, *r.frees); r.frees = []

    def rd(self, e, r):
        self.wait(e, r.ready)

    def dma(self, q, out, in_, r, reads=(), writes=True, track=True, waw=True, **kw):
        if r.sem is None:
            r.sem = self.newsem()
        for x in reads:
            self.rd(q, x)
        if writes:
            if waw:
                self.wr(q, r)
            else:
                self.wait(q, *r.frees); r.frees = []
        ins = self.engs[q].dma_start(out=out, in_=in_, **kw)
        r.sem.cnt += 16
        ins.then_inc(r.sem.sem, 16)
        evt = ('dma', r.sem.sem, r.sem.cnt)
        if track:
            self.last_dma[r.sem.sem.name] = evt
        for x in reads:
            x.frees.append(evt)
        if writes:
            r.ready = evt
        return evt


def _ap_range(ap):
    pst = ap.ap[0][0] if ap.ap[0][0] > 0 else (1 << 40)
    lo = ap.offset % pst
    hi = lo + 1
    for (st, cn) in ap.ap[1:]:
        hi += (cn - 1) * abs(st)
    return ap.tensor.name, lo, hi


class SerialEng:
    def __init__(self, kb, name, eng):
        self._kb = kb; self._name = name; self._eng = eng
        self._recs = {}
        self._selfw = 0
        kb.serial.add(name)

    def __getattr__(self, attr):
        real = getattr(self._eng, attr)
        if attr in ('wait_ge', 'dma_start', 'sem_inc'):
            return real
        kb = self._kb; name = self._name

        def call(*a, **k):
            accs = []
            for i, v in enumerate(a):
                if hasattr(v, 'tensor') and hasattr(v, 'ap'):
                    accs.append((_ap_range(v), i == 0))
            for key, v in k.items():
                if hasattr(v, 'tensor') and hasattr(v, 'ap'):
                    accs.append((_ap_range(v), key in ('out', 'accum_out')))
            need = 0
            for (tn, lo, hi), isw in accs:
                for (rlo, rhi, rc, rw) in self._recs.get(tn, ()):
                    if (rw or isw) and rlo < hi and lo < rhi and rc > need:
                        need = rc
            if need > self._selfw:
                self._eng.wait_ge(kb.sem[name], need)
                self._selfw = need
            ins = real(*a, **k)
            c = kb.cnt[name] + 1
            kb.cnt[name] = c
            ins.then_inc(kb.sem[name], 1)
            for (tn, lo, hi), isw in accs:
                lst = self._recs.setdefault(tn, [])
                lst.append((lo, hi, c, isw))
                if len(lst) > 48:
                    del lst[0:16]
            return ins
        return call


class Ring:
    def __init__(self, items):
        self.items = items; self.i = 0

    def next(self):
        r = self.items[self.i % len(self.items)]; self.i += 1
        return r


def build(S=4096):
    T = S + 16
    NT = S // 512
    KEEP = min(2048, S)
    nc = bass.Bass("TRN2", target_bir_lowering=False)
    es = ExitStack()
    kb = KB(nc, es)
    pe, pool, sp = nc.tensor, nc.gpsimd, nc.sync
    act = SerialEng(kb, 'act', nc.scalar)
    dve = SerialEng(kb, 'dve', nc.vector)

    def din(name, shape, dt=F32):
        return nc.dram_tensor(name, list(shape), dt, kind="ExternalInput").ap()

    def dout(name, shape, dt=F32):
        return nc.dram_tensor(name, list(shape), dt, kind="ExternalOutput").ap()

    def dscr(name, shape, dt):
        return nc.dram_tensor(name, list(shape), dt, kind="Internal").ap()

    sbn = [0]

    def sb(name, shape, dt=F32):
        sbn[0] += 1
        return es_cur[0].enter_context(nc.sbuf_tensor("%s_%d" % (name, sbn[0]), list(shape), dt))

    es_cur = [es]
    KDEBUG = bool(os.environ.get("KDEBUG"))
    dbg_names = []

    def dbg(name, ap, eng='dve'):
        if not KDEBUG:
            return
        shp = list(ap.shape)
        o = nc.dram_tensor("dbg_" + name, shp, ap.dtype, kind="ExternalOutput").ap()
        dbg_names.append("dbg_" + name)
        r = Res(None)
        evt = (eng, kb.sem[eng], kb.cnt[eng])
        kb.wait('sp', evt)
        kb.out_evs.append(kb.dma('sp', o, ap, r, writes=False, track=True))
        for e_ in ('dve', 'act', 'pe', 'pool'):
            kb.wait(e_, kb.last(r))

    xT = din("xT", [128, 8, T]); pT = din("pT", [DEPTH, 128, 2, T])
    cache_k = din("cache_k", [DEPTH, 4, 2048, 512]); cache_v = din("cache_v", [DEPTH, 4, 2048, 512])
    st_re = din("st_re", [DEPTH, 128, 16, 4]); st_im = din("st_im", [DEPTH, 128, 16, 4])
    rel_bias = din("rel_bias", [32, 8])
    w_in = din("w_in", [DEPTH, D, 2048]); w_out = din("w_out", [DEPTH, D, D])
    w_g = din("ffn_w_gate", [DEPTH, 2, D, DFF]); w_u = din("ffn_w_up", [DEPTH, 2, D, DFF])
    w_d = din("ffn_w_down", [DEPTH, 2, DFF, D])
    w_glu = din("w_glu", [DEPTH, 512, 512]); w_pg = din("w_ple_gate", [DEPTH, D, D])
    w_pp = din("w_ple_proj", [DEPTH, 256, D])
    NG = DEPTH * 40 + 8
    gains = din("gains", [128, NG])
    s_are = din("s_are", [DEPTH, 128, 16]); s_aim = din("s_aim", [DEPTH, 128, 16]); s_ldt = din("s_ldt", [DEPTH, 128, 16])
    s_d = din("s_d", [DEPTH, 128, 4]); s_bglu = din("s_bglu", [DEPTH, 128, 4])
    Bn_re = din("Bn_re", [DEPTH, 128, 16, 128]); Bn_im = din("Bn_im", [DEPTH, 128, 16, 128])
    Ct_re = din("Ct_re", [DEPTH, 128, 16, 128]); Ct_im = din("Ct_im", [DEPTH, 128, 16, 128])
    c_iota = din("c_iota", [128, 512]); c_oh = din("c_oh", [33, 3 * EW]); c_ident = din("c_ident", [128, 128])
    c_ohnew = din("c_ohnew", [4, 16, 16])

    yT = dout("yT", [128, 8, T])
    kT_p = dout("kT_p", [DEPTH, 8, 64, KEEP]); v_p = dout("v_p", [DEPTH, KEEP, 512])
    ssm_p_re = dout("ssm_p_re", [DEPTH, 128, 16]); ssm_p_im = dout("ssm_p_im", [DEPTH, 128, 16])
    k_s = dout("k_s", [DEPTH, 4, 2048, 512]); v_s = dout("v_s", [DEPTH, 4, 2048, 512])
    ssm_s_re = dout("ssm_s_re", [DEPTH, 128, 16, 4]); ssm_s_im = dout("ssm_s_im", [DEPTH, 128, 16, 4])

    Xs = dscr("Xs", [128, 8, T], F32)
    MIX = (dout if os.environ.get("KDEBUG") else dscr)("MIX", [D, T], BF16)
    QF = dscr("QF", [8, 64, S], BF16); KF = dscr("KF", [8, 64, S], BF16)
    VT = dscr("VT", [S, 512], BF16)
    U32 = dscr("U32", [128, 4, T], F32)
    QKVS = dscr("QKVS", [16, 1536], F32)
    Dsc = dscr("Dsc", [3, 8, 128, EW], F32)
    Dsc2 = dscr("Dsc2", [3, 8, 128, EW], F32)
    LBD = dscr("LBD", [128, 3 * 4 * 512], BF16)
    WGU = [[dscr("WGU%d%d" % (l, f), [11, 128, 4096], BF16) for f in range(2)] for l in range(DEPTH)]
    WD = [[dscr("WD%d%d" % (l, f), [8, 128, NKF * 128], BF16) for f in range(2)] for l in range(DEPTH)]
    WIN = [dscr("WIN%d" % l, [4, 128, 4096], BF16) for l in range(DEPTH)]
    WOUT = [dscr("WOUT%d" % l, [2, 128, 4096], BF16) for l in range(DEPTH)]
    WPG = [dscr("WPG%d" % l, [2, 128, 4096], BF16) for l in range(DEPTH)]
    WPP = [dscr("WPP%d" % l, [128, 2048], BF16) for l in range(DEPTH)]
    WGLU = [dscr("WGLU%d" % l, [128, 2048], BF16) for l in range(DEPTH)]

    cast_ev = {}

    cast_prev = [None]

    def cast_batch(key, pairs):
        r = Res(None)
        evt = None
        kb.wait('pool', cast_prev[0])
        for (o, i) in pairs:
            evt = kb.dma('pool', o, i, r, writes=False, track=False)
        cast_ev[key] = evt
        cast_prev[0] = evt

    def cast_ffn(l, f, fine=False):
        pairs = []
        for c in range(11):
            pairs.append((WGU[l][f][c, :, 0:2048].rearrange("p (k n) -> p k n", k=8),
                          w_g[l, f, :, c * 256:(c + 1) * 256].rearrange("(k p) n -> p k n", p=128)))
            pairs.append((WGU[l][f][c, :, 2048:4096].rearrange("p (k n) -> p k n", k=8),
                          w_u[l, f, :, c * 256:(c + 1) * 256].rearrange("(k p) n -> p k n", p=128)))
            if fine and c % 2 == 1:
                cast_batch(("gu", l, f, c // 2), pairs); pairs = []
        if fine:
            cast_batch(("gu", l, f, 5), pairs)
        else:
            cast_batch(("gu", l, f, 0), pairs)
            for j in range(1, 6):
                cast_ev[("gu", l, f, j)] = cast_ev[("gu", l, f, 0)]
        pairs = []
        for m in range(8):
            pairs.append((WD[l][f][m].rearrange("p (k n) -> p k n", k=NKF),
                          w_d[l, f, :, m * 128:(m + 1) * 128].rearrange("(k p) n -> p k n", p=128)))
            if fine and m % 2 == 1:
                cast_batch(("d", l, f, m // 2), pairs); pairs = []
        if not fine:
            cast_batch(("d", l, f, 0), pairs)
            for j in range(1, 4):
                cast_ev[("d", l, f, j)] = cast_ev[("d", l, f, 0)]

    def cast_sq(key, dst, src, ncols, kk=8):
        pairs = []
        nch = ncols // 512
        for c in range(nch):
            pairs.append((dst[c].rearrange("p (k n) -> p k n", k=kk),
                          src[:, c * 512:(c + 1) * 512].rearrange("(k p) n -> p k n", p=128)))
        cast_batch(key, pairs)

    for l in range(DEPTH):
        if l == 0:
            cast_ffn(l, 0, fine=True)
            cast_sq(("in", l), WIN[l], w_in[l], 2048)
        cast_sq(("out", l), WOUT[l], w_out[l], 1024)
        cast_batch(("glu", l), [(WGLU[l].rearrange("p (k n) -> p k n", k=4), w_glu[l].rearrange("(k p) n -> p k n", p=128))])
        cast_ffn(l, 1)
        cast_sq(("pg", l), WPG[l], w_pg[l], 1024)
        cast_batch(("pp", l), [(WPP[l].rearrange("p (k n) -> p k n", k=2), w_pp[l].rearrange("(k p) n -> p k n", p=128))])
        if l + 1 < DEPTH:
            cast_ffn(l + 1, 0)
            cast_sq(("in", l + 1), WIN[l + 1], w_in[l + 1], 2048)

    psum = [Res(es.enter_context(nc.psum_tensor("ps%d" % i, [128, 512], F32))) for i in range(8)]
    poolA = Ring(psum[0:4]); poolB = Ring(psum[4:6]); poolC = Ring(psum[6:8])

    ones_bf = Res(sb("ones_bf", [128, 128], BF16))
    kb.ev('dve', dve.memset(ones_bf.ap[:], 1.0))
    ones_bf.ready = kb.ev('dve', dve.memset(ones_bf.ap[:], 1.0))
    gsb = Res(sb("gains_sb", [128, NG]))
    kb.dma('sp', gsb.ap[:], gains, gsb)
    ident = Res(sb("ident", [128, 128]))
    kb.dma('sp', ident.ap[:], c_ident, ident)

    def gcol(l, which):
        base = l * 40
        off = {'ffn0': 0, 'mix': 8, 'att': 16, 'ssm': 20, 'ffn1': 24, 'ple': 32}[which]
        return base + off
    GFINAL = DEPTH * 40

    cpy = Res(None)
    for l in range(DEPTH):
        for (src, dst) in ((cache_k, k_s), (cache_v, v_s)):
            for b in range(4):
                for part in range(4):
                    r0 = 4 + part * 511
                    kb.out_evs.append(kb.dma('pool', dst[l, b, r0 - 4:r0 - 4 + 511, :], src[l, b, r0:r0 + 511, :], cpy, writes=False, track=False))

    identb = Res(sb("identb", [128, 128], BF16))
    EBs0 = Res(sb("EBs0", [128, 8, 4]))
    EBs12 = Res(sb("EBs12", [128, 2, 8]))
    EBnew = Res(sb("EBnew", [16, 16, 8]))
    with ExitStack() as es2:
        es_cur[0] = es2
        LBt = Res(sb("LBt", [128, 3, 4, 512], BF16))
        rb = Res(sb("rb", [32, 8])); eb = Res(sb("eb", [32, 8])); ebr = Res(sb("ebr", [32, 8, 128]))
        oh = Res(sb("oh", [33, 3 * EW])); erep = Res(sb("erep", [128, EW])); stgs = Ring([Res(sb("stg%d" % i, [128, 512])) for i in range(2)]); stg = None
        ohn = Res(sb("ohn", [4, 16, 16]))
        kb.dma('sp', rb.ap[:], rel_bias, rb)
        kb.dma('sp', oh.ap[:], c_oh, oh)
        kb.dma('sp', ohn.ap[:], c_ohnew, ohn)
        kb.rd('act', rb)
        eb.ready = kb.ev('act', act.activation(out=eb.ap[:], in_=rb.ap[:], func=AF.Exp))
        kb.rd('dve', eb)
        ebr.ready = kb.ev('dve', dve.tensor_copy(out=ebr.ap[:], in_=eb.ap[:].unsqueeze(2).to_broadcast([32, 8, 128])))
        dres = Res(None)
        for g in range(3):
            for h in range(8):
                ps = poolC.next()
                kb.wr('pe', ps); kb.rd('pe', ebr); kb.rd('pe', oh)
                ps.ready = kb.ev('pe', pe.matmul(ps.ap[:, 0:EW], lhsT=ebr.ap[:, h, :], rhs=oh.ap[0:32, g * EW:(g + 1) * EW], start=True, stop=True))
                kb.rd('dve', ps); kb.wr('dve', erep)
                e1 = kb.ev('dve', dve.tensor_copy(out=erep.ap[:], in_=ps.ap[:, 0:EW]))
                ps.frees.append(e1); erep.ready = e1
                kb.dma('sp', Dsc[g, h], erep.ap[:], dres, reads=(erep,), writes=False)
        dres_all = kb.last(dres)
        kb.wait('sp', dres_all)
        rbr = Res(sb("rbr", [33, 8, 128]))
        kb.rd('dve', rb)
        dve.memset(rbr.ap[32:33, :, :], -30000.0)
        rbr.ready = kb.ev('dve', dve.tensor_copy(out=rbr.ap[0:32], in_=rb.ap[:].unsqueeze(2).to_broadcast([32, 8, 128])))
        dres2 = Res(None)
        for g in range(3):
            for h in range(8):
                ps = poolC.next()
                kb.wr('pe', ps); kb.rd('pe', rbr); kb.rd('pe', oh)
                ps.ready = kb.ev('pe', pe.matmul(ps.ap[:, 0:EW], lhsT=rbr.ap[:, h, :], rhs=oh.ap[:, g * EW:(g + 1) * EW], start=True, stop=True))
                kb.rd('dve', ps); kb.wr('dve', erep)
                e1 = kb.ev('dve', dve.tensor_copy(out=erep.ap[:], in_=ps.ap[:, 0:EW]))
                ps.frees.append(e1); erep.ready = e1
                kb.dma('sp', Dsc2[g, h], erep.ap[:], dres2, reads=(erep,), writes=False)
        kb.wait('sp', kb.last(dres2))
        for g in range(3):
            for hp in range(4):
                stg = stgs.next()
                kb.wr('sp', stg)
                for hh in range(2):
                    h = 2 * hp + hh
                    for slot in range(2):
                        off = 127 if slot == 0 else 255
                        src = bass.AP(tensor=Dsc2.tensor, offset=Dsc2[g, h].offset + off, ap=[[EW - 1, 128], [1, 128]])
                        kb.dma('sp', stg.ap[:, (hh * 2 + slot) * 128:(hh * 2 + slot + 1) * 128], src, stg, waw=False)
                kb.rd('dve', stg); kb.wr('dve', LBt)
                e1 = kb.ev('dve', dve.tensor_copy(out=LBt.ap[:, g, hp, :], in_=stg.ap[:]))
                stg.frees.append(e1); LBt.ready = e1
        kb.rd('dve', ident)
        identb.ready = kb.ev('dve', dve.tensor_copy(out=identb.ap[:], in_=ident.ap[:]))
        kb.dma('sp', LBD, LBt.ap[:].rearrange("p a b c -> p (a b c)"), LBt, reads=(LBt,), writes=False)
        for h in range(8):
            src = bass.AP(tensor=Dsc.tensor, offset=Dsc[0, h].offset + 255, ap=[[EW - 1, 128], [1, 4]])
            kb.dma('sp', EBs0.ap[:, h, :], src, EBs0, waw=False)
            for g in (1, 2):
                src = bass.AP(tensor=Dsc.tensor, offset=Dsc[g, h].offset + 255, ap=[[EW - 1, 128], [1, 1]])
                kb.dma('sp', EBs12.ap[:, g - 1, h:h + 1], src, EBs12, allow_slow_non_contiguous=True, waw=False)
        psn = poolC.next()
        kb.wr('pe', psn); kb.rd('pe', ohn); kb.rd('pe', eb)
        for q in range(16):
            e1 = pe.matmul(psn.ap[0:16, q * 8:(q + 1) * 8], lhsT=ohn.ap[:, q, :], rhs=eb.ap[0:4, :], start=True, stop=True)
        psn.ready = kb.ev('pe', e1)
        kb.rd('dve', psn)
        e1 = kb.ev('dve', dve.tensor_copy(out=EBnew.ap[:].rearrange("k q h -> k (q h)"), in_=psn.ap[0:16, 0:128]))
        psn.frees.append(e1); EBnew.ready = e1
        kb.soft_fence()
    es_cur[0] = es

    def row_phase(phase):
        with ExitStack() as esr:
            es_cur[0] = esr
            kb.begin_phase()
            NW = 4
            wring = Ring([Res(sb("wslot%d" % i, [128, 4096], BF16)) for i in range(NW)])
            xts = Ring([Res(sb("xt%d" % i, [128, 8, 512])) for i in range(2)])
            hs = Ring([Res(sb("h%d" % i, [128, 8, 512], BF16)) for i in range(2)])
            sqs = Res(sb("sq", [128, 8, 512], BF16))
            abuf = Res(sb("abuf", [128, NKF, 512], BF16))
            sgs = Ring([Res(sb("sg%d" % i, [128, 512])) for i in range(2)])
            rstd = Res(sb("rstd", [128, 512]))
            if phase >= 1:
                mixs = Ring([Res(sb("mix%d" % i, [128, 8, 512], BF16)) for i in range(1)])
                mixn = Res(sb("mixn", [128, 8, 512], BF16))
                pts = Ring([Res(sb("pt%d" % i, [128, 2, 512])) for i in range(2)])
                ptb = Res(sb("ptb", [128, 2, 512], BF16))
                gate = Res(sb("gate", [128, 8, 512]))
            if phase <= 1:
                qsb = Ring([Res(sb("qsb%d" % i, [64, 8, 512], BF16)) for i in range(1)])
                ksb = Ring([Res(sb("ksb%d" % i, [64, 8, 512], BF16)) for i in range(1)])
                k32 = Ring([Res(sb("k32_%d" % i, [64, 512])) for i in range(2)])
                usb = Ring([Res(sb("usb%d" % i, [128, 4, 512])) for i in range(1)])
                vbf = Ring([Res(sb("vbf%d" % i, [128, 512], BF16)) for i in range(2)])
                v32 = Ring([Res(sb("v32_%d" % i, [128, 512])) for i in range(2)])
                tms = Res(sb("tms", [16, 1536]))
            tmp = Ring([Res(sb("tmp%d" % i, [128, 512])) for i in range(2)])

            steps = []

            def norm(xt, n, gc0, hout, kr=range(8), src=None):
                src = src or xt
                nk = len(kr)

                def fn(_):
                    kb.rd('act', src); kb.wr('act', sqs); kb.rd('dve', src); kb.wr('dve', sqs)
                    eA = eD = None
                    for i, k in enumerate(kr):
                        if i % 2 == 0:
                            eA = kb.ev('act', act.activation(out=sqs.ap[:, k, :n], in_=src.ap[:, k, :n], func=AF.Square))
                        else:
                            eD = kb.ev('dve', dve.tensor_tensor(out=sqs.ap[:, k, :n], in0=src.ap[:, k, :n], in1=src.ap[:, k, :n], op=ALU.mult))
                    sqs.ready = eA; src.frees.append(eA); src.frees.append(eD)
                    ps = poolC.next()
                    kb.wr('pe', ps); kb.wait('pe', eA, eD); kb.rd('pe', ones_bf)
                    for i, k in enumerate(kr):
                        e2 = pe.matmul(ps.ap[:, :n], lhsT=ones_bf.ap[:], rhs=sqs.ap[:, k, :n], start=(i == 0), stop=(i == nk - 1))
                    e2 = kb.ev('pe', e2); ps.ready = e2; sqs.frees.append(e2)
                    kb.rd('act', ps); kb.wr('act', rstd)
                    act.activation(out=rstd.ap[:, :n], in_=ps.ap[:, :n], func=AF.Ln, scale=1.0 / (128 * nk), bias=epsb.ap[:, 0:1])
                    e3 = kb.ev('act', act.activation(out=rstd.ap[:, :n], in_=rstd.ap[:, :n], func=AF.Exp, scale=-0.5))
                    ps.frees.append(e3)
                    kb.wait('dve', e3)
                    kb.rd('dve', src); kb.wr('dve', hout); kb.rd('dve', gsb)
                    for i, k in enumerate(kr):
                        e4 = dve.scalar_tensor_tensor(out=hout.ap[:, k, :n], in0=src.ap[:, k, :n], scalar=gsb.ap[:, gc0 + i:gc0 + i + 1],
                                                      in1=rstd.ap[:, :n], op0=ALU.mult, op1=ALU.mult)
                    e4 = kb.ev('dve', e4); hout.ready = e4; src.frees.append(e4); rstd.frees.append(e4); rstd.ready = e4
                steps.append((None, fn))

            def ffn(l, f, xt, h, n):
                for c in range(11):
                    def fn(w, c=c):
                        for mi in range(2):
                            m = 2 * c + mi
                            pg = poolA.next(); pu = poolA.next()
                            kb.wr('pe', pg); kb.wr('pe', pu); kb.rd('pe', h); kb.rd('pe', w)
                            for k in range(8):
                                e1 = pe.matmul(pg.ap[:, :n], lhsT=w.ap[:, k * 256 + mi * 128:k * 256 + mi * 128 + 128], rhs=h.ap[:, k, :n], start=(k == 0), stop=(k == 7))
                            pg.ready = kb.ev('pe', e1)
                            for k in range(8):
                                e1 = pe.matmul(pu.ap[:, :n], lhsT=w.ap[:, 2048 + k * 256 + mi * 128:2048 + k * 256 + mi * 128 + 128], rhs=h.ap[:, k, :n], start=(k == 0), stop=(k == 7))
                            e1 = kb.ev('pe', e1); pu.ready = e1
                            if mi == 1:
                                w.frees.append(e1)
                                if c == 10:
                                    h.frees.append(e1)
                            sg = sgs.next()
                            kb.rd('act', pg); kb.wr('act', sg)
                            e2 = kb.ev('act', act.activation(out=sg.ap[:, :n], in_=pg.ap[:, :n], func=AF.Silu))
                            sg.ready = e2; pg.frees.append(e2)
                            kb.rd('dve', sg); kb.rd('dve', pu)
                            if m == 0:
                                kb.wr('dve', abuf)
                            e3 = kb.ev('dve', dve.tensor_tensor(out=abuf.ap[:, m, :n], in0=sg.ap[:, :n], in1=pu.ap[:, :n], op=ALU.mult))
                            sg.frees.append(e3); pu.frees.append(e3); abuf.ready = e3
                    steps.append(((WGU[l][f][c], ("gu", l, f, c // 2)), fn))
                for m in range(8):
                    def fn(w, m=m):
                        ps = poolB.next()
                        kb.wr('pe', ps); kb.rd('pe', abuf); kb.rd('pe', w)
                        for k in range(NKF):
                            e1 = pe.matmul(ps.ap[:, :n], lhsT=w.ap[:, k * 128:(k + 1) * 128], rhs=abuf.ap[:, k, :n], start=(k == 0), stop=(k == NKF - 1))
                        e1 = kb.ev('pe', e1); ps.ready = e1; w.frees.append(e1)
                        if m == 7:
                            abuf.frees.append(e1)
                        kb.rd('dve', ps); kb.wr('dve', xt)
                        e2 = kb.ev('dve', dve.scalar_tensor_tensor(out=xt.ap[:, m, :n], in0=ps.ap[:, :n], scalar=0.5, in1=xt.ap[:, m, :n], op0=ALU.mult, op1=ALU.add))
                        ps.frees.append(e2); xt.ready = e2
                    steps.append(((WD[l][f][m][:, 0:NKF * 128], ("d", l, f, m // 2)), fn))

            def proj(l, h, n, c0, is_sample):
                def fq(w):
                    if is_sample:
                        ps = poolB.next(); kb.wr('pe', ps); kb.rd('pe', h); kb.rd('pe', w)
                        for k in range(8):
                            e1 = pe.matmul(ps.ap[:16, :], lhsT=h.ap[:, k, :16], rhs=w.ap[:, k * 512:(k + 1) * 512], start=(k == 0), stop=(k == 7))
                        e1 = kb.ev('pe', e1); ps.ready = e1; w.frees.append(e1)
                        kb.rd('act', ps); kb.wr('act', tms)
                        e2 = kb.ev('act', act.mul(out=tms.ap[:, 0:512], in_=ps.ap[:16, :], mul=0.125))
                        ps.frees.append(e2); tms.ready = e2
                        return
                    q = qsb.next(); kb.wr('act', q)
                    for hd in range(8):
                        ps = poolB.next(); kb.wr('pe', ps); kb.rd('pe', h); kb.rd('pe', w)
                        for k in range(8):
                            e1 = pe.matmul(ps.ap[:64, :n], lhsT=w.ap[:, k * 512 + hd * 64:k * 512 + hd * 64 + 64], rhs=h.ap[:, k, :n], start=(k == 0), stop=(k == 7))
                        e1 = kb.ev('pe', e1); ps.ready = e1
                        kb.rd('act', ps)
                        e2 = kb.ev('act', act.mul(out=q.ap[:, hd, :n], in_=ps.ap[:64, :n], mul=0.125))
                        ps.frees.append(e2); q.ready = e2
                    w.frees.append(e1)
                    kb.dma('sp', QF[:, :, c0:c0 + n].rearrange("h p s -> p h s"), q.ap[:, :, :n], q, reads=(q,), writes=False)
                if 'fq' in os.environ.get('KP', 'fqfkfvfu'):
                    steps.append(((WIN[l][0], ("in", l)), fq))

                def fk(w):
                    if is_sample:
                        ps = poolB.next(); kb.wr('pe', ps); kb.rd('pe', h); kb.rd('pe', w)
                        for k in range(8):
                            e1 = pe.matmul(ps.ap[:16, :], lhsT=h.ap[:, k, :16], rhs=w.ap[:, k * 512:(k + 1) * 512], start=(k == 0), stop=(k == 7))
                        e1 = kb.ev('pe', e1); ps.ready = e1; w.frees.append(e1)
                        kb.rd('act', ps); kb.wr('act', tms)
                        e2 = kb.ev('act', act.copy(out=tms.ap[:, 512:1024], in_=ps.ap[:16, :]))
                        ps.frees.append(e2); tms.ready = e2
                        return
                    kk = ksb.next(); kb.wr('act', kk)
                    for hd in range(8):
                        ps = poolB.next(); kb.wr('pe', ps); kb.rd('pe', h); kb.rd('pe', w)
                        for k in range(8):
                            e1 = pe.matmul(ps.ap[:64, :n], lhsT=w.ap[:, k * 512 + hd * 64:k * 512 + hd * 64 + 64], rhs=h.ap[:, k, :n], start=(k == 0), stop=(k == 7))
                        e1 = kb.ev('pe', e1); ps.ready = e1
                        if c0 >= S - KEEP:
                            k3 = k32.next(); kb.rd('dve', ps); kb.wr('dve', k3)
                            e3 = kb.ev('dve', dve.tensor_copy(out=k3.ap[:, :n], in_=ps.ap[:64, :n]))
                            ps.frees.append(e3); k3.ready = e3
                            kb.rd('act', k3)
                            e2 = kb.ev('act', act.copy(out=kk.ap[:, hd, :n], in_=k3.ap[:, :n]))
                            k3.frees.append(e2); kk.ready = e2
                            o0 = c0 - (S - KEEP)
                            kb.out_evs.append(kb.dma('sp', kT_p[l, hd, :, o0:o0 + n], k3.ap[:, :n], k3, reads=(k3,), writes=False))
                        else:
                            kb.rd('act', ps)
                            e2 = kb.ev('act', act.copy(out=kk.ap[:, hd, :n], in_=ps.ap[:64, :n]))
                            ps.frees.append(e2); kk.ready = e2
                    w.frees.append(e1)
                    kb.dma('sp', KF[:, :, c0:c0 + n].rearrange("h p s -> p h s"), kk.ap[:, :, :n], kk, reads=(kk,), writes=False)
                if 'fk' in os.environ.get('KP', 'fqfkfvfu'):
                    steps.append(((WIN[l][1], ("in", l)), fk))

                def fv(w):
                    if is_sample:
                        ps = poolB.next(); kb.wr('pe', ps); kb.rd('pe', h); kb.rd('pe', w)
                        for k in range(8):
                            e1 = pe.matmul(ps.ap[:16, :], lhsT=h.ap[:, k, :16], rhs=w.ap[:, k * 512:(k + 1) * 512], start=(k == 0), stop=(k == 7))
                        e1 = kb.ev('pe', e1); ps.ready = e1; w.frees.append(e1)
                        kb.rd('act', ps); kb.wr('act', tms)
                        e2 = kb.ev('act', act.copy(out=tms.ap[:, 1024:1536], in_=ps.ap[:16, :]))
                        ps.frees.append(e2); tms.ready = e2
                        kb.dma('sp', QKVS, tms.ap[:], tms, reads=(tms,), writes=False)
                        qkvs_ev[l] = kb.last(tms)
                        for b in range(4):
                            kb.out_evs.append(kb.dma('sp', k_s[l, b, 2044:2048, :], tms.ap[b * 4:(b + 1) * 4, 512:1024], tms, reads=(tms,), writes=False))
                            kb.out_evs.append(kb.dma('sp', v_s[l, b, 2044:2048, :], tms.ap[b * 4:(b + 1) * 4, 1024:1536], tms, reads=(tms,), writes=False))
                        return
                    for tb in range(n // 128):
                        ps = poolB.next(); kb.wr('pe', ps); kb.rd('pe', h); kb.rd('pe', w)
                        for k in range(8):
                            e1 = pe.matmul(ps.ap[:, :], lhsT=h.ap[:, k, tb * 128:(tb + 1) * 128], rhs=w.ap[:, k * 512:(k + 1) * 512], start=(k == 0), stop=(k == 7))
                        e1 = kb.ev('pe', e1); ps.ready = e1
                        t0 = c0 + tb * 128
                        v3 = v32.next(); kb.rd('dve', ps); kb.wr('dve', v3)
                        e3 = kb.ev('dve', dve.tensor_copy(out=v3.ap[:], in_=ps.ap[:]))
                        ps.frees.append(e3); v3.ready = e3
                        vb = vbf.next(); kb.rd('act', v3); kb.wr('act', vb)
                        e2 = kb.ev('act', act.copy(out=vb.ap[:], in_=v3.ap[:]))
                        v3.frees.append(e2); vb.ready = e2
                        kb.dma('sp', VT[t0:t0 + 128, :], vb.ap[:], vb, reads=(vb,), writes=False)
                        if t0 >= S - KEEP:
                            kb.out_evs.append(kb.dma('sp', v_p[l, t0 - (S - KEEP):t0 - (S - KEEP) + 128, :], v3.ap[:], v3, reads=(v3,), writes=False))
                    w.frees.append(e1)
                if 'fv' in os.environ.get('KP', 'fqfkfvfu'):
                    steps.append(((WIN[l][2], ("in", l)), fv))

                def fu(w):
                    u = usb.next(); kb.wr('act', u)
                    for m in range(4):
                        ps = poolB.next(); kb.wr('pe', ps); kb.rd('pe', h); kb.rd('pe', w)
                        for k in range(8):
                            e1 = pe.matmul(ps.ap[:, :n], lhsT=w.ap[:, k * 512 + m * 128:k * 512 + m * 128 + 128], rhs=h.ap[:, k, :n], start=(k == 0), stop=(k == 7))
                        e1 = kb.ev('pe', e1); ps.ready = e1
                        kb.rd('act', ps)
                        e2 = kb.ev('act', act.copy(out=u.ap[:, m, :n], in_=ps.ap[:, :n]))
                        ps.frees.append(e2); u.ready = e2
                    w.frees.append(e1); h.frees.append(e1)
                    kb.dma('sp', U32[:, :, c0:c0 + n], u.ap[:, :, :n], u, reads=(u,), writes=False)
                if 'fu' in os.environ.get('KP', 'fqfkfvfu'):
                    steps.append(((WIN[l][3], ("in", l)), fu))

            def lin_res(wscr, key, nchunks, xt, rhs_res, n):
                for c in range(nchunks):
                    def fn(w, c=c):
                        for mi in range(4):
                            m = 4 * c + mi
                            ps = poolB.next(); kb.wr('pe', ps); kb.rd('pe', rhs_res); kb.rd('pe', w)
                            for k in range(8):
                                e1 = pe.matmul(ps.ap[:, :n], lhsT=w.ap[:, k * 512 + mi * 128:k * 512 + mi * 128 + 128], rhs=rhs_res.ap[:, k, :n], start=(k == 0), stop=(k == 7))
                            e1 = kb.ev('pe', e1); ps.ready = e1
                            kb.rd('dve', ps); kb.wr('dve', xt)
                            e2 = kb.ev('dve', dve.tensor_tensor(out=xt.ap[:, m, :n], in0=ps.ap[:, :n], in1=xt.ap[:, m, :n], op=ALU.add))
                            ps.frees.append(e2); xt.ready = e2
                        w.frees.append(e1)
                        if c == nchunks - 1:
                            rhs_res.frees.append(e1)
                    steps.append(((wscr[c], key), fn))

            def ple(l, xt, h, pt, n):
                for c in range(2):
                    def fn(w, c=c):
                        for mi in range(4):
                            m = 4 * c + mi
                            ps = poolB.next(); kb.wr('pe', ps); kb.rd('pe', h); kb.rd('pe', w)
                            for k in range(8):
                                e1 = pe.matmul(ps.ap[:, :n], lhsT=w.ap[:, k * 512 + mi * 128:k * 512 + mi * 128 + 128], rhs=h.ap[:, k, :n], start=(k == 0), stop=(k == 7))
                            e1 = kb.ev('pe', e1); ps.ready = e1
                            kb.rd('act', ps)
                            if m == 0:
                                kb.wr('act', gate)
                            e2 = kb.ev('act', act.activation(out=gate.ap[:, m, :n], in_=ps.ap[:, :n], func=AF.Sigmoid))
                            ps.frees.append(e2); gate.ready = e2
                        w.frees.append(e1)
                        if c == 1:
                            h.frees.append(e1)
                    steps.append(((WPG[l][c], ("pg", l)), fn))

                def fp(w):
                    kb.rd('act', pt); kb.wr('act', ptb)
                    e0 = kb.ev('act', act.copy(out=ptb.ap[:, :, :n], in_=pt.ap[:, :, :n]))
                    ptb.ready = e0; pt.frees.append(e0)
                    for m in range(8):
                        ps = poolB.next(); kb.wr('pe', ps); kb.rd('pe', ptb); kb.rd('pe', w)
                        for k in range(2):
                            e1 = pe.matmul(ps.ap[:, :n], lhsT=w.ap[:, k * 1024 + m * 128:k * 1024 + m * 128 + 128], rhs=ptb.ap[:, k, :n], start=(k == 0), stop=(k == 1))
                        e1 = kb.ev('pe', e1); ps.ready = e1
                        tp = tmp.next()
                        kb.rd('dve', ps); kb.rd('dve', gate); kb.wr('dve', tp); kb.wr('dve', xt)
                        dve.tensor_tensor(out=tp.ap[:, :n], in0=ps.ap[:, :n], in1=gate.ap[:, m, :n], op=ALU.mult)
                        e2 = kb.ev('dve', dve.tensor_tensor(out=xt.ap[:, m, :n], in0=tp.ap[:, :n], in1=xt.ap[:, m, :n], op=ALU.add))
                        ps.frees.append(e2); xt.ready = e2; tp.ready = e2
                    w.frees.append(e1); ptb.frees.append(e1); gate.frees.append(e2)
                steps.append(((WPP[l], ("pp", l)), fp))

            tiles = [(j * 512, 512) for j in range(NT)] + [(S, 16)]
            if os.environ.get("KT"):
                tiles = [tiles[int(x)] for x in os.environ["KT"].split(",")]
            kproj = int(os.environ.get("KPROJ", "1"))
            pre = {}

            def emit_loads(ti):
                (c0_, n_) = tiles[ti]
                xt_ = xts.next()
                src_x_ = xT if phase == 0 else Xs

                def fload(_, xt=xt_, c0=c0_, n=n_, src_x=src_x_):
                    kb.dma('sp', xt.ap[:, :, :n], src_x[:, :, c0:c0 + n], xt)
                steps.append((None, fload))
                mx_ = pt_ = None
                if phase >= 1:
                    mx_ = mixs.next(); pt_ = pts.next()

                    def fl2(_, mx=mx_, pt=pt_, c0=c0_, n=n_, l=phase - 1):
                        kb.dma('sp', mx.ap[:, :, :n], MIX[:, c0:c0 + n].rearrange("(k p) t -> p k t", p=128), mx)
                        kb.dma('sp', pt.ap[:, :, :n], pT[l, :, :, c0:c0 + n], pt)
                    steps.append((None, fl2))
                pre[ti] = (xt_, mx_, pt_)

            emit_loads(0)
            for ti, (c0, n) in enumerate(tiles):
                is_s = (n == 16)
                xt, mx, pt = pre[ti]
                h = hs.next()
                if phase >= 1:
                    l = phase - 1
                    norm(None, n, gcol(l, 'att'), mixn, kr=range(0, 4), src=mx)
                    norm(None, n, gcol(l, 'ssm'), mixn, kr=range(4, 8), src=mx)
                    lin_res(WOUT[l], ("out", l), 2, xt, mixn, n)
                    norm(xt, n, gcol(l, 'ffn1'), h)
                    ffn(l, 1, xt, h, n)
                    if ti + 1 < len(tiles):
                        emit_loads(ti + 1)
                    h = hs.next()
                    norm(xt, n, gcol(l, 'ple'), h)
                    ple(l, xt, h, pt, n)
                    h = hs.next()
                if phase <= 1:
                    l = phase
                    norm(xt, n, gcol(l, 'ffn0'), h)
                    ffn(l, 0, xt, h, n)
                    if phase == 0 and ti + 1 < len(tiles):
                        emit_loads(ti + 1)
                    h = hs.next()
                    norm(xt, n, gcol(l, 'mix'), h)
                    if kproj:
                        proj(l, h, n, c0, is_s)

                    def fstore(_, xt=xt, c0=c0, n=n):
                        kb.dma('sp', Xs[:, :, c0:c0 + n], xt.ap[:, :, :n], xt, reads=(xt,), writes=False)
                        xs_evs.append(kb.last(xt))
                    steps.append((None, fstore))
                else:
                    yb = h

                    def fin(_, xt=xt, c0=c0, n=n):
                        kb.rd('act', xt); kb.wr('act', sqs)
                        for k in range(8):
                            e1 = act.activation(out=sqs.ap[:, k, :n], in_=xt.ap[:, k, :n], func=AF.Square)
                        e1 = kb.ev('act', e1); sqs.ready = e1
                        ps = poolC.next(); kb.wr('pe', ps); kb.rd('pe', sqs)
                        for k in range(8):
                            e2 = pe.matmul(ps.ap[:, :n], lhsT=ones_bf.ap[:], rhs=sqs.ap[:, k, :n], start=(k == 0), stop=(k == 7))
                        e2 = kb.ev('pe', e2); ps.ready = e2; sqs.frees.append(e2)
                        kb.rd('act', ps); kb.wr('act', rstd)
                        e3 = kb.ev('act', act.activation(out=rstd.ap[:, :n], in_=ps.ap[:, :n], func=AF.Sqrt, scale=1.0 / 1024, bias=epsb.ap[:, 0:1]))
                        ps.frees.append(e3)
                        kb.wait('dve', e3); kb.wr('dve', gate)
                        dve.reciprocal(out=rstd.ap[:, :n], in_=rstd.ap[:, :n])
                        for k in range(8):
                            e4 = dve.scalar_tensor_tensor(out=gate.ap[:, k, :n], in0=xt.ap[:, k, :n], scalar=gsb.ap[:, GFINAL + k:GFINAL + k + 1],
                                                          in1=rstd.ap[:, :n], op0=ALU.mult, op1=ALU.mult)
                        e4 = kb.ev('dve', e4); gate.ready = e4; xt.frees.append(e4); rstd.ready = e4
                        kb.out_evs.append(kb.dma('sp', yT[:, :, c0:c0 + n], gate.ap[:, :, :n], gate, reads=(gate,), writes=False))
                    steps.append((None, fin))

            wsteps = [i for i, s in enumerate(steps) if s[0] is not None]
            slot_of = {}
            nl = [0]

            def issue_loads(upto_step):
                while nl[0] < len(wsteps):
                    si = wsteps[nl[0]]
                    if nl[0] >= NW and wsteps[nl[0] - NW] >= upto_step:
                        break
                    if si > upto_step + 40:
                        break
                    w = wring.next()
                    (src, key) = steps[si][0]
                    kb.wait('sp', cast_ev[key])
                    ncol = src.shape[-1]
                    kb.dma('sp', w.ap[:, 0:ncol], src, w)
                    slot_of[si] = w
                    nl[0] += 1
            for i, (chunk, fn) in enumerate(steps):
                issue_loads(i)
                fn(slot_of.get(i))
            kb.fence()
            kb.end_phase()
        es_cur[0] = es

    epsb = Res(sb("epsb", [128, 1]))
    epsb.ready = kb.ev('dve', dve.memset(epsb.ap[:], EPS))
    kb.wait('act', epsb.ready)
    xs_evs = []
    qkvs_ev = {}

    def attention(l):
        with ExitStack() as esa:
            es_cur[0] = esa
            kb.begin_phase()
            nblk = S // 128
            LBt = Res(sb("LBt", [128, 3, 4, 512], BF16))
            kb.dma('sp', LBt.ap[:].rearrange("p a b c -> p (a b c)"), LBD, LBt)
            Vg = [Res(sb("Vg%d" % g, [128, nblk, 512], BF16)) for g in range(3)]
            for g, d in enumerate(BRANCH_D):
                nb = S // (128 * d)
                for r in range(d):
                    src = VT.rearrange("(b p r) f -> r p b f", r=d, p=128)[r]
                    for b0 in range(0, nb, 4):
                        b1 = min(nb, b0 + 4)
                        kb.dma('sp', Vg[g].ap[:, r * nb + b0:r * nb + b1, :], src[:, b0:b1, :], Vg[g], waw=False)
            Qs = Ring([Res(sb("Q%d" % i, [64, 2, S], BF16)) for i in range(1)])
            Ks = Ring([Res(sb("K%d" % i, [64, 2, S], BF16)) for i in range(1)])
            HALF = min(2048, S)
            acc = Res(sb("acc", [64, 2, 2, HALF]))
            Ps = Ring([Res(sb("P%d" % i, [128, 512], BF16)) for i in range(4)])
            poolO = Ring(psum[4:8])
            att = Res(sb("attb", [64, 2, HALF], BF16))
            for hp in range(4):
                Q = Qs.next(); K = Ks.next()
                kb.dma('sp', Q.ap[:], QF[2 * hp:2 * hp + 2].rearrange("h p s -> p h s"), Q)
                kb.dma('sp', K.ap[:], KF[2 * hp:2 * hp + 2].rearrange("h p s -> p h s"), K)
                for half in range(S // HALF):
                    kb.wr('act', acc)
                    acc.ready = kb.ev('act', act.memzero(acc.ap[:]))
                    kb.wait('dve', acc.ready)
                    aunits = []
                    for g, d in enumerate(BRANCH_D):
                        bh = HALF // (128 * d)
                        for r in range(d):
                            for b in range(half * bh, (half + 1) * bh):
                                aunits.append((g, d, r, b))

                    def emitA(g, d, r, b):
                        t0 = r + d * 128 * b
                        sl_q = slice(t0, t0 + 127 * d + 1, d)
                        nsl = 2 if b > 0 else 1
                        pS = poolA.next(); kb.wr('pe', pS); kb.rd('pe', Q); kb.rd('pe', K); kb.rd('pe', LBt); kb.rd('pe', identb)
                        pe.matmul(pS.ap[:, :], lhsT=identb.ap[:], rhs=LBt.ap[:, g, hp, :], start=True, stop=False)
                        for hh in range(2):
                            for s_i in range(nsl):
                                k0 = r + d * 128 * (b - s_i)
                                e1 = pe.matmul(pS.ap[:, (hh * 2 + s_i) * 128:(hh * 2 + s_i + 1) * 128], lhsT=K.ap[:, hh, k0:k0 + 127 * d + 1:d],
                                               rhs=Q.ap[:, hh, sl_q], start=False, stop=(hh == 1 and s_i == nsl - 1))
                        pS.ready = kb.ev('pe', e1)
                        P = Ps.next()
                        kb.rd('act', pS); kb.wr('act', P)
                        if nsl == 2:
                            e2 = act.activation(out=P.ap[:], in_=pS.ap[:], func=AF.Exp)
                        else:
                            for hh in range(2):
                                e2 = act.activation(out=P.ap[:, hh * 256:hh * 256 + 128], in_=pS.ap[:, hh * 256:hh * 256 + 128], func=AF.Exp)
                        e2 = kb.ev('act', e2); P.ready = e2; pS.frees.append(e2)
                        return P

                    def emitB(g, d, r, b, P):
                        nb = S // (128 * d)
                        t0 = r + d * 128 * b
                        nsl = 2 if b > 0 else 1
                        pO = poolO.next(); kb.wr('pe', pO); kb.rd('pe', P); kb.rd('pe', Vg[g])
                        for hh in range(2):
                            h = 2 * hp + hh
                            for s_i in range(nsl):
                                vb = r * nb + (b - s_i)
                                e4 = pe.matmul(pO.ap[:64, (hh * 2) * 128:(hh * 2 + 1) * 128], lhsT=Vg[g].ap[:, vb, h * 64:(h + 1) * 64],
                                               rhs=P.ap[:, (hh * 2 + s_i) * 128:(hh * 2 + s_i + 1) * 128], start=(s_i == 0), stop=(s_i == nsl - 1))
                            for s_i in range(nsl):
                                e4 = pe.matmul(pO.ap[:64, (hh * 2 + 1) * 128:(hh * 2 + 2) * 128], lhsT=ones_bf.ap[:, 0:64],
                                               rhs=P.ap[:, (hh * 2 + s_i) * 128:(hh * 2 + s_i + 1) * 128], start=(s_i == 0), stop=(s_i == nsl - 1))
                        e4 = kb.ev('pe', e4); pO.ready = e4; P.frees.append(e4)
                        kb.rd('dve', pO)
                        l0 = t0 - half * HALF
                        av = acc.ap[:].rearrange("p a b t -> p (a b) t")[:, :, l0:l0 + 127 * d + 1:d]
                        e5 = kb.ev('dve', dve.tensor_tensor(out=av, in0=pO.ap[:64, :].rearrange("p (a q) -> p a q", a=4), in1=av, op=ALU.add))
                        pO.frees.append(e5); acc.ready = e5
                        return e4

                    Pq = {0: emitA(*aunits[0])}
                    for ui, un in enumerate(aunits):
                        if ui + 1 < len(aunits):
                            Pq[ui + 1] = emitA(*aunits[ui + 1])
                        e4 = emitB(*un, Pq.pop(ui))
                    kb.wr('dve', att)
                    kb.rd('act', acc)
                    act.activation(out=acc.ap[:, :, 1, :], in_=acc.ap[:, :, 1, :], func=AF.Ln)
                    eR = kb.ev('act', act.activation(out=acc.ap[:, :, 1, :], in_=acc.ap[:, :, 1, :], func=AF.Exp, scale=-1.0))
                    kb.wait('dve', eR)
                    e6 = kb.ev('dve', dve.tensor_tensor(out=att.ap[:], in0=acc.ap[:, :, 0, :], in1=acc.ap[:, :, 1, :], op=ALU.mult))
                    att.ready = e6; acc.frees.append(e6); acc.ready = e6
                    kb.dma('sp', MIX[hp * 128:(hp + 1) * 128, half * HALF:(half + 1) * HALF].rearrange("(h p) t -> p h t", p=64), att.ap[:], att, reads=(att,), writes=False)
                    mix_evs.append(kb.last(att))
                Q.frees.append(e4); K.frees.append(e4)
            kb.fence()
            kb.end_phase()
        es_cur[0] = es

    mix_evs = []

    def ssm(l):
        with ExitStack() as ess:
            es_cur[0] = ess
            kb.begin_phase()
            L = 512

            def t16(name):
                return Res(sb(name, [128, 16]))
            are = t16("are"); aim = t16("aim"); ldt = t16("ldt"); dt = t16("dt"); rr = t16("rr"); phi = t16("phi")
            t1 = t16("t1"); t2 = t16("t2"); cph = t16("cph"); sph = t16("sph"); abr = t16("abr"); abi = t16("abi"); nabi = t16("nabi")
            fre = t16("fre"); fim = t16("fim"); nfim = t16("nfim")
            dsk = Res(sb("dsk", [128, 4])); bgl = Res(sb("bgl", [128, 4]))
            ld = Res(None)
            for (dst, src) in ((are, s_are[l]), (aim, s_aim[l]), (ldt, s_ldt[l]), (dsk, s_d[l]), (bgl, s_bglu[l])):
                kb.dma('sp', dst.ap[:], src, ld, writes=False)
            Bt = [Res(sb("Bt%d" % i, [128, 16, 128], BF16)) for i in range(2)]
            Ctb = [Res(sb("Ctb%d" % i, [128, 16, 128], BF16)) for i in range(2)]
            cosT = Res(sb("cosT", [128, 16, L])); sinT = Res(sb("sinT", [128, 16, L]))
            wgl = Res(sb("wgl", [128, 2048], BF16))
            kb.wait('sp', cast_ev[("glu", l)])
            kb.dma('sp', wgl.ap[:], WGLU[l], ld, writes=False)
            h0r = Res(sb("h0r", [128, 16, 4])); h0i = Res(sb("h0i", [128, 16, 4]))
            kb.dma('sp', h0r.ap[:], st_re[l], ld, writes=False); kb.dma('sp', h0i.ap[:], st_im[l], ld, writes=False)
            esp = ExitStack(); es_cur[0] = esp
            Bn = [Res(sb("Bn%d" % i, [128, 16, 128])) for i in range(2)]
            kb.dma('sp', Bn[0].ap[:], Bn_re[l], ld, writes=False); kb.dma('sp', Bn[1].ap[:], Bn_im[l], ld, writes=False)
            Cn = [Res(sb("Cn%d" % i, [128, 16, 128])) for i in range(2)]
            kb.dma('sp', Cn[0].ap[:], Ct_re[l], ld, writes=False); kb.dma('sp', Cn[1].ap[:], Ct_im[l], ld, writes=False)
            iot = Res(sb("iot", [128, 512]))
            kb.dma('sp', iot.ap[:], c_iota, ld, writes=False)
            Bb = [Res(sb("Bb%d" % i, [128, 16, 128])) for i in range(2)]
            tmpB = Res(sb("tmpB", [128, 16, 128]))
            ang = Res(sb("ang", [128, L])); tA = Res(sb("tA", [128, L])); tB = Res(sb("tB", [128, L]))
            es_cur[0] = ess
            ld_ev = kb.last(ld)
            for e in ('dve', 'act', 'pe'):
                kb.wait(e, ld_ev)

            def A(ins):
                e1 = kb.ev('act', ins); kb.wait('dve', e1)

            def V(ins):
                e1 = kb.ev('dve', ins); return e1

            def sin_of(dst, src, shift, tmp1, tmp2):
                if shift != 0.0:
                    dve.tensor_scalar(out=tmp2, in0=src, scalar1=shift, scalar2=None, op0=ALU.add)
                    sx = tmp2
                else:
                    sx = src
                dve.tensor_scalar(out=tmp1, in0=sx, scalar1=1.0 / (2 * math.pi), scalar2=MAGIC, op0=ALU.mult, op1=ALU.add)
                dve.tensor_scalar(out=tmp1, in0=tmp1, scalar1=MAGIC, scalar2=None, op0=ALU.subtract)
                dve.scalar_tensor_tensor(out=tmp2, in0=tmp1, scalar=-2 * math.pi, in1=sx, op0=ALU.mult, op1=ALU.add)
                e1 = V(dve.tensor_scalar(out=tmp2, in0=tmp2, scalar1=-3.141592, scalar2=3.141592, op0=ALU.max, op1=ALU.min))
                kb.wait('act', e1)
                A(act.activation(out=dst, in_=tmp2, func=AF.Sin))

            kb.wait('act', ld_ev)
            A(act.activation(out=dt.ap[:], in_=ldt.ap[:], func=AF.Exp))
            e1 = V(dve.tensor_tensor(out=t1.ap[:], in0=are.ap[:], in1=dt.ap[:], op=ALU.mult))
            kb.wait('act', e1)
            A(act.activation(out=rr.ap[:], in_=t1.ap[:], func=AF.Exp))
            dve.tensor_tensor(out=phi.ap[:], in0=aim.ap[:], in1=dt.ap[:], op=ALU.mult)
            dve.tensor_scalar(out=t1.ap[:], in0=phi.ap[:], scalar1=1.0 / (2 * math.pi), scalar2=MAGIC, op0=ALU.mult, op1=ALU.add)
            dve.tensor_scalar(out=t1.ap[:], in0=t1.ap[:], scalar1=MAGIC, scalar2=None, op0=ALU.subtract)
            dve.scalar_tensor_tensor(out=phi.ap[:], in0=t1.ap[:], scalar=-2 * math.pi, in1=phi.ap[:], op0=ALU.mult, op1=ALU.add)
            sin_of(sph.ap[:], phi.ap[:], 0.0, t1.ap[:], t2.ap[:])
            sin_of(cph.ap[:], phi.ap[:], math.pi / 2, t1.ap[:], t2.ap[:])
            dve.tensor_tensor(out=abr.ap[:], in0=rr.ap[:], in1=cph.ap[:], op=ALU.mult)
            dve.tensor_tensor(out=abi.ap[:], in0=rr.ap[:], in1=sph.ap[:], op=ALU.mult)
            dve.tensor_scalar(out=nabi.ap[:], in0=abi.ap[:], scalar1=-1.0, scalar2=None, op0=ALU.mult)
            dve.tensor_tensor(out=t1.ap[:], in0=are.ap[:], in1=are.ap[:], op=ALU.mult)
            dve.tensor_tensor(out=t2.ap[:], in0=aim.ap[:], in1=aim.ap[:], op=ALU.mult)
            dve.tensor_tensor(out=t1.ap[:], in0=t1.ap[:], in1=t2.ap[:], op=ALU.add)
            dve.reciprocal(out=t1.ap[:], in_=t1.ap[:])
            dve.tensor_scalar(out=t2.ap[:], in0=abr.ap[:], scalar1=-1.0, scalar2=None, op0=ALU.add)
            dve.tensor_tensor(out=fre.ap[:], in0=t2.ap[:], in1=are.ap[:], op=ALU.mult)
            dve.tensor_tensor(out=fim.ap[:], in0=abi.ap[:], in1=aim.ap[:], op=ALU.mult)
            dve.tensor_tensor(out=fre.ap[:], in0=fre.ap[:], in1=fim.ap[:], op=ALU.add)
            dve.tensor_tensor(out=fre.ap[:], in0=fre.ap[:], in1=t1.ap[:], op=ALU.mult)
            dve.tensor_tensor(out=fim.ap[:], in0=abi.ap[:], in1=are.ap[:], op=ALU.mult)
            dve.tensor_tensor(out=t2.ap[:], in0=t2.ap[:], in1=aim.ap[:], op=ALU.mult)
            dve.tensor_tensor(out=fim.ap[:], in0=fim.ap[:], in1=t2.ap[:], op=ALU.subtract)
            dve.tensor_tensor(out=fim.ap[:], in0=fim.ap[:], in1=t1.ap[:], op=ALU.mult)
            if l == 0:
                dbg("rr", rr.ap[:]); dbg("abr", abr.ap[:]); dbg("abi", abi.ap[:]); dbg("fre", fre.ap[:]); dbg("fim", fim.ap[:]); dbg("phi", phi.ap[:]); dbg("dt", dt.ap[:])
            frb = fre.ap[:].unsqueeze(2).to_broadcast([128, 16, 128]); fib = fim.ap[:].unsqueeze(2).to_broadcast([128, 16, 128])
            dve.tensor_tensor(out=Bb[0].ap[:], in0=Bn[0].ap[:], in1=frb, op=ALU.mult)
            dve.tensor_tensor(out=tmpB.ap[:], in0=Bn[1].ap[:], in1=fib, op=ALU.mult)
            dve.tensor_tensor(out=Bb[0].ap[:], in0=Bb[0].ap[:], in1=tmpB.ap[:], op=ALU.subtract)
            dve.tensor_tensor(out=Bb[1].ap[:], in0=Bn[1].ap[:], in1=frb, op=ALU.mult)
            dve.tensor_tensor(out=tmpB.ap[:], in0=Bn[0].ap[:], in1=fib, op=ALU.mult)
            eB = V(dve.tensor_tensor(out=Bb[1].ap[:], in0=Bb[1].ap[:], in1=tmpB.ap[:], op=ALU.add))
            kb.wait('pe', eB); kb.rd('pe', ident)
            for ri in range(2):
                for s4 in range(4):
                    ps = poolC.next(); kb.wr('pe', ps)
                    for j in range(4):
                        e1 = pe.transpose(ps.ap[:, j * 128:(j + 1) * 128], Bb[ri].ap[:, s4 * 4 + j, :], ident.ap[:])
                    ps.ready = kb.ev('pe', e1)
                    kb.rd('act', ps)
                    e2 = kb.ev('act', act.copy(out=Bt[ri].ap[:, s4 * 4:(s4 + 1) * 4, :].rearrange("p a b -> p (a b)"), in_=ps.ap[:]))
                    ps.frees.append(e2); Bt[ri].ready = e2
            act.copy(out=Ctb[0].ap[:], in_=Cn[0].ap[:])
            eC = kb.ev('act', act.mul(out=Ctb[1].ap[:], in_=Cn[1].ap[:], mul=-1.0))
            Ctb[0].ready = eC; Ctb[1].ready = eC
            for st in range(16):
                e1 = V(dve.tensor_scalar(out=ang.ap[:], in0=iot.ap[:], scalar1=phi.ap[:, st:st + 1], scalar2=None, op0=ALU.mult))
                sin_of(sinT.ap[:, st, :], ang.ap[:], 0.0, tA.ap[:], tB.ap[:])
                sin_of(cosT.ap[:, st, :], ang.ap[:], math.pi / 2, tA.ap[:], tB.ap[:])

            if l == 0:
                dbg("cos0", cosT.ap[:, 5, :]); dbg("sin0", sinT.ap[:, 5, :]); dbg("Bt0", Bt[0].ap[:, 5, :], 'act'); dbg("Bb0", Bb[0].ap[:, 5, :])
            kb.fence()
            esp.close()
            ub = Ring([Res(sb("ub%d" % i, [128, 4, L], BF16)) for i in range(2)])
            u32 = Ring([Res(sb("u32_%d" % i, [128, 4, L])) for i in range(2)])
            W = {nm: Ring([Res(sb("%s%d" % (nm, i), [128, L])) for i in range(2 if nm in ("hre", "him") else 1)]) for nm in ("a1", "a2", "ure", "uim", "wre", "wim", "hre", "him")}
            hb = Ring([Res(sb("hb%d" % i, [128, 2, L], BF16)) for i in range(3)])
            Hre = Res(sb("Hre", [128, 16])); Him = Res(sb("Him", [128, 16]))
            dve.memset(Hre.ap[:], 0.0); dve.memset(Him.ap[:], 0.0)
            z32 = Res(sb("z32", [128, 4, L])); zbf = Res(sb("zbf", [128, 4, L], BF16))
            yt = Ring([Res(sb("yt%d" % i, [128, L])) for i in range(2)])
            ob = Ring([Res(sb("ob%d" % i, [128, 4, L], BF16)) for i in range(2)])

            def epilogue(u3, n, c0):
                o = ob.next(); kb.wr('dve', o)
                for m in range(4):
                    ps = poolB.next(); kb.wr('pe', ps); kb.rd('pe', zbf)
                    for k in range(4):
                        e1 = pe.matmul(ps.ap[:, :n], lhsT=wgl.ap[:, k * 512 + m * 128:k * 512 + m * 128 + 128], rhs=zbf.ap[:, k, :n], start=(k == 0), stop=(k == 3))
                    e1 = kb.ev('pe', e1); ps.ready = e1
                    g = yt.next(); kb.rd('act', ps); kb.wr('act', g)
                    e2 = kb.ev('act', act.activation(out=g.ap[:, :n], in_=ps.ap[:, :n], func=AF.Sigmoid, bias=bgl.ap[:, m:m + 1]))
                    ps.frees.append(e2); g.ready = e2
                    kb.rd('dve', g)
                    e3 = kb.ev('dve', dve.tensor_tensor(out=o.ap[:, m, :n], in0=z32.ap[:, m, :n], in1=g.ap[:, :n], op=ALU.mult))
                    g.frees.append(e3); o.ready = e3
                zbf.frees.append(e1); z32.frees.append(e3)
                kb.dma('sp', MIX[512:1024, c0:c0 + n].rearrange("(k p) t -> p k t", p=128), o.ap[:, :, :n], o, reads=(o,), writes=False)
                mix_evs.append(kb.last(o))

            def gelu_chunk(psY, u3, fc, n):
                y = yt.next(); s2 = yt.next()
                kb.rd('dve', psY); kb.wr('dve', y); kb.wr('dve', s2)
                if fc == 0:
                    kb.wr('dve', z32); kb.wr('dve', zbf)
                dve.scalar_tensor_tensor(out=y.ap[:, :n], in0=u3.ap[:, fc, :n], scalar=dsk.ap[:, fc:fc + 1], in1=psY.ap[:, :n], op0=ALU.mult, op1=ALU.add)
                dve.tensor_tensor(out=s2.ap[:, :n], in0=y.ap[:, :n], in1=y.ap[:, :n], op=ALU.mult)
                dve.tensor_scalar(out=s2.ap[:, :n], in0=s2.ap[:, :n], scalar1=0.044715, scalar2=1.0, op0=ALU.mult, op1=ALU.add)
                e1 = kb.ev('dve', dve.tensor_tensor(out=s2.ap[:, :n], in0=s2.ap[:, :n], in1=y.ap[:, :n], op=ALU.mult))
                psY.frees.append(e1)
                kb.wait('act', e1)
                e2 = kb.ev('act', act.activation(out=s2.ap[:, :n], in_=s2.ap[:, :n], func=AF.Sigmoid, scale=2.0 * math.sqrt(2.0 / math.pi)))
                kb.wait('dve', e2)
                dve.tensor_tensor(out=z32.ap[:, fc, :n], in0=y.ap[:, :n], in1=s2.ap[:, :n], op=ALU.mult)
                e3 = kb.ev('dve', dve.tensor_copy(out=zbf.ap[:, fc, :n], in_=z32.ap[:, fc, :n]))
                z32.ready = e3; zbf.ready = e3; y.ready = e3; s2.ready = e3

            tile_u = {}

            def prep_tile(tt):
                u3 = u32.next(); u = ub.next()
                kb.dma('sp', u3.ap[:], U32[:, :, tt * L:(tt + 1) * L], u3)
                kb.rd('act', u3); kb.wr('act', u)
                u.ready = kb.ev('act', act.copy(out=u.ap[:], in_=u3.ap[:]))
                tile_u[tt] = (u3, u)

            def emitB(tt, fc, si):
                st = fc * 4 + si
                u3, u = tile_u[tt]
                pr = poolA.next(); pi_ = poolA.next()
                kb.wr('pe', pr); kb.wr('pe', pi_); kb.rd('pe', u); kb.rd('pe', Bt[0]); kb.rd('pe', Bt[1])
                pr.ready = kb.ev('pe', pe.matmul(pr.ap[:], lhsT=Bt[0].ap[:, st, :], rhs=u.ap[:, fc, :], start=True, stop=True))
                pi_.ready = kb.ev('pe', pe.matmul(pi_.ap[:], lhsT=Bt[1].ap[:, st, :], rhs=u.ap[:, fc, :], start=True, stop=True))
                return pr, pi_

            units = [(tt, fc, si) for tt in range(NT) for fc in range(4) for si in range(4)]
            prep_tile(0)
            Bq = {0: emitB(*units[0])}
            psY = None
            for idx, (tt, fc, si) in enumerate(units):
                st = fc * 4 + si
                c0 = tt * L
                u3, u = tile_u[tt]
                if idx + 1 < len(units):
                    ntt = units[idx + 1][0]
                    if ntt not in tile_u:
                        prep_tile(ntt)
                    Bq[idx + 1] = emitB(*units[idx + 1])
                pr, pi_ = Bq.pop(idx)
                if si == 0:
                    psY = poolB.next(); kb.wr('pe', psY)
                a1 = W["a1"].next(); a2 = W["a2"].next(); ure = W["ure"].next(); uim = W["uim"].next()
                wre = W["wre"].next(); wim = W["wim"].next(); hre = W["hre"].next(); him = W["him"].next()
                c_ = cosT.ap[:, st, :]; s_ = sinT.ap[:, st, :]
                kb.rd('dve', pr); kb.rd('dve', pi_)
                for r_ in (a1, a2, ure, uim, wre, wim, hre, him):
                    kb.wr('dve', r_)
                dve.tensor_tensor(out=a1.ap[:], in0=pr.ap[:], in1=c_, op=ALU.mult)
                dve.tensor_tensor(out=a2.ap[:], in0=pi_.ap[:], in1=s_, op=ALU.mult)
                dve.tensor_tensor(out=ure.ap[:], in0=a1.ap[:], in1=a2.ap[:], op=ALU.add)
                dve.tensor_tensor(out=a1.ap[:], in0=pi_.ap[:], in1=c_, op=ALU.mult)
                dve.tensor_tensor(out=a2.ap[:], in0=pr.ap[:], in1=s_, op=ALU.mult)
                e2 = kb.ev('dve', dve.tensor_tensor(out=uim.ap[:], in0=a1.ap[:], in1=a2.ap[:], op=ALU.subtract))
                pr.frees.append(e2); pi_.frees.append(e2)
                rb_ = rr.ap[:, st:st + 1].to_broadcast([128, L])
                dve.tensor_tensor_scan(out=wre.ap[:], data0=rb_, data1=ure.ap[:], initial=Hre.ap[:, st:st + 1], op0=ALU.mult, op1=ALU.add)
                dve.tensor_tensor_scan(out=wim.ap[:], data0=rb_, data1=uim.ap[:], initial=Him.ap[:, st:st + 1], op0=ALU.mult, op1=ALU.add)
                dve.tensor_tensor(out=a1.ap[:], in0=wre.ap[:], in1=c_, op=ALU.mult)
                dve.tensor_tensor(out=a2.ap[:], in0=wim.ap[:], in1=s_, op=ALU.mult)
                dve.tensor_tensor(out=hre.ap[:], in0=a1.ap[:], in1=a2.ap[:], op=ALU.subtract)
                dve.tensor_tensor(out=a1.ap[:], in0=wre.ap[:], in1=s_, op=ALU.mult)
                dve.tensor_tensor(out=a2.ap[:], in0=wim.ap[:], in1=c_, op=ALU.mult)
                dve.tensor_tensor(out=him.ap[:], in0=a1.ap[:], in1=a2.ap[:], op=ALU.add)
                dve.tensor_copy(out=Hre.ap[:, st:st + 1], in_=hre.ap[:, L - 1:L])
                e3 = kb.ev('dve', dve.tensor_copy(out=Him.ap[:, st:st + 1], in_=him.ap[:, L - 1:L]))
                hre.ready = e3; him.ready = e3
                hbb = hb.next(); kb.rd('act', hre); kb.wr('act', hbb)
                act.copy(out=hbb.ap[:, 0, :], in_=hre.ap[:])
                e4 = kb.ev('act', act.copy(out=hbb.ap[:, 1, :], in_=him.ap[:]))
                hbb.ready = e4; hre.frees.append(e4); him.frees.append(e4)
                kb.rd('pe', hbb); kb.rd('pe', Ctb[0])
                pe.matmul(psY.ap[:], lhsT=Ctb[0].ap[:, st, :], rhs=hbb.ap[:, 0, :], start=(si == 0), stop=False)
                e5 = kb.ev('pe', pe.matmul(psY.ap[:], lhsT=Ctb[1].ap[:, st, :], rhs=hbb.ap[:, 1, :], start=False, stop=(si == 3)))
                hbb.frees.append(e5)
                if si == 3:
                    psY.ready = e5
                    gelu_chunk(psY, u3, fc, L)
                    if fc == 3:
                        u.frees.append(e5)
                        epilogue(u3, L, c0)
                        u3.frees.append(('dve', kb.sem['dve'], kb.cnt['dve']))
            eH = kb.ev('dve', dve.tensor_copy(out=t1.ap[:], in_=Hre.ap[:]))
            kb.wait('sp', eH)
            r1 = Res(None)
            kb.out_evs.append(kb.dma('sp', ssm_p_re[l], Hre.ap[:], r1, writes=False))
            kb.out_evs.append(kb.dma('sp', ssm_p_im[l], Him.ap[:], r1, writes=False))

            u3 = u32.next(); u = ub.next()
            kb.dma('sp', u3.ap[:, :, 0:16], U32[:, :, S:S + 16], u3)
            kb.rd('act', u3); kb.wr('act', u)
            e1 = kb.ev('act', act.copy(out=u.ap[:, :, 0:16], in_=u3.ap[:, :, 0:16])); u.ready = e1
            hs_re = Res(sb("hs_re", [128, 16, 16])); hs_im = Res(sb("hs_im", [128, 16, 16]))
            hsb = Res(sb("hsb", [128, 2, 16, 16], BF16))
            tq1 = Res(sb("tq1", [128, 16, 4])); tq2 = Res(sb("tq2", [128, 16, 4]))
            pbr = poolA.next(); pbi = poolA.next()
            kb.wr('pe', pbr); kb.wr('pe', pbi); kb.rd('pe', u)
            for st in range(16):
                pe.matmul(pbr.ap[:, st * 16:(st + 1) * 16], lhsT=Bt[0].ap[:, st, :], rhs=u.ap[:, st // 4, 0:16], start=True, stop=True)
                e1 = pe.matmul(pbi.ap[:, st * 16:(st + 1) * 16], lhsT=Bt[1].ap[:, st, :], rhs=u.ap[:, st // 4, 0:16], start=True, stop=True)
            e1 = kb.ev('pe', e1); pbr.ready = e1; pbi.ready = e1
            bre = pbr.ap[:, 0:256].rearrange("p (s b t) -> p s b t", s=16, t=4)
            bim = pbi.ap[:, 0:256].rearrange("p (s b t) -> p s b t", s=16, t=4)
            hr = hs_re.ap[:].rearrange("p s (b t) -> p s b t", t=4); hi = hs_im.ap[:].rearrange("p s (b t) -> p s b t", t=4)
            abr_b = abr.ap[:].unsqueeze(2).to_broadcast([128, 16, 4]); abi_b = abi.ap[:].unsqueeze(2).to_broadcast([128, 16, 4])
            kb.rd('dve', pbr); kb.rd('dve', pbi)
            for tau in range(4):
                pre = h0r.ap[:] if tau == 0 else hr[:, :, :, tau - 1]
                pim = h0i.ap[:] if tau == 0 else hi[:, :, :, tau - 1]
                dve.tensor_tensor(out=tq1.ap[:], in0=pre, in1=abr_b, op=ALU.mult)
                dve.tensor_tensor(out=tq2.ap[:], in0=pim, in1=abi_b, op=ALU.mult)
                dve.tensor_tensor(out=tq1.ap[:], in0=tq1.ap[:], in1=tq2.ap[:], op=ALU.subtract)
                dve.tensor_tensor(out=hr[:, :, :, tau], in0=tq1.ap[:], in1=bre[:, :, :, tau], op=ALU.add)
                dve.tensor_tensor(out=tq1.ap[:], in0=pim, in1=abr_b, op=ALU.mult)
                dve.tensor_tensor(out=tq2.ap[:], in0=pre, in1=abi_b, op=ALU.mult)
                dve.tensor_tensor(out=tq1.ap[:], in0=tq1.ap[:], in1=tq2.ap[:], op=ALU.add)
                dve.tensor_tensor(out=hi[:, :, :, tau], in0=tq1.ap[:], in1=bim[:, :, :, tau], op=ALU.add)
            dve.tensor_copy(out=hsb.ap[:, 0], in_=hs_re.ap[:])
            e2 = kb.ev('dve', dve.tensor_copy(out=hsb.ap[:, 1], in_=hs_im.ap[:]))
            pbr.frees.append(e2); pbi.frees.append(e2)
            kb.wait('pe', e2)
            for fc in range(4):
                psY = poolB.next(); kb.wr('pe', psY)
                for si in range(4):
                    st = fc * 4 + si
                    pe.matmul(psY.ap[:, 0:16], lhsT=Ctb[0].ap[:, st, :], rhs=hsb.ap[:, 0, st, :], start=(si == 0), stop=False)
                    e5 = pe.matmul(psY.ap[:, 0:16], lhsT=Ctb[1].ap[:, st, :], rhs=hsb.ap[:, 1, st, :], start=False, stop=(si == 3))
                e5 = kb.ev('pe', e5)
                psY.ready = e5
                gelu_chunk(psY, u3, fc, 16)
            epilogue(u3, 16, S)
            fs_re = Res(sb("fs_re", [128, 16, 4])); fs_im = Res(sb("fs_im", [128, 16, 4]))
            dve.tensor_copy(out=fs_re.ap[:], in_=hs_re.ap[:].rearrange("p s (b t) -> p s b t", t=4)[:, :, :, 3])
            eF = kb.ev('dve', dve.tensor_copy(out=fs_im.ap[:], in_=hs_im.ap[:].rearrange("p s (b t) -> p s b t", t=4)[:, :, :, 3]))
            kb.wait('sp', eF)
            kb.out_evs.append(kb.dma('sp', ssm_s_re[l], fs_re.ap[:], r1, writes=False))
            kb.out_evs.append(kb.dma('sp', ssm_s_im[l], fs_im.ap[:], r1, writes=False))
            kb.wait('sp', kb.last(r1))
            kb.fence()
            kb.end_phase()
        es_cur[0] = es

    def sample_attention(l):
        with ExitStack() as esx:
            es_cur[0] = esx
            kb.begin_phase()
            kb.wait('sp', qkvs_ev[l])
            ld = Res(None)
            knew = Res(sb("knew", [16, 512])); vnew = Res(sb("vnew", [16, 512]))
            kb.dma('sp', knew.ap[:], QKVS[:, 512:1024], ld, writes=False)
            kb.dma('sp', vnew.ap[:], QKVS[:, 1024:1536], ld, writes=False)
            ones32 = Res(sb("ones32", [128, 1])); dve.memset(ones32.ap[:], 1.0)
            hmask = Res(sb("hmask", [8, 8, 64]))
            kb.dma('sp', hmask.ap[:], c_hmask, ld, writes=False)
            attT = Res(sb("attT", [128, 4, 16], BF16))
            psT = poolC.next(); kb.wr('pe', psT)
            Kt = Ring([Res(sb("Kt%d" % i, [128, 9, 512])) for i in range(2)])
            Vt = Ring([Res(sb("Vt%d" % i, [128, 9, 512])) for i in range(2)])
            qbs = Ring([Res(sb("qb%d" % i, [128, 512])) for i in range(2)])
            prod = Res(sb("prod", [128, 512])); Sx = Res(sb("Sx", [128, 4, 8])); Px = Ring([Res(sb("Px%d" % i, [128, 4, 8])) for i in range(2)])
            Z = Res(sb("Z", [8, 512])); rden = Res(sb("rden", [8, 1]))
            kb.wait('dve', kb.last(ld)); kb.wait('pe', kb.last(ld))
            lastpe = None
            for b in range(4):
                K = Kt.next(); V = Vt.next()
                for (Tl, cache) in ((K, cache_k), (V, cache_v)):
                    kb.wr('sp', Tl)
                    if Tl.sem is None:
                        Tl.sem = kb.newsem()
                    def ld1(dst, src):
                        ins = sp.dma_start(out=dst, in_=src); Tl.sem.cnt += 16; ins.then_inc(Tl.sem.sem, 16)
                    ld1(Tl.ap[:, 0, :], cache[l, b, 1920:2048, :])
                    for i in range(4):
                        src = bass.AP(tensor=cache.tensor, offset=cache[l, b, 1536 + i, :].offset, ap=[[4 * 512, 128], [1, 512]])
                        ld1(Tl.ap[:, 1 + i, :], src)
                        src = bass.AP(tensor=cache.tensor, offset=cache[l, b, i, :].offset, ap=[[16 * 512, 128], [1, 512]])
                        ld1(Tl.ap[:, 5 + i, :], src)
                    Tl.ready = kb.last(Tl)
                for i in range(4):
                    t = b * 4 + i
                    qb = qbs.next()
                    kb.dma('sp', qb.ap[:], bass.AP(tensor=QKVS.tensor, offset=QKVS[t, 0:512].offset, ap=[[0, 128], [1, 512]]), qb)
                    kb.rd('dve', qb); kb.rd('dve', K); kb.wr('dve', Sx)
                    tiles = ((0, 128), (1 + i, 128), (5 + i, 128))
                    for j, (ti, np_) in enumerate(tiles):
                        dve.tensor_tensor(out=prod.ap[:], in0=K.ap[:, ti, :], in1=qb.ap[:], op=ALU.mult)
                        dve.tensor_reduce(out=Sx.ap[:, j, :], in_=prod.ap[:].rearrange("p (h d) -> p h d", d=64), axis=AX.X, op=ALU.add)
                    dve.tensor_tensor(out=prod.ap[0:16, :], in0=knew.ap[:], in1=qb.ap[0:16, :], op=ALU.mult)
                    dve.memset(Sx.ap[:, 3, :], -30000.0)
                    e1 = kb.ev('dve', dve.tensor_reduce(out=Sx.ap[0:16, 3, :], in_=prod.ap[0:16, :].rearrange("p (h d) -> p h d", d=64), axis=AX.X, op=ALU.add))
                    qb.frees.append(e1)
                    P = Px.next(); kb.wait('act', e1); kb.wr('act', P)
                    e2 = kb.ev('act', act.activation(out=P.ap[:], in_=Sx.ap[:], func=AF.Exp))
                    Sx.frees.append(e2)
                    kb.wait('dve', e2)
                    dve.tensor_tensor(out=P.ap[:, 0, :], in0=P.ap[:, 0, :], in1=EBs0.ap[:, :, i], op=ALU.mult)
                    dve.tensor_tensor(out=P.ap[:, 1:3, :], in0=P.ap[:, 1:3, :], in1=EBs12.ap[:], op=ALU.mult)
                    e3 = kb.ev('dve', dve.tensor_tensor(out=P.ap[0:16, 3, :], in0=P.ap[0:16, 3, :], in1=EBnew.ap[:, t, :], op=ALU.mult))
                    P.ready = e3
                    pO = poolB.next(); pD = poolA.next()
                    kb.wr('pe', pO); kb.wr('pe', pD); kb.rd('pe', P); kb.rd('pe', V)
                    for j, (ti, np_) in enumerate(tiles):
                        pe.matmul(pO.ap[0:8, :], lhsT=P.ap[:, j, :], rhs=V.ap[:, ti, :], start=(j == 0), stop=False)
                    e4 = kb.ev('pe', pe.matmul(pO.ap[0:8, :], lhsT=P.ap[0:16, 3, :], rhs=vnew.ap[:], start=False, stop=True))
                    pO.ready = e4
                    for j in range(3):
                        pe.matmul(pD.ap[0:8, 0:1], lhsT=P.ap[:, j, :], rhs=ones32.ap[:], start=(j == 0), stop=False)
                    e4 = kb.ev('pe', pe.matmul(pD.ap[0:8, 0:1], lhsT=P.ap[0:16, 3, :], rhs=ones32.ap[0:16, :], start=False, stop=True))
                    pD.ready = e4; P.frees.append(e4)
                    kb.rd('dve', pD); kb.rd('dve', pO); kb.wr('dve', Z)
                    dve.reciprocal(out=rden.ap[:], in_=pD.ap[0:8, 0:1])
                    e5 = kb.ev('dve', dve.scalar_tensor_tensor(out=Z.ap[:], in0=pO.ap[0:8, :], scalar=rden.ap[:, 0:1], in1=hmask.ap[:].rearrange("h a d -> h (a d)"), op0=ALU.mult, op1=ALU.mult))
                    pO.frees.append(e5); pD.frees.append(e5); Z.ready = e5
                    kb.rd('pe', Z)
                    for c in range(4):
                        lastpe = pe.matmul(psT.ap[:, c * 16 + t:c * 16 + t + 1], lhsT=Z.ap[:, c * 128:(c + 1) * 128], rhs=ones32.ap[0:8, :], start=True, stop=True)
                    e6 = kb.ev('pe', lastpe); Z.frees.append(e6)
                K.frees.append(('dve', kb.sem['dve'], kb.cnt['dve'])); V.frees.append(e6)
            psT.ready = e6
            kb.rd('dve', psT)
            e7 = kb.ev('dve', dve.tensor_copy(out=attT.ap[:].rearrange("p c t -> p (c t)"), in_=psT.ap[:, 0:64]))
            psT.frees.append(e7); attT.ready = e7
            kb.dma('sp', MIX[0:512, S:S + 16].rearrange("(k p) t -> p k t", p=128), attT.ap[:], attT, reads=(attT,), writes=False)
            mix_evs.append(kb.last(attT))
            kb.wait('sp', kb.last(attT))
            kb.fence()
            kb.end_phase()
        es_cur[0] = es

    c_hmask = din("c_hmask", [8, 8, 64])

    kstop = int(os.environ.get("KSTOP", "99"))
    stage = [0]

    def go(fn, *a):
        stage[0] += 1
        if stage[0] <= kstop:
            fn(*a)
    go(row_phase, 0)
    for l in range(DEPTH):
        go(attention, l)
        go(ssm, l)
        go(sample_attention, l)
        del mix_evs[:]; del xs_evs[:]
        go(row_phase, l + 1)
    for e1 in kb.out_evs:
        kb.wait('sp', e1)
    es.close()
    nc._dbg_names = dbg_names
    return nc


def _fm(v):
    return np.ascontiguousarray(v.reshape(-1, 128).T)


def _consts():
    iota = np.tile(np.arange(1, 513, dtype=np.float32)[None, :], (128, 1))
    oh = np.zeros((33, 3 * EW), np.float32)
    for g, d in enumerate(BRANCH_D):
        steps = np.arange(0, 129)
        bk = t5_bucket(steps * d)
        oh[32, g * EW:(g + 1) * EW] = 1.0
        for st_, b_ in zip(steps, bk):
            oh[b_, g * EW + st_ + 127] = 1.0
            oh[32, g * EW + st_ + 127] = 0.0
    ident = np.eye(128, dtype=np.float32)
    ohnew = np.zeros((4, 16, 16), np.float32)
    for q in range(16):
        for k in range(16):
            if q // 4 == k // 4 and k % 4 <= q % 4:
                m = q % 4 - k % 4
                ohnew[m, q, k] = 3.0 if m == 0 else 1.0
    hmask = np.zeros((8, 8, 64), np.float32)
    for h in range(8):
        hmask[h, h, :] = 1.0
    return dict(c_iota=iota, c_oh=oh, c_ident=ident, c_ohnew=ohnew, c_hmask=hmask)


def _core_inputs(c, S, inp, shared):
    T = S + 16
    f32 = np.float32
    xs = inp['x_sample'][4 * c:4 * c + 4].reshape(16, D)
    xall = np.concatenate([inp['x_prompt'][c], xs], axis=0)
    xT = np.ascontiguousarray(xall.reshape(T, 8, 128).transpose(2, 1, 0))
    pall = np.concatenate([inp['p_prompt'][:, c], inp['p_sample'][:, 4 * c:4 * c + 4].reshape(DEPTH, 16, 256)], axis=1)
    pT = np.ascontiguousarray(pall.reshape(DEPTH, T, 2, 128).transpose(0, 3, 2, 1))
    ck = np.ascontiguousarray(inp['cache_k'][:, 4 * c:4 * c + 4].reshape(DEPTH, 4, 2048, 512))
    cv = np.ascontiguousarray(inp['cache_v'][:, 4 * c:4 * c + 4].reshape(DEPTH, 4, 2048, 512))

    def st_lay(a):
        return np.ascontiguousarray(a.reshape(DEPTH, 4, 16, 2, 64).transpose(0, 3, 4, 2, 1).reshape(DEPTH, 128, 16, 4))
    d = dict(shared)
    d.update(xT=xT, pT=pT, cache_k=ck, cache_v=cv,
             st_re=st_lay(inp['state_ssm_re'][:, 4 * c:4 * c + 4]), st_im=st_lay(inp['state_ssm_im'][:, 4 * c:4 * c + 4]))
    return d


def _shared_inputs(inp):
    f32 = np.float32
    sh = {}
    for k in ('rel_bias', 'w_in', 'w_out', 'ffn_w_gate', 'ffn_w_up', 'ffn_w_down', 'w_glu', 'w_ple_gate', 'w_ple_proj'):
        sh[k] = np.ascontiguousarray(inp[k], dtype=f32)
    cols = []
    for l in range(DEPTH):
        cols += [_fm(inp['norm_ffn'][l, 0]), _fm(inp['norm_mix'][l]), _fm(inp['norm_att_out'][l]), _fm(inp['norm_ssm_out'][l]),
                 _fm(inp['norm_ffn'][l, 1]), _fm(inp['norm_ple'][l])]
    cols.append(_fm(inp['norm_final']))
    sh['gains'] = np.ascontiguousarray(np.concatenate(cols, axis=1), dtype=f32)

    def gn(a):
        return np.ascontiguousarray(a.reshape(DEPTH, 16, 2, 64).transpose(0, 2, 3, 1).reshape(DEPTH, 128, 16), dtype=f32)
    sh['s_are'] = gn(inp['ssm_a_re']); sh['s_aim'] = gn(inp['ssm_a_im'])
    sh['s_ldt'] = gn(np.broadcast_to(inp['ssm_log_dt'][:, :, None], (DEPTH, 32, 64)))
    sh['s_d'] = np.ascontiguousarray(inp['ssm_d'].reshape(DEPTH, 4, 128).transpose(0, 2, 1), dtype=f32)
    sh['s_bglu'] = np.ascontiguousarray(inp['b_glu'].reshape(DEPTH, 4, 128).transpose(0, 2, 1), dtype=f32)

    def bn(b):
        out = np.zeros((DEPTH, 2, 64, 16, 128), f32)
        bb = b.reshape(DEPTH, 16, 2, 64, 16)
        for st in range(16):
            for gl in range(2):
                r0 = (st % 4) * 32 + gl * 16
                out[:, gl, :, st, r0:r0 + 16] = bb[:, st, gl]
        return out.reshape(DEPTH, 128, 16, 128)

    def ct(cc):
        out = np.zeros((DEPTH, 2, 64, 16, 128), f32)
        c5 = cc.reshape(DEPTH, 16, 2, 16, 64)
        for st in range(16):
            for gl in range(2):
                r0 = (st % 4) * 32 + gl * 16
                out[:, gl, :, st, r0:r0 + 16] = c5[:, st, gl].transpose(0, 2, 1)
        return out.reshape(DEPTH, 128, 16, 128)
    sh['Bn_re'] = bn(np.asarray(inp['ssm_b_re'])); sh['Bn_im'] = bn(np.asarray(inp['ssm_b_im']))
    sh['Ct_re'] = ct(np.asarray(inp['ssm_c_re'])); sh['Ct_im'] = ct(np.asarray(inp['ssm_c_im']))
    sh.update(_consts())
    return sh


def _run(inp, S, n_cores):
    inp = {k: np.asarray(v) for k, v in inp.items()}
    nc = build(S)
    shared = _shared_inputs(inp)
    in_maps = [_core_inputs(c, S, inp, shared) for c in range(n_cores)]
    res = run_bass_kernel_spmd(nc, in_maps, core_ids=list(range(n_cores)))
    R = res.results
    global LAST_RAW
    LAST_RAW = R
    KEEP = min(2048, S)
    B = n_cores
    y_p = np.zeros((B, S, D), np.float32); y_s = np.zeros((4 * B, 4, D), np.float32)
    k_p = np.zeros((DEPTH, B, KEEP, NH, HD), np.float32); v_p = np.zeros_like(k_p)
    r_p = np.zeros((DEPTH, B, 32, 64), np.float32); i_p = np.zeros_like(r_p)
    k_s = np.zeros((DEPTH, 4 * B, 2048, NH, HD), np.float32); v_s = np.zeros_like(k_s)
    r_s = np.zeros((DEPTH, 4 * B, 32, 64), np.float32); i_s = np.zeros_like(r_s)
    for c in range(n_cores):
        r = R[c]
        yT = r['yT']
        yall = yT.transpose(2, 1, 0).reshape(S + 16, D)
        y_p[c] = yall[:S]; y_s[4 * c:4 * c + 4] = yall[S:].reshape(4, 4, D)
        k_p[:, c] = r['kT_p'].transpose(0, 3, 1, 2)
        v_p[:, c] = r['v_p'].reshape(DEPTH, KEEP, NH, HD)

        def gst(a):
            return a.reshape(DEPTH, 2, 64, 16).transpose(0, 3, 1, 2).reshape(DEPTH, 32, 64)
        r_p[:, c] = gst(r['ssm_p_re']); i_p[:, c] = gst(r['ssm_p_im'])
        k_s[:, 4 * c:4 * c + 4] = r['k_s'].reshape(DEPTH, 4, 2048, NH, HD)
        v_s[:, 4 * c:4 * c + 4] = r['v_s'].reshape(DEPTH, 4, 2048, NH, HD)

        def gss(a):
            return a.reshape(DEPTH, 2, 64, 16, 4).transpose(0, 4, 3, 1, 2).reshape(DEPTH, 4, 32, 64)
        r_s[:, 4 * c:4 * c + 4] = gss(r['ssm_s_re']); i_s[:, 4 * c:4 * c + 4] = gss(r['ssm_s_im'])
    return (y_p, y_s, k_p, v_p, r_p, i_p, k_s, v_s, r_s, i_s)


def kernel(**inputs):
    return _run(inputs, 4096, 8)
```

```python
import math
import os
from contextlib import ExitStack
import numpy as np
import concourse.bass as bass
import concourse.mybir as mybir
from concourse.bass_utils import run_bass_kernel_spmd

F32, BF16 = mybir.dt.float32, mybir.dt.bfloat16
AF = mybir.ActivationFunctionType
ALU = mybir.AluOpType
AX = mybir.AxisListType

D = 1024; DFF = 2816; DEPTH = 2; NH = 8; HD = 64; EPS = 1e-6
NKF = DFF // 128
BRANCH_D = (1, 4, 16)
EW = 384
MAGIC = 12582912.0


def t5_bucket(dist):
    dist = np.asarray(dist, dtype=np.int64)
    exact = 16
    ratio = np.log(np.maximum(dist, 1) / exact) / np.log(2048 / exact)
    large = np.minimum(exact + (ratio * (32 - exact)).astype(np.int64), 31)
    return np.where(dist < exact, dist, large).astype(np.int32)


class Res:
    def __init__(self, ap, sem=None):
        self.ap = ap; self.ready = None; self.frees = []; self.sem = sem; self.dcnt = 0


class SemH:
    def __init__(self, sem):
        self.sem = sem; self.cnt = 0


class KB:
    def __init__(self, nc, es):
        self.nc = nc; self.es = es
        self.engs = {'pe': nc.tensor, 'act': nc.scalar, 'dve': nc.vector, 'pool': nc.gpsimd, 'sp': nc.sync}
        self.sem = {}; self.cnt = {}; self.waited = {}
        for e in ('pe', 'act', 'dve', 'pool'):
            self.sem[e] = es.enter_context(nc.semaphore("sem_" + e)); self.cnt[e] = 0
        self.nsem = 0
        self.out_evs = []
        self.last_dma = {}
        self.sem_pool = []
        self.phase_sems = None
        self.serial = set()

    def fence(self):
        for evt in list(self.last_dma.values()):
            self.wait('sp', evt)
        ET = mybir.EngineType
        self.nc.multi_engine_barrier([ET.PE, ET.Activation, ET.DVE, ET.SP])

    def newsem(self):
        if self.sem_pool:
            h = self.sem_pool.pop()
        else:
            self.nsem += 1
            h = SemH(self.es.enter_context(self.nc.semaphore("ds%d" % self.nsem)))
        if self.phase_sems is not None:
            self.phase_sems.append(h)
        return h

    def soft_fence(self):
        evs = [(e, self.sem[e], self.cnt[e]) for e in ('pe', 'act', 'dve') if self.cnt[e] > 0] + list(self.last_dma.values())
        for e in ('pe', 'act', 'dve', 'sp'):
            for v in evs:
                self.wait(e, v)

    def begin_phase(self):
        self.phase_sems = []

    def end_phase(self):
        self.sem_pool.extend(self.phase_sems)
        self.phase_sems = None

    def last(self, r):
        return ('dma', r.sem.sem, r.sem.cnt)

    def ev(self, e, ins):
        if e in self.serial:
            return (e, self.sem[e], self.cnt[e])
        self.cnt[e] += 1
        ins.then_inc(self.sem[e], 1)
        return (e, self.sem[e], self.cnt[e])

    def wait(self, e, *evs):
        for v in evs:
            if v is None:
                continue
            src, s, val = v
            if src == e:
                continue
            key = (e, s.name if hasattr(s, 'name') else id(s))
            if self.waited.get(key, 0) >= val:
                continue
            self.engs[e].wait_ge(s, val)
            self.waited[key] = val

    def wr(self, e, r):
        self.wait(e, r.ready, *r.frees); r.frees = []

    def rd(self, e, r):
        self.wait(e, r.ready)

    def dma(self, q, out, in_, r, reads=(), writes=True, track=True, waw=True, **kw):
        if r.sem is None:
            r.sem = self.newsem()
        for x in reads:
            self.rd(q, x)
        if writes:
            if waw:
                self.wr(q, r)
            else:
                self.wait(q, *r.frees); r.frees = []
        ins = self.engs[q].dma_start(out=out, in_=in_, **kw)
        r.sem.cnt += 16
        ins.then_inc(r.sem.sem, 16)
        evt = ('dma', r.sem.sem, r.sem.cnt)
        if track:
            self.last_dma[r.sem.sem.name] = evt
        for x in reads:
            x.frees.append(evt)
        if writes:
            r.ready = evt
        return evt


def _ap_range(ap):
    pst = ap.ap[0][0] if ap.ap[0][0] > 0 else (1 << 40)
    lo = ap.offset % pst
    hi = lo + 1
    for (st, cn) in ap.ap[1:]:
        hi += (cn - 1) * abs(st)
    return ap.tensor.name, lo, hi


class SerialEng:
    def __init__(self, kb, name, eng):
        self._kb = kb; self._name = name; self._eng = eng
        self._recs = {}
        self._selfw = 0
        kb.serial.add(name)

    def __getattr__(self, attr):
        real = getattr(self._eng, attr)
        if attr in ('wait_ge', 'dma_start', 'sem_inc'):
            return real
        kb = self._kb; name = self._name

        def call(*a, **k):
            accs = []
            for i, v in enumerate(a):
                if hasattr(v, 'tensor') and hasattr(v, 'ap'):
                    accs.append((_ap_range(v), i == 0))
            for key, v in k.items():
                if hasattr(v, 'tensor') and hasattr(v, 'ap'):
                    accs.append((_ap_range(v), key in ('out', 'accum_out')))
            need = 0
            for (tn, lo, hi), isw in accs:
                for (rlo, rhi, rc, rw) in self._recs.get(tn, ()):
                    if (rw or isw) and rlo < hi and lo < rhi and rc > need:
                        need = rc
            if need > self._selfw:
                self._eng.wait_ge(kb.sem[name], need)
                self._selfw = need
            ins = real(*a, **k)
            c = kb.cnt[name] + 1
            kb.cnt[name] = c
            ins.then_inc(kb.sem[name], 1)
            for (tn, lo, hi), isw in accs:
                lst = self._recs.setdefault(tn, [])
                lst.append((lo, hi, c, isw))
                if len(lst) > 48:
                    del lst[0:16]
            return ins
        return call


class Ring:
    def __init__(self, items):
        self.items = items; self.i = 0

    def next(self):
        r = self.items[self.i % len(self.items)]; self.i += 1
        return r


def build(S=4096):
    T = S + 16
    NT = S // 512
    KEEP = min(2048, S)
    nc = bass.Bass("TRN2", target_bir_lowering=False)
    es = ExitStack()
    kb = KB(nc, es)
    pe, pool, sp = nc.tensor, nc.gpsimd, nc.sync
    act = SerialEng(kb, 'act', nc.scalar)
    dve = SerialEng(kb, 'dve', nc.vector)

    def din(name, shape, dt=F32):
        return nc.dram_tensor(name, list(shape), dt, kind="ExternalInput").ap()

    def dout(name, shape, dt=F32):
        return nc.dram_tensor(name, list(shape), dt, kind="ExternalOutput").ap()

    def dscr(name, shape, dt):
        return nc.dram_tensor(name, list(shape), dt, kind="Internal").ap()

    sbn = [0]

    def sb(name, shape, dt=F32):
        sbn[0] += 1
        return es_cur[0].enter_context(nc.sbuf_tensor("%s_%d" % (name, sbn[0]), list(shape), dt))

    es_cur = [es]
    KDEBUG = bool(os.environ.get("KDEBUG"))
    dbg_names = []

    def dbg(name, ap, eng='dve'):
        if not KDEBUG:
            return
        shp = list(ap.shape)
        o = nc.dram_tensor("dbg_" + name, shp, ap.dtype, kind="ExternalOutput").ap()
        dbg_names.append("dbg_" + name)
        r = Res(None)
        evt = (eng, kb.sem[eng], kb.cnt[eng])
        kb.wait('sp', evt)
        kb.out_evs.append(kb.dma('sp', o, ap, r, writes=False, track=True))
        for e_ in ('dve', 'act', 'pe', 'pool'):
            kb.wait(e_, kb.last(r))

    xT = din("xT", [128, 8, T]); pT = din("pT", [DEPTH, 128, 2, T])
    cache_k = din("cache_k", [DEPTH, 4, 2048, 512]); cache_v = din("cache_v", [DEPTH, 4, 2048, 512])
    st_re = din("st_re", [DEPTH, 128, 16, 4]); st_im = din("st_im", [DEPTH, 128, 16, 4])
    rel_bias = din("rel_bias", [32, 8])
    w_in = din("w_in", [DEPTH, D, 2048]); w_out = din("w_out", [DEPTH, D, D])
    w_g = din("ffn_w_gate", [DEPTH, 2, D, DFF]); w_u = din("ffn_w_up", [DEPTH, 2, D, DFF])
    w_d = din("ffn_w_down", [DEPTH, 2, DFF, D])
    w_glu = din("w_glu", [DEPTH, 512, 512]); w_pg = din("w_ple_gate", [DEPTH, D, D])
    w_pp = din("w_ple_proj", [DEPTH, 256, D])
    NG = DEPTH * 40 + 8
    gains = din("gains", [128, NG])
    s_are = din("s_are", [DEPTH, 128, 16]); s_aim = din("s_aim", [DEPTH, 128, 16]); s_ldt = din("s_ldt", [DEPTH, 128, 16])
    s_d = din("s_d", [DEPTH, 128, 4]); s_bglu = din("s_bglu", [DEPTH, 128, 4])
    Bn_re = din("Bn_re", [DEPTH, 128, 16, 128]); Bn_im = din("Bn_im", [DEPTH, 128, 16, 128])
    Ct_re = din("Ct_re", [DEPTH, 128, 16, 128]); Ct_im = din("Ct_im", [DEPTH, 128, 16, 128])
    c_iota = din("c_iota", [128, 512]); c_oh = din("c_oh", [33, 3 * EW]); c_ident = din("c_ident", [128, 128])
    c_ohnew = din("c_ohnew", [4, 16, 16])

    yT = dout("yT", [128, 8, T])
    kT_p = dout("kT_p", [DEPTH, 8, 64, KEEP]); v_p = dout("v_p", [DEPTH, KEEP, 512])
    ssm_p_re = dout("ssm_p_re", [DEPTH, 128, 16]); ssm_p_im = dout("ssm_p_im", [DEPTH, 128, 16])
    k_s = dout("k_s", [DEPTH, 4, 2048, 512]); v_s = dout("v_s", [DEPTH, 4, 2048, 512])
    ssm_s_re = dout("ssm_s_re", [DEPTH, 128, 16, 4]); ssm_s_im = dout("ssm_s_im", [DEPTH, 128, 16, 4])

    Xs = dscr("Xs", [128, 8, T], F32)
    MIX = (dout if os.environ.get("KDEBUG") else dscr)("MIX", [D, T], BF16)
    QF = dscr("QF", [8, 64, S], BF16); KF = dscr("KF", [8, 64, S], BF16)
    VT = dscr("VT", [S, 512], BF16)
    U32 = dscr("U32", [128, 4, T], F32)
    QKVS = dscr("QKVS", [16, 1536], F32)
    Dsc = dscr("Dsc", [3, 8, 128, EW], F32)
    Dsc2 = dscr("Dsc2", [3, 8, 128, EW], F32)
    LBD = dscr("LBD", [128, 3 * 4 * 512], BF16)
    WGU = [[dscr("WGU%d%d" % (l, f), [11, 128, 4096], BF16) for f in range(2)] for l in range(DEPTH)]
    WD = [[dscr("WD%d%d" % (l, f), [8, 128, NKF * 128], BF16) for f in range(2)] for l in range(DEPTH)]
    WIN = [dscr("WIN%d" % l, [4, 128, 4096], BF16) for l in range(DEPTH)]
    WOUT = [dscr("WOUT%d" % l, [2, 128, 4096], BF16) for l in range(DEPTH)]
    WPG = [dscr("WPG%d" % l, [2, 128, 4096], BF16) for l in range(DEPTH)]
    WPP = [dscr("WPP%d" % l, [128, 2048], BF16) for l in range(DEPTH)]
    WGLU = [dscr("WGLU%d" % l, [128, 2048], BF16) for l in range(DEPTH)]

    cast_ev = {}

    cast_prev = [None]

    def cast_batch(key, pairs):
        r = Res(None)
        evt = None
        kb.wait('pool', cast_prev[0])
        for (o, i) in pairs:
            evt = kb.dma('pool', o, i, r, writes=False, track=False)
        cast_ev[key] = evt
        cast_prev[0] = evt

    def cast_ffn(l, f, fine=False):
        pairs = []
        for c in range(11):
            pairs.append((WGU[l][f][c, :, 0:2048].rearrange("p (k n) -> p k n", k=8),
                          w_g[l, f, :, c * 256:(c + 1) * 256].rearrange("(k p) n -> p k n", p=128)))
            pairs.append((WGU[l][f][c, :, 2048:4096].rearrange("p (k n) -> p k n", k=8),
                          w_u[l, f, :, c * 256:(c + 1) * 256].rearrange("(k p) n -> p k n", p=128)))
            if fine and c % 2 == 1:
                cast_batch(("gu", l, f, c // 2), pairs); pairs = []
        if fine:
            cast_batch(("gu", l, f, 5), pairs)
        else:
            cast_batch(("gu", l, f, 0), pairs)
            for j in range(1, 6):
                cast_ev[("gu", l, f, j)] = cast_ev[("gu", l, f, 0)]
        pairs = []
        for m in range(8):
            pairs.append((WD[l][f][m].rearrange("p (k n) -> p k n", k=NKF),
                          w_d[l, f, :, m * 128:(m + 1) * 128].rearrange("(k p) n -> p k n", p=128)))
            if fine and m % 2 == 1:
                cast_batch(("d", l, f, m // 2), pairs); pairs = []
        if not fine:
            cast_batch(("d", l, f, 0), pairs)
            for j in range(1, 4):
                cast_ev[("d", l, f, j)] = cast_ev[("d", l, f, 0)]

    def cast_sq(key, dst, src, ncols, kk=8):
        pairs = []
        nch = ncols // 512
        for c in range(nch):
            pairs.append((dst[c].rearrange("p (k n) -> p k n", k=kk),
                          src[:, c * 512:(c + 1) * 512].rearrange("(k p) n -> p k n", p=128)))
        cast_batch(key, pairs)

    for l in range(DEPTH):
        if l == 0:
            cast_ffn(l, 0, fine=True)
            cast_sq(("in", l), WIN[l], w_in[l], 2048)
        cast_sq(("out", l), WOUT[l], w_out[l], 1024)
        cast_batch(("glu", l), [(WGLU[l].rearrange("p (k n) -> p k n", k=4), w_glu[l].rearrange("(k p) n -> p k n", p=128))])
        cast_ffn(l, 1)
        cast_sq(("pg", l), WPG[l], w_pg[l], 1024)
        cast_batch(("pp", l), [(WPP[l].rearrange("p (k n) -> p k n", k=2), w_pp[l].rearrange("(k p) n -> p k n", p=128))])
        if l + 1 < DEPTH:
            cast_ffn(l + 1, 0)
            cast_sq(("in", l + 1), WIN[l + 1], w_in[l + 1], 2048)

    psum = [Res(es.enter_context(nc.psum_tensor("ps%d" % i, [128, 512], F32))) for i in range(8)]
    poolA = Ring(psum[0:4]); poolB = Ring(psum[4:6]); poolC = Ring(psum[6:8])

    ones_bf = Res(sb("ones_bf", [128, 128], BF16))
    kb.ev('dve', dve.memset(ones_bf.ap[:], 1.0))
    ones_bf.ready = kb.ev('dve', dve.memset(ones_bf.ap[:], 1.0))
    gsb = Res(sb("gains_sb", [128, NG]))
    kb.dma('sp', gsb.ap[:], gains, gsb)
    ident = Res(sb("ident", [128, 128]))
    kb.dma('sp', ident.ap[:], c_ident, ident)

    def gcol(l, which):
        base = l * 40
        off = {'ffn0': 0, 'mix': 8, 'att': 16, 'ssm': 20, 'ffn1': 24, 'ple': 32}[which]
        return base + off
    GFINAL = DEPTH * 40

    cpy = Res(None)
    for l in range(DEPTH):
        for (src, dst) in ((cache_k, k_s), (cache_v, v_s)):
            for b in range(4):
                for part in range(4):
                    r0 = 4 + part * 511
                    kb.out_evs.append(kb.dma('pool', dst[l, b, r0 - 4:r0 - 4 + 511, :], src[l, b, r0:r0 + 511, :], cpy, writes=False, track=False))

    identb = Res(sb("identb", [128, 128], BF16))
    EBs0 = Res(sb("EBs0", [128, 8, 4]))
    EBs12 = Res(sb("EBs12", [128, 2, 8]))
    EBnew = Res(sb("EBnew", [16, 16, 8]))
    with ExitStack() as es2:
        es_cur[0] = es2
        LBt = Res(sb("LBt", [128, 3, 4, 512], BF16))
        rb = Res(sb("rb", [32, 8])); eb = Res(sb("eb", [32, 8])); ebr = Res(sb("ebr", [32, 8, 128]))
        oh = Res(sb("oh", [33, 3 * EW])); erep = Res(sb("erep", [128, EW])); stgs = Ring([Res(sb("stg%d" % i, [128, 512])) for i in range(2)]); stg = None
        ohn = Res(sb("ohn", [4, 16, 16]))
        kb.dma('sp', rb.ap[:], rel_bias, rb)
        kb.dma('sp', oh.ap[:], c_oh, oh)
        kb.dma('sp', ohn.ap[:], c_ohnew, ohn)
        kb.rd('act', rb)
        eb.ready = kb.ev('act', act.activation(out=eb.ap[:], in_=rb.ap[:], func=AF.Exp))
        kb.rd('dve', eb)
        ebr.ready = kb.ev('dve', dve.tensor_copy(out=ebr.ap[:], in_=eb.ap[:].unsqueeze(2).to_broadcast([32, 8, 128])))
        dres = Res(None)
        for g in range(3):
            for h in range(8):
                ps = poolC.next()
                kb.wr('pe', ps); kb.rd('pe', ebr); kb.rd('pe', oh)
                ps.ready = kb.ev('pe', pe.matmul(ps.ap[:, 0:EW], lhsT=ebr.ap[:, h, :], rhs=oh.ap[0:32, g * EW:(g + 1) * EW], start=True, stop=True))
                kb.rd('dve', ps); kb.wr('dve', erep)
                e1 = kb.ev('dve', dve.tensor_copy(out=erep.ap[:], in_=ps.ap[:, 0:EW]))
                ps.frees.append(e1); erep.ready = e1
                kb.dma('sp', Dsc[g, h], erep.ap[:], dres, reads=(erep,), writes=False)
        dres_all = kb.last(dres)
        kb.wait('sp', dres_all)
        rbr = Res(sb("rbr", [33, 8, 128]))
        kb.rd('dve', rb)
        dve.memset(rbr.ap[32:33, :, :], -30000.0)
        rbr.ready = kb.ev('dve', dve.tensor_copy(out=rbr.ap[0:32], in_=rb.ap[:].unsqueeze(2).to_broadcast([32, 8, 128])))
        dres2 = Res(None)
        for g in range(3):
            for h in range(8):
                ps = poolC.next()
                kb.wr('pe', ps); kb.rd('pe', rbr); kb.rd('pe', oh)
                ps.ready = kb.ev('pe', pe.matmul(ps.ap[:, 0:EW], lhsT=rbr.ap[:, h, :], rhs=oh.ap[:, g * EW:(g + 1) * EW], start=True, stop=True))
                kb.rd('dve', ps); kb.wr('dve', erep)
                e1 = kb.ev('dve', dve.tensor_copy(out=erep.ap[:], in_=ps.ap[:, 0:EW]))
                ps.frees.append(e1); erep.ready = e1
                kb.dma('sp', Dsc2[g, h], erep.ap[:], dres2, reads=(erep,), writes=False)
        kb.wait('sp', kb.last(dres2))
        for g in range(3):
            for hp in range(4):
                stg = stgs.next()
                kb.wr('sp', stg)
                for hh in range(2):
                    h = 2 * hp + hh
                    for slot in range(2):
                        off = 127 if slot == 0 else 255
                        src = bass.AP(tensor=Dsc2.tensor, offset=Dsc2[g, h].offset + off, ap=[[EW - 1, 128], [1, 128]])
                        kb.dma('sp', stg.ap[:, (hh * 2 + slot) * 128:(hh * 2 + slot + 1) * 128], src, stg, waw=False)
                kb.rd('dve', stg); kb.wr('dve', LBt)
                e1 = kb.ev('dve', dve.tensor_copy(out=LBt.ap[:, g, hp, :], in_=stg.ap[:]))
                stg.frees.append(e1); LBt.ready = e1
        kb.rd('dve', ident)
        identb.ready = kb.ev('dve', dve.tensor_copy(out=identb.ap[:], in_=ident.ap[:]))
        kb.dma('sp', LBD, LBt.ap[:].rearrange("p a b c -> p (a b c)"), LBt, reads=(LBt,), writes=False)
        for h in range(8):
            src = bass.AP(tensor=Dsc.tensor, offset=Dsc[0, h].offset + 255, ap=[[EW - 1, 128], [1, 4]])
            kb.dma('sp', EBs0.ap[:, h, :], src, EBs0, waw=False)
            for g in (1, 2):
                src = bass.AP(tensor=Dsc.tensor, offset=Dsc[g, h].offset + 255, ap=[[EW - 1, 128], [1, 1]])
                kb.dma('sp', EBs12.ap[:, g - 1, h:h + 1], src, EBs12, allow_slow_non_contiguous=True, waw=False)
        psn = poolC.next()
        kb.wr('pe', psn); kb.rd('pe', ohn); kb.rd('pe', eb)
        for q in range(16):
            e1 = pe.matmul(psn.ap[0:16, q * 8:(q + 1) * 8], lhsT=ohn.ap[:, q, :], rhs=eb.ap[0:4, :], start=True, stop=True)
        psn.ready = kb.ev('pe', e1)
        kb.rd('dve', psn)
        e1 = kb.ev('dve', dve.tensor_copy(out=EBnew.ap[:].rearrange("k q h -> k (q h)"), in_=psn.ap[0:16, 0:128]))
        psn.frees.append(e1); EBnew.ready = e1
        kb.soft_fence()
    es_cur[0] = es

    def row_phase(phase):
        with ExitStack() as esr:
            es_cur[0] = esr
            kb.begin_phase()
            NW = 4
            wring = Ring([Res(sb("wslot%d" % i, [128, 4096], BF16)) for i in range(NW)])
            xts = Ring([Res(sb("xt%d" % i, [128, 8, 512])) for i in range(2)])
            hs = Ring([Res(sb("h%d" % i, [128, 8, 512], BF16)) for i in range(2)])
            sqs = Res(sb("sq", [128, 8, 512], BF16))
            abuf = Res(sb("abuf", [128, NKF, 512], BF16))
            sgs = Ring([Res(sb("sg%d" % i, [128, 512])) for i in range(2)])
            rstd = Res(sb("rstd", [128, 512]))
            if phase >= 1:
                mixs = Ring([Res(sb("mix%d" % i, [128, 8, 512], BF16)) for i in range(1)])
                mixn = Res(sb("mixn", [128, 8, 512], BF16))
                pts = Ring([Res(sb("pt%d" % i, [128, 2, 512])) for i in range(2)])
                ptb = Res(sb("ptb", [128, 2, 512], BF16))
                gate = Res(sb("gate", [128, 8, 512]))
            if phase <= 1:
                qsb = Ring([Res(sb("qsb%d" % i, [64, 8, 512], BF16)) for i in range(1)])
                ksb = Ring([Res(sb("ksb%d" % i, [64, 8, 512], BF16)) for i in range(1)])
                k32 = Ring([Res(sb("k32_%d" % i, [64, 512])) for i in range(2)])
                usb = Ring([Res(sb("usb%d" % i, [128, 4, 512])) for i in range(1)])
                vbf = Ring([Res(sb("vbf%d" % i, [128, 512], BF16)) for i in range(2)])
                v32 = Ring([Res(sb("v32_%d" % i, [128, 512])) for i in range(2)])
                tms = Res(sb("tms", [16, 1536]))
            tmp = Ring([Res(sb("tmp%d" % i, [128, 512])) for i in range(2)])

            steps = []

            def norm(xt, n, gc0, hout, kr=range(8), src=None):
                src = src or xt
                nk = len(kr)

                def fn(_):
                    kb.rd('act', src); kb.wr('act', sqs); kb.rd('dve', src); kb.wr('dve', sqs)
                    eA = eD = None
                    for i, k in enumerate(kr):
                        if i % 2 == 0:
                            eA = kb.ev('act', act.activation(out=sqs.ap[:, k, :n], in_=src.ap[:, k, :n], func=AF.Square))
                        else:
                            eD = kb.ev('dve', dve.tensor_tensor(out=sqs.ap[:, k, :n], in0=src.ap[:, k, :n], in1=src.ap[:, k, :n], op=ALU.mult))
                    sqs.ready = eA; src.frees.append(eA); src.frees.append(eD)
                    ps = poolC.next()
                    kb.wr('pe', ps); kb.wait('pe', eA, eD); kb.rd('pe', ones_bf)
                    for i, k in enumerate(kr):
                        e2 = pe.matmul(ps.ap[:, :n], lhsT=ones_bf.ap[:], rhs=sqs.ap[:, k, :n], start=(i == 0), stop=(i == nk - 1))
                    e2 = kb.ev('pe', e2); ps.ready = e2; sqs.frees.append(e2)
                    kb.rd('act', ps); kb.wr('act', rstd)
                    act.activation(out=rstd.ap[:, :n], in_=ps.ap[:, :n], func=AF.Ln, scale=1.0 / (128 * nk), bias=epsb.ap[:, 0:1])
                    e3 = kb.ev('act', act.activation(out=rstd.ap[:, :n], in_=rstd.ap[:, :n], func=AF.Exp, scale=-0.5))
                    ps.frees.append(e3)
                    kb.wait('dve', e3)
                    kb.rd('dve', src); kb.wr('dve', hout); kb.rd('dve', gsb)
                    for i, k in enumerate(kr):
                        e4 = dve.scalar_tensor_tensor(out=hout.ap[:, k, :n], in0=src.ap[:, k, :n], scalar=gsb.ap[:, gc0 + i:gc0 + i + 1],
                                                      in1=rstd.ap[:, :n], op0=ALU.mult, op1=ALU.mult)
                    e4 = kb.ev('dve', e4); hout.ready = e4; src.frees.append(e4); rstd.frees.append(e4); rstd.ready = e4
                steps.append((None, fn))

            def ffn(l, f, xt, h, n):
                for c in range(11):
                    def fn(w, c=c):
                        for mi in range(2):
                            m = 2 * c + mi
                            pg = poolA.next(); pu = poolA.next()
                            kb.wr('pe', pg); kb.wr('pe', pu); kb.rd('pe', h); kb.rd('pe', w)
                            for k in range(8):
                                e1 = pe.matmul(pg.ap[:, :n], lhsT=w.ap[:, k * 256 + mi * 128:k * 256 + mi * 128 + 128], rhs=h.ap[:, k, :n], start=(k == 0), stop=(k == 7))
                            pg.ready = kb.ev('pe', e1)
                            for k in range(8):
                                e1 = pe.matmul(pu.ap[:, :n], lhsT=w.ap[:, 2048 + k * 256 + mi * 128:2048 + k * 256 + mi * 128 + 128], rhs=h.ap[:, k, :n], start=(k == 0), stop=(k == 7))
                            e1 = kb.ev('pe', e1); pu.ready = e1
                            if mi == 1:
                                w.frees.append(e1)
                                if c == 10:
                                    h.frees.append(e1)
                            sg = sgs.next()
                            kb.rd('act', pg); kb.wr('act', sg)
                            e2 = kb.ev('act', act.activation(out=sg.ap[:, :n], in_=pg.ap[:, :n], func=AF.Silu))
                            sg.ready = e2; pg.frees.append(e2)
                            kb.rd('dve', sg); kb.rd('dve', pu)
                            if m == 0:
                                kb.wr('dve', abuf)
                            e3 = kb.ev('dve', dve.tensor_tensor(out=abuf.ap[:, m, :n], in0=sg.ap[:, :n], in1=pu.ap[:, :n], op=ALU.mult))
                            sg.frees.append(e3); pu.frees.append(e3); abuf.ready = e3
                    steps.append(((WGU[l][f][c], ("gu", l, f, c // 2)), fn))
                for m in range(8):
                    def fn(w, m=m):
                        ps = poolB.next()
                        kb.wr('pe', ps); kb.rd('pe', abuf); kb.rd('pe', w)
                        for k in range(NKF):
                            e1 = pe.matmul(ps.ap[:, :n], lhsT=w.ap[:, k * 128:(k + 1) * 128], rhs=abuf.ap[:, k, :n], start=(k == 0), stop=(k == NKF - 1))
                        e1 = kb.ev('pe', e1); ps.ready = e1; w.frees.append(e1)
                        if m == 7:
                            abuf.frees.append(e1)
                        kb.rd('dve', ps); kb.wr('dve', xt)
                        e2 = kb.ev('dve', dve.scalar_tensor_tensor(out=xt.ap[:, m, :n], in0=ps.ap[:, :n], scalar=0.5, in1=xt.ap[:, m, :n], op0=ALU.mult, op1=ALU.add))
                        ps.frees.append(e2); xt.ready = e2
                    steps.append(((WD[l][f][m][:, 0:NKF * 128], ("d", l, f, m // 2)), fn))

            def proj(l, h, n, c0, is_sample):
                def fq(w):
                    if is_sample:
                        ps = poolB.next(); kb.wr('pe', ps); kb.rd('pe', h); kb.rd('pe', w)
                        for k in range(8):
                            e1 = pe.matmul(ps.ap[:16, :], lhsT=h.ap[:, k, :16], rhs=w.ap[:, k * 512:(k + 1) * 512], start=(k == 0), stop=(k == 7))
                        e1 = kb.ev('pe', e1); ps.ready = e1; w.frees.append(e1)
                        kb.rd('act', ps); kb.wr('act', tms)
                        e2 = kb.ev('act', act.mul(out=tms.ap[:, 0:512], in_=ps.ap[:16, :], mul=0.125))
                        ps.frees.append(e2); tms.ready = e2
                        return
                    q = qsb.next(); kb.wr('act', q)
                    for hd in range(8):
                        ps = poolB.next(); kb.wr('pe', ps); kb.rd('pe', h); kb.rd('pe', w)
                        for k in range(8):
                            e1 = pe.matmul(ps.ap[:64, :n], lhsT=w.ap[:, k * 512 + hd * 64:k * 512 + hd * 64 + 64], rhs=h.ap[:, k, :n], start=(k == 0), stop=(k == 7))
                        e1 = kb.ev('pe', e1); ps.ready = e1
                        kb.rd('act', ps)
                        e2 = kb.ev('act', act.mul(out=q.ap[:, hd, :n], in_=ps.ap[:64, :n], mul=0.125))
                        ps.frees.append(e2); q.ready = e2
                    w.frees.append(e1)
                    kb.dma('sp', QF[:, :, c0:c0 + n].rearrange("h p s -> p h s"), q.ap[:, :, :n], q, reads=(q,), writes=False)
                if 'fq' in os.environ.get('KP', 'fqfkfvfu'):
                    steps.append(((WIN[l][0], ("in", l)), fq))

                def fk(w):
                    if is_sample:
                        ps = poolB.next(); kb.wr('pe', ps); kb.rd('pe', h); kb.rd('pe', w)
                        for k in range(8):
                            e1 = pe.matmul(ps.ap[:16, :], lhsT=h.ap[:, k, :16], rhs=w.ap[:, k * 512:(k + 1) * 512], start=(k == 0), stop=(k == 7))
                        e1 = kb.ev('pe', e1); ps.ready = e1; w.frees.append(e1)
                        kb.rd('act', ps); kb.wr('act', tms)
                        e2 = kb.ev('act', act.copy(out=tms.ap[:, 512:1024], in_=ps.ap[:16, :]))
                        ps.frees.append(e2); tms.ready = e2
                        return
                    kk = ksb.next(); kb.wr('act', kk)
                    for hd in range(8):
                        ps = poolB.next(); kb.wr('pe', ps); kb.rd('pe', h); kb.rd('pe', w)
                        for k in range(8):
                            e1 = pe.matmul(ps.ap[:64, :n], lhsT=w.ap[:, k * 512 + hd * 64:k * 512 + hd * 64 + 64], rhs=h.ap[:, k, :n], start=(k == 0), stop=(k == 7))
                        e1 = kb.ev('pe', e1); ps.ready = e1
                        if c0 >= S - KEEP:
                            k3 = k32.next(); kb.rd('dve', ps); kb.wr('dve', k3)
                            e3 = kb.ev('dve', dve.tensor_copy(out=k3.ap[:, :n], in_=ps.ap[:64, :n]))
                            ps.frees.append(e3); k3.ready = e3
                            kb.rd('act', k3)
                            e2 = kb.ev('act', act.copy(out=kk.ap[:, hd, :n], in_=k3.ap[:, :n]))
                            k3.frees.append(e2); kk.ready = e2
                            o0 = c0 - (S - KEEP)
                            kb.out_evs.append(kb.dma('sp', kT_p[l, hd, :, o0:o0 + n], k3.ap[:, :n], k3, reads=(k3,), writes=False))
                        else:
                            kb.rd('act', ps)
                            e2 = kb.ev('act', act.copy(out=kk.ap[:, hd, :n], in_=ps.ap[:64, :n]))
                            ps.frees.append(e2); kk.ready = e2
                    w.frees.append(e1)
                    kb.dma('sp', KF[:, :, c0:c0 + n].rearrange("h p s -> p h s"), kk.ap[:, :, :n], kk, reads=(kk,), writes=False)
                if 'fk' in os.environ.get('KP', 'fqfkfvfu'):
                    steps.append(((WIN[l][1], ("in", l)), fk))

                def fv(w):
                    if is_sample:
                        ps = poolB.next(); kb.wr('pe', ps); kb.rd('pe', h); kb.rd('pe', w)
                        for k in range(8):
                            e1 = pe.matmul(ps.ap[:16, :], lhsT=h.ap[:, k, :16], rhs=w.ap[:, k * 512:(k + 1) * 512], start=(k == 0), stop=(k == 7))
                        e1 = kb.ev('pe', e1); ps.ready = e1; w.frees.append(e1)
                        kb.rd('act', ps); kb.wr('act', tms)
                        e2 = kb.ev('act', act.copy(out=tms.ap[:, 1024:1536], in_=ps.ap[:16, :]))
                        ps.frees.append(e2); tms.ready = e2
                        kb.dma('sp', QKVS, tms.ap[:], tms, reads=(tms,), writes=False)
                        qkvs_ev[l] = kb.last(tms)
                        for b in range(4):
                            kb.out_evs.append(kb.dma('sp', k_s[l, b, 2044:2048, :], tms.ap[b * 4:(b + 1) * 4, 512:1024], tms, reads=(tms,), writes=False))
                            kb.out_evs.append(kb.dma('sp', v_s[l, b, 2044:2048, :], tms.ap[b * 4:(b + 1) * 4, 1024:1536], tms, reads=(tms,), writes=False))
                        return
                    for tb in range(n // 128):
                        ps = poolB.next(); kb.wr('pe', ps); kb.rd('pe', h); kb.rd('pe', w)
                        for k in range(8):
                            e1 = pe.matmul(ps.ap[:, :], lhsT=h.ap[:, k, tb * 128:(tb + 1) * 128], rhs=w.ap[:, k * 512:(k + 1) * 512], start=(k == 0), stop=(k == 7))
                        e1 = kb.ev('pe', e1); ps.ready = e1
                        t0 = c0 + tb * 128
                        v3 = v32.next(); kb.rd('dve', ps); kb.wr('dve', v3)
                        e3 = kb.ev('dve', dve.tensor_copy(out=v3.ap[:], in_=ps.ap[:]))
                        ps.frees.append(e3); v3.ready = e3
                        vb = vbf.next(); kb.rd('act', v3); kb.wr('act', vb)
                        e2 = kb.ev('act', act.copy(out=vb.ap[:], in_=v3.ap[:]))
                        v3.frees.append(e2); vb.ready = e2
                        kb.dma('sp', VT[t0:t0 + 128, :], vb.ap[:], vb, reads=(vb,), writes=False)
                        if t0 >= S - KEEP:
                            kb.out_evs.append(kb.dma('sp', v_p[l, t0 - (S - KEEP):t0 - (S - KEEP) + 128, :], v3.ap[:], v3, reads=(v3,), writes=False))
                    w.frees.append(e1)
                if 'fv' in os.environ.get('KP', 'fqfkfvfu'):
                    steps.append(((WIN[l][2], ("in", l)), fv))

                def fu(w):
                    u = usb.next(); kb.wr('act', u)
                    for m in range(4):
                        ps = poolB.next(); kb.wr('pe', ps); kb.rd('pe', h); kb.rd('pe', w)
                        for k in range(8):
                            e1 = pe.matmul(ps.ap[:, :n], lhsT=w.ap[:, k * 512 + m * 128:k * 512 + m * 128 + 128], rhs=h.ap[:, k, :n], start=(k == 0), stop=(k == 7))
                        e1 = kb.ev('pe', e1); ps.ready = e1
                        kb.rd('act', ps)
                        e2 = kb.ev('act', act.copy(out=u.ap[:, m, :n], in_=ps.ap[:, :n]))
                        ps.frees.append(e2); u.ready = e2
                    w.frees.append(e1); h.frees.append(e1)
                    kb.dma('sp', U32[:, :, c0:c0 + n], u.ap[:, :, :n], u, reads=(u,), writes=False)
                if 'fu' in os.environ.get('KP', 'fqfkfvfu'):
                    steps.append(((WIN[l][3], ("in", l)), fu))

            def lin_res(wscr, key, nchunks, xt, rhs_res, n):
                for c in range(nchunks):
                    def fn(w, c=c):
                        for mi in range(4):
                            m = 4 * c + mi
                            ps = poolB.next(); kb.wr('pe', ps); kb.rd('pe', rhs_res); kb.rd('pe', w)
                            for k in range(8):
                                e1 = pe.matmul(ps.ap[:, :n], lhsT=w.ap[:, k * 512 + mi * 128:k * 512 + mi * 128 + 128], rhs=rhs_res.ap[:, k, :n], start=(k == 0), stop=(k == 7))
                            e1 = kb.ev('pe', e1); ps.ready = e1
                            kb.rd('dve', ps); kb.wr('dve', xt)
                            e2 = kb.ev('dve', dve.tensor_tensor(out=xt.ap[:, m, :n], in0=ps.ap[:, :n], in1=xt.ap[:, m, :n], op=ALU.add))
                            ps.frees.append(e2); xt.ready = e2
                        w.frees.append(e1)
                        if c == nchunks - 1:
                            rhs_res.frees.append(e1)
                    steps.append(((wscr[c], key), fn))

            def ple(l, xt, h, pt, n):
                for c in range(2):
                    def fn(w, c=c):
                        for mi in range(4):
                            m = 4 * c + mi
                            ps = poolB.next(); kb.wr('pe', ps); kb.rd('pe', h); kb.rd('pe', w)
                            for k in range(8):
                                e1 = pe.matmul(ps.ap[:, :n], lhsT=w.ap[:, k * 512 + mi * 128:k * 512 + mi * 128 + 128], rhs=h.ap[:, k, :n], start=(k == 0), stop=(k == 7))
                            e1 = kb.ev('pe', e1); ps.ready = e1
                            kb.rd('act', ps)
                            if m == 0:
                                kb.wr('act', gate)
                            e2 = kb.ev('act', act.activation(out=gate.ap[:, m, :n], in_=ps.ap[:, :n], func=AF.Sigmoid))
                            ps.frees.append(e2); gate.ready = e2
                        w.frees.append(e1)
                        if c == 1:
                            h.frees.append(e1)
                    steps.append(((WPG[l][c], ("pg", l)), fn))

                def fp(w):
                    kb.rd('act', pt); kb.wr('act', ptb)
                    e0 = kb.ev('act', act.copy(out=ptb.ap[:, :, :n], in_=pt.ap[:, :, :n]))
                    ptb.ready = e0; pt.frees.append(e0)
                    for m in range(8):
                        ps = poolB.next(); kb.wr('pe', ps); kb.rd('pe', ptb); kb.rd('pe', w)
                        for k in range(2):
                            e1 = pe.matmul(ps.ap[:, :n], lhsT=w.ap[:, k * 1024 + m * 128:k * 1024 + m * 128 + 128], rhs=ptb.ap[:, k, :n], start=(k == 0), stop=(k == 1))
                        e1 = kb.ev('pe', e1); ps.ready = e1
                        tp = tmp.next()
                        kb.rd('dve', ps); kb.rd('dve', gate); kb.wr('dve', tp); kb.wr('dve', xt)
                        dve.tensor_tensor(out=tp.ap[:, :n], in0=ps.ap[:, :n], in1=gate.ap[:, m, :n], op=ALU.mult)
                        e2 = kb.ev('dve', dve.tensor_tensor(out=xt.ap[:, m, :n], in0=tp.ap[:, :n], in1=xt.ap[:, m, :n], op=ALU.add))
                        ps.frees.append(e2); xt.ready = e2; tp.ready = e2
                    w.frees.append(e1); ptb.frees.append(e1); gate.frees.append(e2)
                steps.append(((WPP[l], ("pp", l)), fp))

            tiles = [(j * 512, 512) for j in range(NT)] + [(S, 16)]
            if os.environ.get("KT"):
                tiles = [tiles[int(x)] for x in os.environ["KT"].split(",")]
            kproj = int(os.environ.get("KPROJ", "1"))
            pre = {}

            def emit_loads(ti):
                (c0_, n_) = tiles[ti]
                xt_ = xts.next()
                src_x_ = xT if phase == 0 else Xs

                def fload(_, xt=xt_, c0=c0_, n=n_, src_x=src_x_):
                    kb.dma('sp', xt.ap[:, :, :n], src_x[:, :, c0:c0 + n], xt)
                steps.append((None, fload))
                mx_ = pt_ = None
                if phase >= 1:
                    mx_ = mixs.next(); pt_ = pts.next()

                    def fl2(_, mx=mx_, pt=pt_, c0=c0_, n=n_, l=phase - 1):
                        kb.dma('sp', mx.ap[:, :, :n], MIX[:, c0:c0 + n].rearrange("(k p) t -> p k t", p=128), mx)
                        kb.dma('sp', pt.ap[:, :, :n], pT[l, :, :, c0:c0 + n], pt)
                    steps.append((None, fl2))
                pre[ti] = (xt_, mx_, pt_)

            emit_loads(0)
            for ti, (c0, n) in enumerate(tiles):
                is_s = (n == 16)
                xt, mx, pt = pre[ti]
                h = hs.next()
                if phase >= 1:
                    l = phase - 1
                    norm(None, n, gcol(l, 'att'), mixn, kr=range(0, 4), src=mx)
                    norm(None, n, gcol(l, 'ssm'), mixn, kr=range(4, 8), src=mx)
                    lin_res(WOUT[l], ("out", l), 2, xt, mixn, n)
                    norm(xt, n, gcol(l, 'ffn1'), h)
                    ffn(l, 1, xt, h, n)
                    if ti + 1 < len(tiles):
                        emit_loads(ti + 1)
                    h = hs.next()
                    norm(xt, n, gcol(l, 'ple'), h)
                    ple(l, xt, h, pt, n)
                    h = hs.next()
                if phase <= 1:
                    l = phase
                    norm(xt, n, gcol(l, 'ffn0'), h)
                    ffn(l, 0, xt, h, n)
                    if phase == 0 and ti + 1 < len(tiles):
                        emit_loads(ti + 1)
                    h = hs.next()
                    norm(xt, n, gcol(l, 'mix'), h)
                    if kproj:
                        proj(l, h, n, c0, is_s)

                    def fstore(_, xt=xt, c0=c0, n=n):
                        kb.dma('sp', Xs[:, :, c0:c0 + n], xt.ap[:, :, :n], xt, reads=(xt,), writes=False)
                        xs_evs.append(kb.last(xt))
                    steps.append((None, fstore))
                else:
                    yb = h

                    def fin(_, xt=xt, c0=c0, n=n):
                        kb.rd('act', xt); kb.wr('act', sqs)
                        for k in range(8):
                            e1 = act.activation(out=sqs.ap[:, k, :n], in_=xt.ap[:, k, :n], func=AF.Square)
                        e1 = kb.ev('act', e1); sqs.ready = e1
                        ps = poolC.next(); kb.wr('pe', ps); kb.rd('pe', sqs)
                        for k in range(8):
                            e2 = pe.matmul(ps.ap[:, :n], lhsT=ones_bf.ap[:], rhs=sqs.ap[:, k, :n], start=(k == 0), stop=(k == 7))
                        e2 = kb.ev('pe', e2); ps.ready = e2; sqs.frees.append(e2)
                        kb.rd('act', ps); kb.wr('act', rstd)
                        e3 = kb.ev('act', act.activation(out=rstd.ap[:, :n], in_=ps.ap[:, :n], func=AF.Sqrt, scale=1.0 / 1024, bias=epsb.ap[:, 0:1]))
                        ps.frees.append(e3)
                        kb.wait('dve', e3); kb.wr('dve', gate)
                        dve.reciprocal(out=rstd.ap[:, :n], in_=rstd.ap[:, :n])
                        for k in range(8):
                            e4 = dve.scalar_tensor_tensor(out=gate.ap[:, k, :n], in0=xt.ap[:, k, :n], scalar=gsb.ap[:, GFINAL + k:GFINAL + k + 1],
                                                          in1=rstd.ap[:, :n], op0=ALU.mult, op1=ALU.mult)
                        e4 = kb.ev('dve', e4); gate.ready = e4; xt.frees.append(e4); rstd.ready = e4
                        kb.out_evs.append(kb.dma('sp', yT[:, :, c0:c0 + n], gate.ap[:, :, :n], gate, reads=(gate,), writes=False))
                    steps.append((None, fin))

            wsteps = [i for i, s in enumerate(steps) if s[0] is not None]
            slot_of = {}
            nl = [0]

            def issue_loads(upto_step):
                while nl[0] < len(wsteps):
                    si = wsteps[nl[0]]
                    if nl[0] >= NW and wsteps[nl[0] - NW] >= upto_step:
                        break
                    if si > upto_step + 40:
                        break
                    w = wring.next()
                    (src, key) = steps[si][0]
                    kb.wait('sp', cast_ev[key])
                    ncol = src.shape[-1]
                    kb.dma('sp', w.ap[:, 0:ncol], src, w)
                    slot_of[si] = w
                    nl[0] += 1
            for i, (chunk, fn) in enumerate(steps):
                issue_loads(i)
                fn(slot_of.get(i))
            kb.fence()
            kb.end_phase()
        es_cur[0] = es

    epsb = Res(sb("epsb", [128, 1]))
    epsb.ready = kb.ev('dve', dve.memset(epsb.ap[:], EPS))
    kb.wait('act', epsb.ready)
    xs_evs = []
    qkvs_ev = {}

    def attention(l):
        with ExitStack() as esa:
            es_cur[0] = esa
            kb.begin_phase()
            nblk = S // 128
            LBt = Res(sb("LBt", [128, 3, 4, 512], BF16))
            kb.dma('sp', LBt.ap[:].rearrange("p a b c -> p (a b c)"), LBD, LBt)
            Vg = [Res(sb("Vg%d" % g, [128, nblk, 512], BF16)) for g in range(3)]
            for g, d in enumerate(BRANCH_D):
                nb = S // (128 * d)
                for r in range(d):
                    src = VT.rearrange("(b p r) f -> r p b f", r=d, p=128)[r]
                    for b0 in range(0, nb, 4):
                        b1 = min(nb, b0 + 4)
                        kb.dma('sp', Vg[g].ap[:, r * nb + b0:r * nb + b1, :], src[:, b0:b1, :], Vg[g], waw=False)
            Qs = Ring([Res(sb("Q%d" % i, [64, 2, S], BF16)) for i in range(1)])
            Ks = Ring([Res(sb("K%d" % i, [64, 2, S], BF16)) for i in range(1)])
            HALF = min(2048, S)
            acc = Res(sb("acc", [64, 2, 2, HALF]))
            Ps = Ring([Res(sb("P%d" % i, [128, 512], BF16)) for i in range(4)])
            poolO = Ring(psum[4:8])
            att = Res(sb("attb", [64, 2, HALF], BF16))
            for hp in range(4):
                Q = Qs.next(); K = Ks.next()
                kb.dma('sp', Q.ap[:], QF[2 * hp:2 * hp + 2].rearrange("h p s -> p h s"), Q)
                kb.dma('sp', K.ap[:], KF[2 * hp:2 * hp + 2].rearrange("h p s -> p h s"), K)
                for half in range(S // HALF):
                    kb.wr('act', acc)
                    acc.ready = kb.ev('act', act.memzero(acc.ap[:]))
                    kb.wait('dve', acc.ready)
                    aunits = []
                    for g, d in enumerate(BRANCH_D):
                        bh = HALF // (128 * d)
                        for r in range(d):
                            for b in range(half * bh, (half + 1) * bh):
                                aunits.append((g, d, r, b))

                    def emitA(g, d, r, b):
                        t0 = r + d * 128 * b
                        sl_q = slice(t0, t0 + 127 * d + 1, d)
                        nsl = 2 if b > 0 else 1
                        pS = poolA.next(); kb.wr('pe', pS); kb.rd('pe', Q); kb.rd('pe', K); kb.rd('pe', LBt); kb.rd('pe', identb)
                        pe.matmul(pS.ap[:, :], lhsT=identb.ap[:], rhs=LBt.ap[:, g, hp, :], start=True, stop=False)
                        for hh in range(2):
                            for s_i in range(nsl):
                                k0 = r + d * 128 * (b - s_i)
                                e1 = pe.matmul(pS.ap[:, (hh * 2 + s_i) * 128:(hh * 2 + s_i + 1) * 128], lhsT=K.ap[:, hh, k0:k0 + 127 * d + 1:d],
                                               rhs=Q.ap[:, hh, sl_q], start=False, stop=(hh == 1 and s_i == nsl - 1))
                        pS.ready = kb.ev('pe', e1)
                        P = Ps.next()
                        kb.rd('act', pS); kb.wr('act', P)
                        if nsl == 2:
                            e2 = act.activation(out=P.ap[:], in_=pS.ap[:], func=AF.Exp)
                        else:
                            for hh in range(2):
                                e2 = act.activation(out=P.ap[:, hh * 256:hh * 256 + 128], in_=pS.ap[:, hh * 256:hh * 256 + 128], func=AF.Exp)
                        e2 = kb.ev('act', e2); P.ready = e2; pS.frees.append(e2)
                        return P

                    def emitB(g, d, r, b, P):
                        nb = S // (128 * d)
                        t0 = r + d * 128 * b
                        nsl = 2 if b > 0 else 1
                        pO = poolO.next(); kb.wr('pe', pO); kb.rd('pe', P); kb.rd('pe', Vg[g])
                        for hh in range(2):
                            h = 2 * hp + hh
                            for s_i in range(nsl):
                                vb = r * nb + (b - s_i)
                                e4 = pe.matmul(pO.ap[:64, (hh * 2) * 128:(hh * 2 + 1) * 128], lhsT=Vg[g].ap[:, vb, h * 64:(h + 1) * 64],
                                               rhs=P.ap[:, (hh * 2 + s_i) * 128:(hh * 2 + s_i + 1) * 128], start=(s_i == 0), stop=(s_i == nsl - 1))
                            for s_i in range(nsl):
                                e4 = pe.matmul(pO.ap[:64, (hh * 2 + 1) * 128:(hh * 2 + 2) * 128], lhsT=ones_bf.ap[:, 0:64],
                                               rhs=P.ap[:, (hh * 2 + s_i) * 128:(hh * 2 + s_i + 1) * 128], start=(s_i == 0), stop=(s_i == nsl - 1))
                        e4 = kb.ev('pe', e4); pO.ready = e4; P.frees.append(e4)
                        kb.rd('dve', pO)
                        l0 = t0 - half * HALF
                        av = acc.ap[:].rearrange("p a b t -> p (a b) t")[:, :, l0:l0 + 127 * d + 1:d]
                        e5 = kb.ev('dve', dve.tensor_tensor(out=av, in0=pO.ap[:64, :].rearrange("p (a q) -> p a q", a=4), in1=av, op=ALU.add))
                        pO.frees.append(e5); acc.ready = e5
                        return e4

                    Pq = {0: emitA(*aunits[0])}
                    for ui, un in enumerate(aunits):
                        if ui + 1 < len(aunits):
                            Pq[ui + 1] = emitA(*aunits[ui + 1])
                        e4 = emitB(*un, Pq.pop(ui))
                    kb.wr('dve', att)
                    kb.rd('act', acc)
                    act.activation(out=acc.ap[:, :, 1, :], in_=acc.ap[:, :, 1, :], func=AF.Ln)
                    eR = kb.ev('act', act.activation(out=acc.ap[:, :, 1, :], in_=acc.ap[:, :, 1, :], func=AF.Exp, scale=-1.0))
                    kb.wait('dve', eR)
                    e6 = kb.ev('dve', dve.tensor_tensor(out=att.ap[:], in0=acc.ap[:, :, 0, :], in1=acc.ap[:, :, 1, :], op=ALU.mult))
                    att.ready = e6; acc.frees.append(e6); acc.ready = e6
                    kb.dma('sp', MIX[hp * 128:(hp + 1) * 128, half * HALF:(half + 1) * HALF].rearrange("(h p) t -> p h t", p=64), att.ap[:], att, reads=(att,), writes=False)
                    mix_evs.append(kb.last(att))
                Q.frees.append(e4); K.frees.append(e4)
            kb.fence()
            kb.end_phase()
        es_cur[0] = es

    mix_evs = []

    def ssm(l):
        with ExitStack() as ess:
            es_cur[0] = ess
            kb.begin_phase()
            L = 512

            def t16(name):
                return Res(sb(name, [128, 16]))
            are = t16("are"); aim = t16("aim"); ldt = t16("ldt"); dt = t16("dt"); rr = t16("rr"); phi = t16("phi")
            t1 = t16("t1"); t2 = t16("t2"); cph = t16("cph"); sph = t16("sph"); abr = t16("abr"); abi = t16("abi"); nabi = t16("nabi")
            fre = t16("fre"); fim = t16("fim"); nfim = t16("nfim")
            dsk = Res(sb("dsk", [128, 4])); bgl = Res(sb("bgl", [128, 4]))
            ld = Res(None)
            for (dst, src) in ((are, s_are[l]), (aim, s_aim[l]), (ldt, s_ldt[l]), (dsk, s_d[l]), (bgl, s_bglu[l])):
                kb.dma('sp', dst.ap[:], src, ld, writes=False)
            Bt = [Res(sb("Bt%d" % i, [128, 16, 128], BF16)) for i in range(2)]
            Ctb = [Res(sb("Ctb%d" % i, [128, 16, 128], BF16)) for i in range(2)]
            cosT = Res(sb("cosT", [128, 16, L])); sinT = Res(sb("sinT", [128, 16, L]))
            wgl = Res(sb("wgl", [128, 2048], BF16))
            kb.wait('sp', cast_ev[("glu", l)])
            kb.dma('sp', wgl.ap[:], WGLU[l], ld, writes=False)
            h0r = Res(sb("h0r", [128, 16, 4])); h0i = Res(sb("h0i", [128, 16, 4]))
            kb.dma('sp', h0r.ap[:], st_re[l], ld, writes=False); kb.dma('sp', h0i.ap[:], st_im[l], ld, writes=False)
            esp = ExitStack(); es_cur[0] = esp
            Bn = [Res(sb("Bn%d" % i, [128, 16, 128])) for i in range(2)]
            kb.dma('sp', Bn[0].ap[:], Bn_re[l], ld, writes=False); kb.dma('sp', Bn[1].ap[:], Bn_im[l], ld, writes=False)
            Cn = [Res(sb("Cn%d" % i, [128, 16, 128])) for i in range(2)]
            kb.dma('sp', Cn[0].ap[:], Ct_re[l], ld, writes=False); kb.dma('sp', Cn[1].ap[:], Ct_im[l], ld, writes=False)
            iot = Res(sb("iot", [128, 512]))
            kb.dma('sp', iot.ap[:], c_iota, ld, writes=False)
            Bb = [Res(sb("Bb%d" % i, [128, 16, 128])) for i in range(2)]
            tmpB = Res(sb("tmpB", [128, 16, 128]))
            ang = Res(sb("ang", [128, L])); tA = Res(sb("tA", [128, L])); tB = Res(sb("tB", [128, L]))
            es_cur[0] = ess
            ld_ev = kb.last(ld)
            for e in ('dve', 'act', 'pe'):
                kb.wait(e, ld_ev)

            def A(ins):
                e1 = kb.ev('act', ins); kb.wait('dve', e1)

            def V(ins):
                e1 = kb.ev('dve', ins); return e1

            def sin_of(dst, src, shift, tmp1, tmp2):
                if shift != 0.0:
                    dve.tensor_scalar(out=tmp2, in0=src, scalar1=shift, scalar2=None, op0=ALU.add)
                    sx = tmp2
                else:
                    sx = src
                dve.tensor_scalar(out=tmp1, in0=sx, scalar1=1.0 / (2 * math.pi), scalar2=MAGIC, op0=ALU.mult, op1=ALU.add)
                dve.tensor_scalar(out=tmp1, in0=tmp1, scalar1=MAGIC, scalar2=None, op0=ALU.subtract)
                dve.scalar_tensor_tensor(out=tmp2, in0=tmp1, scalar=-2 * math.pi, in1=sx, op0=ALU.mult, op1=ALU.add)
                e1 = V(dve.tensor_scalar(out=tmp2, in0=tmp2, scalar1=-3.141592, scalar2=3.141592, op0=ALU.max, op1=ALU.min))
                kb.wait('act', e1)
                A(act.activation(out=dst, in_=tmp2, func=AF.Sin))

            kb.wait('act', ld_ev)
            A(act.activation(out=dt.ap[:], in_=ldt.ap[:], func=AF.Exp))
            e1 = V(dve.tensor_tensor(out=t1.ap[:], in0=are.ap[:], in1=dt.ap[:], op=ALU.mult))
            kb.wait('act', e1)
            A(act.activation(out=rr.ap[:], in_=t1.ap[:], func=AF.Exp))
            dve.tensor_tensor(out=phi.ap[:], in0=aim.ap[:], in1=dt.ap[:], op=ALU.mult)
            dve.tensor_scalar(out=t1.ap[:], in0=phi.ap[:], scalar1=1.0 / (2 * math.pi), scalar2=MAGIC, op0=ALU.mult, op1=ALU.add)
            dve.tensor_scalar(out=t1.ap[:], in0=t1.ap[:], scalar1=MAGIC, scalar2=None, op0=ALU.subtract)
            dve.scalar_tensor_tensor(out=phi.ap[:], in0=t1.ap[:], scalar=-2 * math.pi, in1=phi.ap[:], op0=ALU.mult, op1=ALU.add)
            sin_of(sph.ap[:], phi.ap[:], 0.0, t1.ap[:], t2.ap[:])
            sin_of(cph.ap[:], phi.ap[:], math.pi / 2, t1.ap[:], t2.ap[:])
            dve.tensor_tensor(out=abr.ap[:], in0=rr.ap[:], in1=cph.ap[:], op=ALU.mult)
            dve.tensor_tensor(out=abi.ap[:], in0=rr.ap[:], in1=sph.ap[:], op=ALU.mult)
            dve.tensor_scalar(out=nabi.ap[:], in0=abi.ap[:], scalar1=-1.0, scalar2=None, op0=ALU.mult)
            dve.tensor_tensor(out=t1.ap[:], in0=are.ap[:], in1=are.ap[:], op=ALU.mult)
            dve.tensor_tensor(out=t2.ap[:], in0=aim.ap[:], in1=aim.ap[:], op=ALU.mult)
            dve.tensor_tensor(out=t1.ap[:], in0=t1.ap[:], in1=t2.ap[:], op=ALU.add)
            dve.reciprocal(out=t1.ap[:], in_=t1.ap[:])
            dve.tensor_scalar(out=t2.ap[:], in0=abr.ap[:], scalar1=-1.0, scalar2=None, op0=ALU.add)
            dve.tensor_tensor(out=fre.ap[:], in0=t2.ap[:], in1=are.ap[:], op=ALU.mult)
            dve.tensor_tensor(out=fim.ap[:], in0=abi.ap[:], in1=aim.ap[:], op=ALU.mult)
            dve.tensor_tensor(out=fre.ap[:], in0=fre.ap[:], in1=fim.ap[:], op=ALU.add)
            dve.tensor_tensor(out=fre.ap[:], in0=fre.ap[:], in1=t1.ap[:], op=ALU.mult)
            dve.tensor_tensor(out=fim.ap[:], in0=abi.ap[:], in1=are.ap[:], op=ALU.mult)
            dve.tensor_tensor(out=t2.ap[:], in0=t2.ap[:], in1=aim.ap[:], op=ALU.mult)
            dve.tensor_tensor(out=fim.ap[:], in0=fim.ap[:], in1=t2.ap[:], op=ALU.subtract)
            dve.tensor_tensor(out=fim.ap[:], in0=fim.ap[:], in1=t1.ap[:], op=ALU.mult)
            if l == 0:
                dbg("rr", rr.ap[:]); dbg("abr", abr.ap[:]); dbg("abi", abi.ap[:]); dbg("fre", fre.ap[:]); dbg("fim", fim.ap[:]); dbg("phi", phi.ap[:]); dbg("dt", dt.ap[:])
            frb = fre.ap[:].unsqueeze(2).to_broadcast([128, 16, 128]); fib = fim.ap[:].unsqueeze(2).to_broadcast([128, 16, 128])
            dve.tensor_tensor(out=Bb[0].ap[:], in0=Bn[0].ap[:], in1=frb, op=ALU.mult)
            dve.tensor_tensor(out=tmpB.ap[:], in0=Bn[1].ap[:], in1=fib, op=ALU.mult)
            dve.tensor_tensor(out=Bb[0].ap[:], in0=Bb[0].ap[:], in1=tmpB.ap[:], op=ALU.subtract)
            dve.tensor_tensor(out=Bb[1].ap[:], in0=Bn[1].ap[:], in1=frb, op=ALU.mult)
            dve.tensor_tensor(out=tmpB.ap[:], in0=Bn[0].ap[:], in1=fib, op=ALU.mult)
            eB = V(dve.tensor_tensor(out=Bb[1].ap[:], in0=Bb[1].ap[:], in1=tmpB.ap[:], op=ALU.add))
            kb.wait('pe', eB); kb.rd('pe', ident)
            for ri in range(2):
                for s4 in range(4):
                    ps = poolC.next(); kb.wr('pe', ps)
                    for j in range(4):
                        e1 = pe.transpose(ps.ap[:, j * 128:(j + 1) * 128], Bb[ri].ap[:, s4 * 4 + j, :], ident.ap[:])
                    ps.ready = kb.ev('pe', e1)
                    kb.rd('act', ps)
                    e2 = kb.ev('act', act.copy(out=Bt[ri].ap[:, s4 * 4:(s4 + 1) * 4, :].rearrange("p a b -> p (a b)"), in_=ps.ap[:]))
                    ps.frees.append(e2); Bt[ri].ready = e2
            act.copy(out=Ctb[0].ap[:], in_=Cn[0].ap[:])
            eC = kb.ev('act', act.mul(out=Ctb[1].ap[:], in_=Cn[1].ap[:], mul=-1.0))
            Ctb[0].ready = eC; Ctb[1].ready = eC
            for st in range(16):
                e1 = V(dve.tensor_scalar(out=ang.ap[:], in0=iot.ap[:], scalar1=phi.ap[:, st:st + 1], scalar2=None, op0=ALU.mult))
                sin_of(sinT.ap[:, st, :], ang.ap[:], 0.0, tA.ap[:], tB.ap[:])
                sin_of(cosT.ap[:, st, :], ang.ap[:], math.pi / 2, tA.ap[:], tB.ap[:])

            if l == 0:
                dbg("cos0", cosT.ap[:, 5, :]); dbg("sin0", sinT.ap[:, 5, :]); dbg("Bt0", Bt[0].ap[:, 5, :], 'act'); dbg("Bb0", Bb[0].ap[:, 5, :])
            kb.fence()
            esp.close()
            ub = Ring([Res(sb("ub%d" % i, [128, 4, L], BF16)) for i in range(2)])
            u32 = Ring([Res(sb("u32_%d" % i, [128, 4, L])) for i in range(2)])
            W = {nm: Ring([Res(sb("%s%d" % (nm, i), [128, L])) for i in range(2 if nm in ("hre", "him") else 1)]) for nm in ("a1", "a2", "ure", "uim", "wre", "wim", "hre", "him")}
            hb = Ring([Res(sb("hb%d" % i, [128, 2, L], BF16)) for i in range(3)])
            Hre = Res(sb("Hre", [128, 16])); Him = Res(sb("Him", [128, 16]))
            dve.memset(Hre.ap[:], 0.0); dve.memset(Him.ap[:], 0.0)
            z32 = Res(sb("z32", [128, 4, L])); zbf = Res(sb("zbf", [128, 4, L], BF16))
            yt = Ring([Res(sb("yt%d" % i, [128, L])) for i in range(2)])
            ob = Ring([Res(sb("ob%d" % i, [128, 4, L], BF16)) for i in range(2)])

            def epilogue(u3, n, c0):
                o = ob.next(); kb.wr('dve', o)
                for m in range(4):
                    ps = poolB.next(); kb.wr('pe', ps); kb.rd('pe', zbf)
                    for k in range(4):
                        e1 = pe.matmul(ps.ap[:, :n], lhsT=wgl.ap[:, k * 512 + m * 128:k * 512 + m * 128 + 128], rhs=zbf.ap[:, k, :n], start=(k == 0), stop=(k == 3))
                    e1 = kb.ev('pe', e1); ps.ready = e1
                    g = yt.next(); kb.rd('act', ps); kb.wr('act', g)
                    e2 = kb.ev('act', act.activation(out=g.ap[:, :n], in_=ps.ap[:, :n], func=AF.Sigmoid, bias=bgl.ap[:, m:m + 1]))
                    ps.frees.append(e2); g.ready = e2
                    kb.rd('dve', g)
                    e3 = kb.ev('dve', dve.tensor_tensor(out=o.ap[:, m, :n], in0=z32.ap[:, m, :n], in1=g.ap[:, :n], op=ALU.mult))
                    g.frees.append(e3); o.ready = e3
                zbf.frees.append(e1); z32.frees.append(e3)
                kb.dma('sp', MIX[512:1024, c0:c0 + n].rearrange("(k p) t -> p k t", p=128), o.ap[:, :, :n], o, reads=(o,), writes=False)
                mix_evs.append(kb.last(o))

            def gelu_chunk(psY, u3, fc, n):
                y = yt.next(); s2 = yt.next()
                kb.rd('dve', psY); kb.wr('dve', y); kb.wr('dve', s2)
                if fc == 0:
                    kb.wr('dve', z32); kb.wr('dve', zbf)
                dve.scalar_tensor_tensor(out=y.ap[:, :n], in0=u3.ap[:, fc, :n], scalar=dsk.ap[:, fc:fc + 1], in1=psY.ap[:, :n], op0=ALU.mult, op1=ALU.add)
                dve.tensor_tensor(out=s2.ap[:, :n], in0=y.ap[:, :n], in1=y.ap[:, :n], op=ALU.mult)
                dve.tensor_scalar(out=s2.ap[:, :n], in0=s2.ap[:, :n], scalar1=0.044715, scalar2=1.0, op0=ALU.mult, op1=ALU.add)
                e1 = kb.ev('dve', dve.tensor_tensor(out=s2.ap[:, :n], in0=s2.ap[:, :n], in1=y.ap[:, :n], op=ALU.mult))
                psY.frees.append(e1)
                kb.wait('act', e1)
                e2 = kb.ev('act', act.activation(out=s2.ap[:, :n], in_=s2.ap[:, :n], func=AF.Sigmoid, scale=2.0 * math.sqrt(2.0 / math.pi)))
                kb.wait('dve', e2)
                dve.tensor_tensor(out=z32.ap[:, fc, :n], in0=y.ap[:, :n], in1=s2.ap[:, :n], op=ALU.mult)
                e3 = kb.ev('dve', dve.tensor_copy(out=zbf.ap[:, fc, :n], in_=z32.ap[:, fc, :n]))
                z32.ready = e3; zbf.ready = e3; y.ready = e3; s2.ready = e3

            tile_u = {}

            def prep_tile(tt):
                u3 = u32.next(); u = ub.next()
                kb.dma('sp', u3.ap[:], U32[:, :, tt * L:(tt + 1) * L], u3)
                kb.rd('act', u3); kb.wr('act', u)
                u.ready = kb.ev('act', act.copy(out=u.ap[:], in_=u3.ap[:]))
                tile_u[tt] = (u3, u)

            def emitB(tt, fc, si):
                st = fc * 4 + si
                u3, u = tile_u[tt]
                pr = poolA.next(); pi_ = poolA.next()
                kb.wr('pe', pr); kb.wr('pe', pi_); kb.rd('pe', u); kb.rd('pe', Bt[0]); kb.rd('pe', Bt[1])
                pr.ready = kb.ev('pe', pe.matmul(pr.ap[:], lhsT=Bt[0].ap[:, st, :], rhs=u.ap[:, fc, :], start=True, stop=True))
                pi_.ready = kb.ev('pe', pe.matmul(pi_.ap[:], lhsT=Bt[1].ap[:, st, :], rhs=u.ap[:, fc, :], start=True, stop=True))
                return pr, pi_

            units = [(tt, fc, si) for tt in range(NT) for fc in range(4) for si in range(4)]
            prep_tile(0)
            Bq = {0: emitB(*units[0])}
            psY = None
            for idx, (tt, fc, si) in enumerate(units):
                st = fc * 4 + si
                c0 = tt * L
                u3, u = tile_u[tt]
                if fc == 0 and si == 1 and tt + 1 < NT and (tt + 1) not in tile_u:
                    prep_tile(tt + 1)
                if idx + 1 < len(units):
                    ntt = units[idx + 1][0]
                    if ntt not in tile_u:
                        prep_tile(ntt)
                    Bq[idx + 1] = emitB(*units[idx + 1])
                pr, pi_ = Bq.pop(idx)
                if si == 0:
                    psY = poolB.next(); kb.wr('pe', psY)
                a1 = W["a1"].next(); a2 = W["a2"].next(); ure = W["ure"].next(); uim = W["uim"].next()
                wre = W["wre"].next(); wim = W["wim"].next(); hre = W["hre"].next(); him = W["him"].next()
                c_ = cosT.ap[:, st, :]; s_ = sinT.ap[:, st, :]
                kb.rd('dve', pr); kb.rd('dve', pi_)
                for r_ in (a1, a2, ure, uim, wre, wim, hre, him):
                    kb.wr('dve', r_)
                dve.tensor_tensor(out=a1.ap[:], in0=pr.ap[:], in1=c_, op=ALU.mult)
                dve.tensor_tensor(out=a2.ap[:], in0=pi_.ap[:], in1=s_, op=ALU.mult)
                dve.tensor_tensor(out=ure.ap[:], in0=a1.ap[:], in1=a2.ap[:], op=ALU.add)
                dve.tensor_tensor(out=a1.ap[:], in0=pi_.ap[:], in1=c_, op=ALU.mult)
                dve.tensor_tensor(out=a2.ap[:], in0=pr.ap[:], in1=s_, op=ALU.mult)
                e2 = kb.ev('dve', dve.tensor_tensor(out=uim.ap[:], in0=a1.ap[:], in1=a2.ap[:], op=ALU.subtract))
                pr.frees.append(e2); pi_.frees.append(e2)
                rb_ = rr.ap[:, st:st + 1].to_broadcast([128, L])
                dve.tensor_tensor_scan(out=wre.ap[:], data0=rb_, data1=ure.ap[:], initial=Hre.ap[:, st:st + 1], op0=ALU.mult, op1=ALU.add)
                dve.tensor_tensor_scan(out=wim.ap[:], data0=rb_, data1=uim.ap[:], initial=Him.ap[:, st:st + 1], op0=ALU.mult, op1=ALU.add)
                dve.tensor_tensor(out=a1.ap[:], in0=wre.ap[:], in1=c_, op=ALU.mult)
                dve.tensor_tensor(out=a2.ap[:], in0=wim.ap[:], in1=s_, op=ALU.mult)
                dve.tensor_tensor(out=hre.ap[:], in0=a1.ap[:], in1=a2.ap[:], op=ALU.subtract)
                dve.tensor_tensor(out=a1.ap[:], in0=wre.ap[:], in1=s_, op=ALU.mult)
                dve.tensor_tensor(out=a2.ap[:], in0=wim.ap[:], in1=c_, op=ALU.mult)
                dve.tensor_tensor(out=him.ap[:], in0=a1.ap[:], in1=a2.ap[:], op=ALU.add)
                dve.tensor_copy(out=Hre.ap[:, st:st + 1], in_=hre.ap[:, L - 1:L])
                e3 = kb.ev('dve', dve.tensor_copy(out=Him.ap[:, st:st + 1], in_=him.ap[:, L - 1:L]))
                hre.ready = e3; him.ready = e3
                hbb = hb.next(); kb.rd('act', hre); kb.wr('act', hbb)
                act.copy(out=hbb.ap[:, 0, :], in_=hre.ap[:])
                e4 = kb.ev('act', act.copy(out=hbb.ap[:, 1, :], in_=him.ap[:]))
                hbb.ready = e4; hre.frees.append(e4); him.frees.append(e4)
                kb.rd('pe', hbb); kb.rd('pe', Ctb[0])
                pe.matmul(psY.ap[:], lhsT=Ctb[0].ap[:, st, :], rhs=hbb.ap[:, 0, :], start=(si == 0), stop=False)
                e5 = kb.ev('pe', pe.matmul(psY.ap[:], lhsT=Ctb[1].ap[:, st, :], rhs=hbb.ap[:, 1, :], start=False, stop=(si == 3)))
                hbb.frees.append(e5)
                if si == 3:
                    psY.ready = e5
                    gelu_chunk(psY, u3, fc, L)
                    if fc == 3:
                        u.frees.append(e5)
                        epilogue(u3, L, c0)
                        u3.frees.append(('dve', kb.sem['dve'], kb.cnt['dve']))
            eH = kb.ev('dve', dve.tensor_copy(out=t1.ap[:], in_=Hre.ap[:]))
            kb.wait('sp', eH)
            r1 = Res(None)
            kb.out_evs.append(kb.dma('sp', ssm_p_re[l], Hre.ap[:], r1, writes=False))
            kb.out_evs.append(kb.dma('sp', ssm_p_im[l], Him.ap[:], r1, writes=False))

            u3 = u32.next(); u = ub.next()
            kb.dma('sp', u3.ap[:, :, 0:16], U32[:, :, S:S + 16], u3)
            kb.rd('act', u3); kb.wr('act', u)
            e1 = kb.ev('act', act.copy(out=u.ap[:, :, 0:16], in_=u3.ap[:, :, 0:16])); u.ready = e1
            hs_re = Res(sb("hs_re", [128, 16, 16])); hs_im = Res(sb("hs_im", [128, 16, 16]))
            hsb = Res(sb("hsb", [128, 2, 16, 16], BF16))
            tq1 = Res(sb("tq1", [128, 16, 4])); tq2 = Res(sb("tq2", [128, 16, 4]))
            pbr = poolA.next(); pbi = poolA.next()
            kb.wr('pe', pbr); kb.wr('pe', pbi); kb.rd('pe', u)
            for st in range(16):
                pe.matmul(pbr.ap[:, st * 16:(st + 1) * 16], lhsT=Bt[0].ap[:, st, :], rhs=u.ap[:, st // 4, 0:16], start=True, stop=True)
                e1 = pe.matmul(pbi.ap[:, st * 16:(st + 1) * 16], lhsT=Bt[1].ap[:, st, :], rhs=u.ap[:, st // 4, 0:16], start=True, stop=True)
            e1 = kb.ev('pe', e1); pbr.ready = e1; pbi.ready = e1
            bre = pbr.ap[:, 0:256].rearrange("p (s b t) -> p s b t", s=16, t=4)
            bim = pbi.ap[:, 0:256].rearrange("p (s b t) -> p s b t", s=16, t=4)
            hr = hs_re.ap[:].rearrange("p s (b t) -> p s b t", t=4); hi = hs_im.ap[:].rearrange("p s (b t) -> p s b t", t=4)
            abr_b = abr.ap[:].unsqueeze(2).to_broadcast([128, 16, 4]); abi_b = abi.ap[:].unsqueeze(2).to_broadcast([128, 16, 4])
            kb.rd('dve', pbr); kb.rd('dve', pbi)
            for tau in range(4):
                pre = h0r.ap[:] if tau == 0 else hr[:, :, :, tau - 1]
                pim = h0i.ap[:] if tau == 0 else hi[:, :, :, tau - 1]
                dve.tensor_tensor(out=tq1.ap[:], in0=pre, in1=abr_b, op=ALU.mult)
                dve.tensor_tensor(out=tq2.ap[:], in0=pim, in1=abi_b, op=ALU.mult)
                dve.tensor_tensor(out=tq1.ap[:], in0=tq1.ap[:], in1=tq2.ap[:], op=ALU.subtract)
                dve.tensor_tensor(out=hr[:, :, :, tau], in0=tq1.ap[:], in1=bre[:, :, :, tau], op=ALU.add)
                dve.tensor_tensor(out=tq1.ap[:], in0=pim, in1=abr_b, op=ALU.mult)
                dve.tensor_tensor(out=tq2.ap[:], in0=pre, in1=abi_b, op=ALU.mult)
                dve.tensor_tensor(out=tq1.ap[:], in0=tq1.ap[:], in1=tq2.ap[:], op=ALU.add)
                dve.tensor_tensor(out=hi[:, :, :, tau], in0=tq1.ap[:], in1=bim[:, :, :, tau], op=ALU.add)
            dve.tensor_copy(out=hsb.ap[:, 0], in_=hs_re.ap[:])
            e2 = kb.ev('dve', dve.tensor_copy(out=hsb.ap[:, 1], in_=hs_im.ap[:]))
            pbr.frees.append(e2); pbi.frees.append(e2)
            kb.wait('pe', e2)
            for fc in range(4):
                psY = poolB.next(); kb.wr('pe', psY)
                for si in range(4):
                    st = fc * 4 + si
                    pe.matmul(psY.ap[:, 0:16], lhsT=Ctb[0].ap[:, st, :], rhs=hsb.ap[:, 0, st, :], start=(si == 0), stop=False)
                    e5 = pe.matmul(psY.ap[:, 0:16], lhsT=Ctb[1].ap[:, st, :], rhs=hsb.ap[:, 1, st, :], start=False, stop=(si == 3))
                e5 = kb.ev('pe', e5)
                psY.ready = e5
                gelu_chunk(psY, u3, fc, 16)
            epilogue(u3, 16, S)
            fs_re = Res(sb("fs_re", [128, 16, 4])); fs_im = Res(sb("fs_im", [128, 16, 4]))
            dve.tensor_copy(out=fs_re.ap[:], in_=hs_re.ap[:].rearrange("p s (b t) -> p s b t", t=4)[:, :, :, 3])
            eF = kb.ev('dve', dve.tensor_copy(out=fs_im.ap[:], in_=hs_im.ap[:].rearrange("p s (b t) -> p s b t", t=4)[:, :, :, 3]))
            kb.wait('sp', eF)
            kb.out_evs.append(kb.dma('sp', ssm_s_re[l], fs_re.ap[:], r1, writes=False))
            kb.out_evs.append(kb.dma('sp', ssm_s_im[l], fs_im.ap[:], r1, writes=False))
            kb.wait('sp', kb.last(r1))
            kb.fence()
            kb.end_phase()
        es_cur[0] = es

    def sample_attention(l):
        with ExitStack() as esx:
            es_cur[0] = esx
            kb.begin_phase()
            kb.wait('sp', qkvs_ev[l])
            ld = Res(None)
            knew = Res(sb("knew", [16, 512])); vnew = Res(sb("vnew", [16, 512]))
            kb.dma('sp', knew.ap[:], QKVS[:, 512:1024], ld, writes=False)
            kb.dma('sp', vnew.ap[:], QKVS[:, 1024:1536], ld, writes=False)
            ones32 = Res(sb("ones32", [128, 1])); dve.memset(ones32.ap[:], 1.0)
            hmask = Res(sb("hmask", [8, 8, 64]))
            kb.dma('sp', hmask.ap[:], c_hmask, ld, writes=False)
            attT = Res(sb("attT", [128, 4, 16], BF16))
            psT = poolC.next(); kb.wr('pe', psT)
            Kt = Ring([Res(sb("Kt%d" % i, [128, 9, 512])) for i in range(2)])
            Vt = Ring([Res(sb("Vt%d" % i, [128, 9, 512])) for i in range(2)])
            qbs = Ring([Res(sb("qb%d" % i, [128, 512])) for i in range(2)])
            prod = Res(sb("prod", [128, 512])); Sx = Res(sb("Sx", [128, 4, 8])); Px = Ring([Res(sb("Px%d" % i, [128, 4, 8])) for i in range(2)])
            Z = Res(sb("Z", [8, 512])); rden = Res(sb("rden", [8, 1]))
            kb.wait('dve', kb.last(ld)); kb.wait('pe', kb.last(ld))
            lastpe = None
            for b in range(4):
                K = Kt.next(); V = Vt.next()
                for (Tl, cache) in ((K, cache_k), (V, cache_v)):
                    kb.wr('sp', Tl)
                    if Tl.sem is None:
                        Tl.sem = kb.newsem()
                    def ld1(dst, src):
                        ins = sp.dma_start(out=dst, in_=src); Tl.sem.cnt += 16; ins.then_inc(Tl.sem.sem, 16)
                    ld1(Tl.ap[:, 0, :], cache[l, b, 1920:2048, :])
                    for i in range(4):
                        src = bass.AP(tensor=cache.tensor, offset=cache[l, b, 1536 + i, :].offset, ap=[[4 * 512, 128], [1, 512]])
                        ld1(Tl.ap[:, 1 + i, :], src)
                        src = bass.AP(tensor=cache.tensor, offset=cache[l, b, i, :].offset, ap=[[16 * 512, 128], [1, 512]])
                        ld1(Tl.ap[:, 5 + i, :], src)
                    Tl.ready = kb.last(Tl)
                for i in range(4):
                    t = b * 4 + i
                    qb = qbs.next()
                    kb.dma('sp', qb.ap[:], bass.AP(tensor=QKVS.tensor, offset=QKVS[t, 0:512].offset, ap=[[0, 128], [1, 512]]), qb)
                    kb.rd('dve', qb); kb.rd('dve', K); kb.wr('dve', Sx)
                    tiles = ((0, 128), (1 + i, 128), (5 + i, 128))
                    for j, (ti, np_) in enumerate(tiles):
                        dve.tensor_tensor(out=prod.ap[:], in0=K.ap[:, ti, :], in1=qb.ap[:], op=ALU.mult)
                        dve.tensor_reduce(out=Sx.ap[:, j, :], in_=prod.ap[:].rearrange("p (h d) -> p h d", d=64), axis=AX.X, op=ALU.add)
                    dve.tensor_tensor(out=prod.ap[0:16, :], in0=knew.ap[:], in1=qb.ap[0:16, :], op=ALU.mult)
                    dve.memset(Sx.ap[:, 3, :], -30000.0)
                    e1 = kb.ev('dve', dve.tensor_reduce(out=Sx.ap[0:16, 3, :], in_=prod.ap[0:16, :].rearrange("p (h d) -> p h d", d=64), axis=AX.X, op=ALU.add))
                    qb.frees.append(e1)
                    P = Px.next(); kb.wait('act', e1); kb.wr('act', P)
                    e2 = kb.ev('act', act.activation(out=P.ap[:], in_=Sx.ap[:], func=AF.Exp))
                    Sx.frees.append(e2)
                    kb.wait('dve', e2)
                    dve.tensor_tensor(out=P.ap[:, 0, :], in0=P.ap[:, 0, :], in1=EBs0.ap[:, :, i], op=ALU.mult)
                    dve.tensor_tensor(out=P.ap[:, 1:3, :], in0=P.ap[:, 1:3, :], in1=EBs12.ap[:], op=ALU.mult)
                    e3 = kb.ev('dve', dve.tensor_tensor(out=P.ap[0:16, 3, :], in0=P.ap[0:16, 3, :], in1=EBnew.ap[:, t, :], op=ALU.mult))
                    P.ready = e3
                    pO = poolB.next(); pD = poolA.next()
                    kb.wr('pe', pO); kb.wr('pe', pD); kb.rd('pe', P); kb.rd('pe', V)
                    for j, (ti, np_) in enumerate(tiles):
                        pe.matmul(pO.ap[0:8, :], lhsT=P.ap[:, j, :], rhs=V.ap[:, ti, :], start=(j == 0), stop=False)
                    e4 = kb.ev('pe', pe.matmul(pO.ap[0:8, :], lhsT=P.ap[0:16, 3, :], rhs=vnew.ap[:], start=False, stop=True))
                    pO.ready = e4
                    for j in range(3):
                        pe.matmul(pD.ap[0:8, 0:1], lhsT=P.ap[:, j, :], rhs=ones32.ap[:], start=(j == 0), stop=False)
                    e4 = kb.ev('pe', pe.matmul(pD.ap[0:8, 0:1], lhsT=P.ap[0:16, 3, :], rhs=ones32.ap[0:16, :], start=False, stop=True))
                    pD.ready = e4; P.frees.append(e4)
                    kb.rd('dve', pD); kb.rd('dve', pO); kb.wr('dve', Z)
                    dve.reciprocal(out=rden.ap[:], in_=pD.ap[0:8, 0:1])
                    e5 = kb.ev('dve', dve.scalar_tensor_tensor(out=Z.ap[:], in0=pO.ap[0:8, :], scalar=rden.ap[:, 0:1], in1=hmask.ap[:].rearrange("h a d -> h (a d)"), op0=ALU.mult, op1=ALU.mult))
                    pO.frees.append(e5); pD.frees.append(e5); Z.ready = e5
                    kb.rd('pe', Z)
                    for c in range(4):
                        lastpe = pe.matmul(psT.ap[:, c * 16 + t:c * 16 + t + 1], lhsT=Z.ap[:, c * 128:(c + 1) * 128], rhs=ones32.ap[0:8, :], start=True, stop=True)
                    e6 = kb.ev('pe', lastpe); Z.frees.append(e6)
                K.frees.append(('dve', kb.sem['dve'], kb.cnt['dve'])); V.frees.append(e6)
            psT.ready = e6
            kb.rd('dve', psT)
            e7 = kb.ev('dve', dve.tensor_copy(out=attT.ap[:].rearrange("p c t -> p (c t)"), in_=psT.ap[:, 0:64]))
            psT.frees.append(e7); attT.ready = e7
            kb.dma('sp', MIX[0:512, S:S + 16].rearrange("(k p) t -> p k t", p=128), attT.ap[:], attT, reads=(attT,), writes=False)
            mix_evs.append(kb.last(attT))
            kb.wait('sp', kb.last(attT))
            kb.fence()
            kb.end_phase()
        es_cur[0] = es

    c_hmask = din("c_hmask", [8, 8, 64])

    kstop = int(os.environ.get("KSTOP", "99"))
    stage = [0]

    def go(fn, *a):
        stage[0] += 1
        if stage[0] <= kstop:
            fn(*a)
    go(row_phase, 0)
    for l in range(DEPTH):
        go(attention, l)
        go(ssm, l)
        go(sample_attention, l)
        del mix_evs[:]; del xs_evs[:]
        go(row_phase, l + 1)
    for e1 in kb.out_evs:
        kb.wait('sp', e1)
    es.close()
    nc._dbg_names = dbg_names
    return nc


def _fm(v):
    return np.ascontiguousarray(v.reshape(-1, 128).T)


def _consts():
    iota = np.tile(np.arange(1, 513, dtype=np.float32)[None, :], (128, 1))
    oh = np.zeros((33, 3 * EW), np.float32)
    for g, d in enumerate(BRANCH_D):
        steps = np.arange(0, 129)
        bk = t5_bucket(steps * d)
        oh[32, g * EW:(g + 1) * EW] = 1.0
        for st_, b_ in zip(steps, bk):
            oh[b_, g * EW + st_ + 127] = 1.0
            oh[32, g * EW + st_ + 127] = 0.0
    ident = np.eye(128, dtype=np.float32)
    ohnew = np.zeros((4, 16, 16), np.float32)
    for q in range(16):
        for k in range(16):
            if q // 4 == k // 4 and k % 4 <= q % 4:
                m = q % 4 - k % 4
                ohnew[m, q, k] = 3.0 if m == 0 else 1.0
    hmask = np.zeros((8, 8, 64), np.float32)
    for h in range(8):
        hmask[h, h, :] = 1.0
    return dict(c_iota=iota, c_oh=oh, c_ident=ident, c_ohnew=ohnew, c_hmask=hmask)


def _core_inputs(c, S, inp, shared):
    T = S + 16
    f32 = np.float32
    xs = inp['x_sample'][4 * c:4 * c + 4].reshape(16, D)
    xall = np.concatenate([inp['x_prompt'][c], xs], axis=0)
    xT = np.ascontiguousarray(xall.reshape(T, 8, 128).transpose(2, 1, 0))
    pall = np.concatenate([inp['p_prompt'][:, c], inp['p_sample'][:, 4 * c:4 * c + 4].reshape(DEPTH, 16, 256)], axis=1)
    pT = np.ascontiguousarray(pall.reshape(DEPTH, T, 2, 128).transpose(0, 3, 2, 1))
    ck = np.ascontiguousarray(inp['cache_k'][:, 4 * c:4 * c + 4].reshape(DEPTH, 4, 2048, 512))
    cv = np.ascontiguousarray(inp['cache_v'][:, 4 * c:4 * c + 4].reshape(DEPTH, 4, 2048, 512))

    def st_lay(a):
        return np.ascontiguousarray(a.reshape(DEPTH, 4, 16, 2, 64).transpose(0, 3, 4, 2, 1).reshape(DEPTH, 128, 16, 4))
    d = dict(shared)
    d.update(xT=xT, pT=pT, cache_k=ck, cache_v=cv,
             st_re=st_lay(inp['state_ssm_re'][:, 4 * c:4 * c + 4]), st_im=st_lay(inp['state_ssm_im'][:, 4 * c:4 * c + 4]))
    return d


def _shared_inputs(inp):
    f32 = np.float32
    sh = {}
    for k in ('rel_bias', 'w_in', 'w_out', 'ffn_w_gate', 'ffn_w_up', 'ffn_w_down', 'w_glu', 'w_ple_gate', 'w_ple_proj'):
        sh[k] = np.ascontiguousarray(inp[k], dtype=f32)
    cols = []
    for l in range(DEPTH):
        cols += [_fm(inp['norm_ffn'][l, 0]), _fm(inp['norm_mix'][l]), _fm(inp['norm_att_out'][l]), _fm(inp['norm_ssm_out'][l]),
                 _fm(inp['norm_ffn'][l, 1]), _fm(inp['norm_ple'][l])]
    cols.append(_fm(inp['norm_final']))
    sh['gains'] = np.ascontiguousarray(np.concatenate(cols, axis=1), dtype=f32)

    def gn(a):
        return np.ascontiguousarray(a.reshape(DEPTH, 16, 2, 64).transpose(0, 2, 3, 1).reshape(DEPTH, 128, 16), dtype=f32)
    sh['s_are'] = gn(inp['ssm_a_re']); sh['s_aim'] = gn(inp['ssm_a_im'])
    sh['s_ldt'] = gn(np.broadcast_to(inp['ssm_log_dt'][:, :, None], (DEPTH, 32, 64)))
    sh['s_d'] = np.ascontiguousarray(inp['ssm_d'].reshape(DEPTH, 4, 128).transpose(0, 2, 1), dtype=f32)
    sh['s_bglu'] = np.ascontiguousarray(inp['b_glu'].reshape(DEPTH, 4, 128).transpose(0, 2, 1), dtype=f32)

    def bn(b):
        out = np.zeros((DEPTH, 2, 64, 16, 128), f32)
        bb = b.reshape(DEPTH, 16, 2, 64, 16)
        for st in range(16):
            for gl in range(2):
                r0 = (st % 4) * 32 + gl * 16
                out[:, gl, :, st, r0:r0 + 16] = bb[:, st, gl]
        return out.reshape(DEPTH, 128, 16, 128)

    def ct(cc):
        out = np.zeros((DEPTH, 2, 64, 16, 128), f32)
        c5 = cc.reshape(DEPTH, 16, 2, 16, 64)
        for st in range(16):
            for gl in range(2):
                r0 = (st % 4) * 32 + gl * 16
                out[:, gl, :, st, r0:r0 + 16] = c5[:, st, gl].transpose(0, 2, 1)
        return out.reshape(DEPTH, 128, 16, 128)
    sh['Bn_re'] = bn(np.asarray(inp['ssm_b_re'])); sh['Bn_im'] = bn(np.asarray(inp['ssm_b_im']))
    sh['Ct_re'] = ct(np.asarray(inp['ssm_c_re'])); sh['Ct_im'] = ct(np.asarray(inp['ssm_c_im']))
    sh.update(_consts())
    return sh


def _run(inp, S, n_cores):
    inp = {k: np.asarray(v) for k, v in inp.items()}
    nc = build(S)
    shared = _shared_inputs(inp)
    in_maps = [_core_inputs(c, S, inp, shared) for c in range(n_cores)]
    res = run_bass_kernel_spmd(nc, in_maps, core_ids=list(range(n_cores)))
    R = res.results
    global LAST_RAW
    LAST_RAW = R
    KEEP = min(2048, S)
    B = n_cores
    y_p = np.zeros((B, S, D), np.float32); y_s = np.zeros((4 * B, 4, D), np.float32)
    k_p = np.zeros((DEPTH, B, KEEP, NH, HD), np.float32); v_p = np.zeros_like(k_p)
    r_p = np.zeros((DEPTH, B, 32, 64), np.float32); i_p = np.zeros_like(r_p)
    k_s = np.zeros((DEPTH, 4 * B, 2048, NH, HD), np.float32); v_s = np.zeros_like(k_s)
    r_s = np.zeros((DEPTH, 4 * B, 32, 64), np.float32); i_s = np.zeros_like(r_s)
    for c in range(n_cores):
        r = R[c]
        yT = r['yT']
        yall = yT.transpose(2, 1, 0).reshape(S + 16, D)
        y_p[c] = yall[:S]; y_s[4 * c:4 * c + 4] = yall[S:].reshape(4, 4, D)
        k_p[:, c] = r['kT_p'].transpose(0, 3, 1, 2)
        v_p[:, c] = r['v_p'].reshape(DEPTH, KEEP, NH, HD)

        def gst(a):
            return a.reshape(DEPTH, 2, 64, 16).transpose(0, 3, 1, 2).reshape(DEPTH, 32, 64)
        r_p[:, c] = gst(r['ssm_p_re']); i_p[:, c] = gst(r['ssm_p_im'])
        k_s[:, 4 * c:4 * c + 4] = r['k_s'].reshape(DEPTH, 4, 2048, NH, HD)
        v_s[:, 4 * c:4 * c + 4] = r['v_s'].reshape(DEPTH, 4, 2048, NH, HD)

        def gss(a):
            return a.reshape(DEPTH, 2, 64, 16, 4).transpose(0, 4, 3, 1, 2).reshape(DEPTH, 4, 32, 64)
        r_s[:, 4 * c:4 * c + 4] = gss(r['ssm_s_re']); i_s[:, 4 * c:4 * c + 4] = gss(r['ssm_s_im'])
    return (y_p, y_s, k_p, v_p, r_p, i_p, k_s, v_s, r_s, i_s)


def kernel(**inputs):
    return _run(inputs, 4096, 8)
```
